# Optimizing a Trainium2 kernel written in Bass

```python
import jax, jax.numpy as jnp
from jax import lax
import numpy as np

D_MODEL = 1024
BATCH = 2
SEQ = 8192
DEPTH = 2
DEC_BATCH = 128
DEC_SEQ = 8
PAST_LEN = 2048
PAGE_SIZE = 128

N_MIXERS = 2
N_GLA_LAYERS = (DEPTH + 1) // 2
N_SB_LAYERS = DEPTH // 2
GLA_HEADS = 4
GLA_DK = D_MODEL // 2 // GLA_HEADS
GLA_DV = D_MODEL // GLA_HEADS
GLA_KEY_W = GLA_HEADS * GLA_DK
GLA_VAL_W = GLA_HEADS * GLA_DV
GLA_GATE_RANK = 16
GLA_TAU = 16.0
GLA_CHUNK = 64
GLA_IN_W = 2 * GLA_KEY_W + 2 * GLA_VAL_W + GLA_GATE_RANK
SB_HEADS = 16
SB_DH = D_MODEL // SB_HEADS
SB_W = SB_HEADS * SB_DH
SB_IN_W = 4 * SB_W
SB_QBLOCK = 128
SB_BIAS_INIT = -6.0
EPS = 1e-6

kernel_name = 'hybrid_gla_stickbreaking_decoder_step'


def rms_norm(x, gain):
    xf = x.astype(jnp.float32)
    y = xf * lax.rsqrt(jnp.mean(xf * xf, axis=-1, keepdims=True) + EPS)
    return (y * gain.astype(jnp.float32)).astype(x.dtype)


def gla_recurrence(q, k, v, log_a, s0):
    bsz, length, nh = q.shape[0], q.shape[1], q.shape[2]
    c = min(GLA_CHUNK, length)
    n = length // c

    def to_chunks(t):
        t = t.astype(jnp.float32).reshape(bsz, n, c, nh, t.shape[3])
        return t.transpose(1, 0, 3, 2, 4)

    causal = jnp.tril(jnp.ones((c, c), dtype=bool))

    def step(s, inp):
        qc, kc, vc, gc = inp
        b = jnp.cumsum(gc, axis=-2)
        b_last = b[..., -1:, :]
        qd = qc * jnp.exp(b)
        kd = kc * jnp.exp(-b)
        att = jnp.where(causal, jnp.einsum('bhtk,bhsk->bhts', qd, kd), 0.0)
        o = jnp.einsum('bhtk,bhkv->bhtv', qd, s) + jnp.einsum('bhts,bhsv->bhtv', att, vc)
        s_new = s * jnp.exp(b_last)[..., 0, :, None] + jnp.einsum(
            'bhsk,bhsv->bhkv', kc * jnp.exp(b_last - b), vc)
        return s_new, o

    s_fin, o = lax.scan(step, s0.astype(jnp.float32),
                        (to_chunks(q), to_chunks(k), to_chunks(v), to_chunks(log_a)))
    o = o.transpose(1, 0, 3, 2, 4).reshape(bsz, length, nh, v.shape[3])
    return o, s_fin


def gla_mixer(h, w_in, w_up, b_a, onorm, w_out, s0):
    bsz, length, _ = h.shape
    u = h @ w_in
    o1 = GLA_KEY_W
    o2 = 2 * GLA_KEY_W
    o3 = o2 + GLA_VAL_W
    o4 = o3 + GLA_VAL_W
    q = u[..., :o1].reshape(bsz, length, GLA_HEADS, GLA_DK) * (GLA_DK ** -0.5)
    k = u[..., o1:o2].reshape(bsz, length, GLA_HEADS, GLA_DK)
    v = u[..., o2:o3].reshape(bsz, length, GLA_HEADS, GLA_DV)
    g = u[..., o3:o4]
    a_code = u[..., o4:]
    log_a = jax.nn.log_sigmoid((a_code @ w_up + b_a).astype(jnp.float32)) / GLA_TAU
    log_a = log_a.reshape(bsz, length, GLA_HEADS, GLA_DK)
    o, s_fin = gla_recurrence(q, k, v, log_a, s0)
    o = rms_norm(o, onorm).reshape(bsz, length, GLA_VAL_W).astype(h.dtype)
    return (o * jax.nn.silu(g)) @ w_out, s_fin


def sb_attend(q, k, v, bias, q_offset):
    bsz, t_len, nh, dh = q.shape
    blk = min(SB_QBLOCK, t_len)
    nb = t_len // blk
    qb = q.reshape(bsz, nb, blk, nh, dh).transpose(1, 0, 2, 3, 4)
    q_pos = (q_offset + jnp.arange(t_len, dtype=jnp.int32)).reshape(nb, blk)
    k_pos = jnp.arange(k.shape[1], dtype=jnp.int32)
    bias_f = bias.astype(jnp.float32)[None, :, None, None]

    def one_block(args):
        qi, pi = args
        z = jnp.einsum('bqhd,bkhd->bhqk', qi, k).astype(jnp.float32) * (dh ** -0.5) + bias_f
        valid = k_pos[None, :] < pi[:, None]
        log_1mb = jnp.where(valid, jax.nn.log_sigmoid(-z), 0.0)
        rest = lax.cumsum(log_1mb, axis=3, reverse=True)
        a = jnp.exp(jnp.where(valid, z + rest, -jnp.inf))
        return jnp.einsum('bhqk,bkhd->bqhd', a.astype(v.dtype), v)

    o = lax.map(one_block, (qb, q_pos))
    return o.transpose(1, 0, 2, 3, 4).reshape(bsz, t_len, nh, dh)


def sb_mixer(h, w_in, qn, kn, bias, w_out, k_past, v_past):
    bsz, length, _ = h.shape
    u = h @ w_in
    q = rms_norm(u[..., :SB_W].reshape(bsz, length, SB_HEADS, SB_DH), qn)
    k = rms_norm(u[..., SB_W:2 * SB_W].reshape(bsz, length, SB_HEADS, SB_DH), kn)
    v = u[..., 2 * SB_W:3 * SB_W].reshape(bsz, length, SB_HEADS, SB_DH)
    g = u[..., 3 * SB_W:]
    if k_past is None:
        k_all, v_all, off = k, v, 0
    else:
        k_all = jnp.concatenate([k_past.astype(k.dtype), k], axis=1)
        v_all = jnp.concatenate([v_past.astype(v.dtype), v], axis=1)
        off = k_past.shape[1]
    o = sb_attend(q, k_all, v_all, bias, off).reshape(bsz, length, SB_W)
    return (o * jax.nn.silu(g)) @ w_out, k, v


def gather_pages(pool, page_table):
    rows = pool[page_table]
    return rows.reshape(rows.shape[0], rows.shape[1] * rows.shape[2], rows.shape[3], rows.shape[4])


def setup_inputs(seed: int = 0) -> dict:
    key = jax.random.key(seed)
    ks = jax.random.split(key, 17)
    n_pages = PAST_LEN // PAGE_SIZE
    n_used = DEC_BATCH * n_pages
    n_phys = n_used + (n_used + 3) // 4
    nrm = jax.random.normal
    f32 = jnp.float32
    x_prompt = nrm(ks[0], (BATCH, SEQ, D_MODEL), f32)
    x_sample = nrm(ks[1], (DEC_BATCH, DEC_SEQ, D_MODEL), f32)
    state_gla = 0.1 * nrm(ks[2], (N_GLA_LAYERS, DEC_BATCH, GLA_HEADS, GLA_DK, GLA_DV), f32)
    cache_k = nrm(ks[3], (N_SB_LAYERS, n_phys, PAGE_SIZE, SB_HEADS, SB_DH), f32)
    cache_v = nrm(ks[4], (N_SB_LAYERS, n_phys, PAGE_SIZE, SB_HEADS, SB_DH), f32)
    page_table = jax.random.permutation(ks[5], n_phys)[:n_used].reshape(DEC_BATCH, n_pages).astype(jnp.int32)
    norm_gain = 1.0 + 0.02 * nrm(ks[6], (DEPTH, D_MODEL), f32)
    w_in_a = nrm(ks[7], (N_GLA_LAYERS, D_MODEL, GLA_IN_W), f32) * D_MODEL ** -0.5
    w_alpha_up = nrm(ks[8], (N_GLA_LAYERS, GLA_GATE_RANK, GLA_KEY_W), f32) * GLA_GATE_RANK ** -0.5
    b_alpha = 0.1 * nrm(ks[9], (N_GLA_LAYERS, GLA_KEY_W), f32)
    onorm_a = 1.0 + 0.02 * nrm(ks[10], (N_GLA_LAYERS, GLA_DV), f32)
    w_out_a = nrm(ks[11], (N_GLA_LAYERS, GLA_VAL_W, D_MODEL), f32) * GLA_VAL_W ** -0.5
    w_in_b = nrm(ks[12], (N_SB_LAYERS, D_MODEL, SB_IN_W), f32) * D_MODEL ** -0.5
    qnorm_b = 1.0 + 0.02 * nrm(ks[13], (N_SB_LAYERS, SB_DH), f32)
    knorm_b = 1.0 + 0.02 * nrm(ks[14], (N_SB_LAYERS, SB_DH), f32)
    sb_bias = SB_BIAS_INIT + 0.1 * nrm(ks[16], (N_SB_LAYERS, SB_HEADS), f32)
    w_out_b = nrm(ks[15], (N_SB_LAYERS, SB_W, D_MODEL), f32) * SB_W ** -0.5
    return {'x_prompt': x_prompt, 'x_sample': x_sample, 'state_gla': state_gla,
            'cache_k': cache_k, 'cache_v': cache_v, 'page_table': page_table,
            'norm_gain': norm_gain, 'w_in_a': w_in_a, 'w_alpha_up': w_alpha_up,
            'b_alpha': b_alpha, 'onorm_a': onorm_a, 'w_out_a': w_out_a,
            'w_in_b': w_in_b, 'qnorm_b': qnorm_b, 'knorm_b': knorm_b, 'sb_bias': sb_bias,
            'w_out_b': w_out_b}


def reference(x_prompt, x_sample, state_gla, cache_k, cache_v, page_table,
              norm_gain, w_in_a, w_alpha_up, b_alpha, onorm_a, w_out_a,
              w_in_b, qnorm_b, knorm_b, sb_bias, w_out_b):
    xp, xs = x_prompt, x_sample
    sg_p, sg_s, k_p, v_p, k_s, v_s = [], [], [], [], [], []
    for i in range(DEPTH):
        j = i // N_MIXERS
        hp = rms_norm(xp, norm_gain[i])
        hs = rms_norm(xs, norm_gain[i])
        if i % N_MIXERS == 0:
            s0_p = jnp.zeros((xp.shape[0], GLA_HEADS, GLA_DK, GLA_DV), jnp.float32)
            yp, st_p = gla_mixer(hp, w_in_a[j], w_alpha_up[j], b_alpha[j], onorm_a[j], w_out_a[j], s0_p)
            ys, st_s = gla_mixer(hs, w_in_a[j], w_alpha_up[j], b_alpha[j], onorm_a[j], w_out_a[j], state_gla[j])
            sg_p.append(st_p)
            sg_s.append(st_s)
        else:
            yp, kp, vp = sb_mixer(hp, w_in_b[j], qnorm_b[j], knorm_b[j], sb_bias[j], w_out_b[j], None, None)
            k_past = gather_pages(cache_k[j], page_table)
            v_past = gather_pages(cache_v[j], page_table)
            ys, kn, vn = sb_mixer(hs, w_in_b[j], qnorm_b[j], knorm_b[j], sb_bias[j], w_out_b[j], k_past, v_past)
            k_p.append(kp)
            v_p.append(vp)
            k_s.append(kn)
            v_s.append(vn)
        xp = xp + yp
        xs = xs + ys
    return (xp, xs, jnp.stack(sg_p), jnp.stack(sg_s), jnp.stack(k_p), jnp.stack(v_p), jnp.stack(k_s), jnp.stack(v_s))
```

```python
import numpy as np
import ml_dtypes
import concourse.bass as bass
import concourse.mybir as mybir
from concourse.bass_utils import run_bass_kernel_spmd


ENGS = ("pe", "act", "dve", "pool", "sp")


class Prog:
    def __init__(self, nc, same_engine_sync=True):
        self.nc = nc
        self.ops = {e: [] for e in ENGS}
        self.count = {e: 0 for e in ENGS}
        self.esem = {}
        self.dsem = {}
        self.last_w = {}
        self.reads = {}
        self.known = {e: {} for e in ENGS}
        self.sems = {}
        self.same_engine_sync = same_engine_sync
        self._ctx = []
        self.final_tokens = []
        self.pending = {e: {} for e in ENGS}

    def _new_sem(self, name):
        cm = self.nc.semaphore(name)
        h = cm.__enter__()
        self._ctx.append(cm)
        sid = len(self.sems)
        self.sems[sid] = h
        return sid

    def _eng_sem(self, e):
        if e not in self.esem:
            self.esem[e] = self._new_sem("es_" + e)
        return self.esem[e]

    def _collect(self, e, reads, writes, is_dma):
        need = dict(self.pending[e])
        self.pending[e] = {}
        def add(tok):
            s, v = tok
            if need.get(s, 0) < v:
                need[s] = v
        for k in reads:
            for t in self.last_w.get(k, ()):
                add(t)
        for k in writes:
            for t in self.last_w.get(k, ()):
                add(t)
            for t in self.reads.get(k, ()):
                add(t)
        waits = []
        own = self.esem.get(e)
        for s, v in need.items():
            if self.known[e].get(s, 0) >= v:
                continue
            if (not is_dma) and s == own:
                if e == "pe" or not self.same_engine_sync:
                    continue
            waits.append((s, v))
            self.known[e][s] = v
        return waits

    def op(self, e, fn, reads=(), writes=()):
        waits = self._collect(e, reads, writes, False)
        s = self._eng_sem(e)
        self.count[e] += 1
        tok = (s, self.count[e])
        self.ops[e].append((waits, fn, tok, 1))
        for k in reads:
            self.reads.setdefault(k, []).append(tok)
        for k in writes:
            self.last_w[k] = [tok]
            self.reads[k] = []
        return tok

    def dma(self, e, fn, reads=(), writes=(), final=False):
        assert writes
        waits = self._collect(e, reads, writes, True)
        wk = writes[0]
        if wk not in self.dsem:
            self.dsem[wk] = [self._new_sem("ds%d" % len(self.dsem)), 0]
        ent = self.dsem[wk]
        ent[1] += 16
        tok = (ent[0], ent[1])
        self.ops[e].append((waits, fn, tok, 16))
        for k in reads:
            self.reads.setdefault(k, []).append(tok)
        for k in writes:
            self.last_w[k] = [tok]
            self.reads[k] = []
        if final:
            self.final_tokens.append(tok)
        return tok

    def barrier(self):
        toks = [(self.esem[e], self.count[e]) for e in self.esem if self.count[e] > 0]
        toks += [(s, c) for (s, c) in self.dsem.values()]
        for e in ENGS:
            for s, v in toks:
                if self.pending[e].get(s, 0) < v:
                    self.pending[e][s] = v

    def emit(self):
        nc = self.nc
        with nc.Block() as block:
            def run(e, eng):
                for waits, fn, tok, inc in self.ops[e]:
                    for s, v in waits:
                        eng.wait_ge(self.sems[s], v)
                    fn(eng).then_inc(self.sems[tok[0]], inc)
                if e == "sp":
                    fin = {}
                    for s, v in self.final_tokens:
                        fin[s] = max(fin.get(s, 0), v)
                    for s, v in fin.items():
                        eng.wait_ge(self.sems[s], v)

            @block.tensor
            def _(eng):
                run("pe", eng)

            @block.scalar
            def _(eng):
                run("act", eng)

            @block.vector
            def _(eng):
                run("dve", eng)

            @block.gpsimd
            def _(eng):
                run("pool", eng)

            @block.sync
            def _(eng):
                run("sp", eng)

    def close(self):
        for cm in reversed(self._ctx):
            cm.__exit__(None, None, None)
        self._ctx = []


F32 = mybir.dt.float32
BF16 = mybir.dt.bfloat16
I32 = mybir.dt.int32
AF = mybir.ActivationFunctionType
ALU = mybir.AluOpType
AX = mybir.AxisListType

D = 1024
EPS = 1e-6
GH, GDK, GDV = 4, 128, 256
GIN = 3088
SH, SDH = 16, 64


class Bld:
    def __init__(self, nc):
        self.nc = nc
        self.p = Prog(nc)
        self._cms = []
        self.t = {}

    def sb(self, name, shape, dt):
        cm = self.nc.sbuf_tensor(name, list(shape), dt)
        h = cm.__enter__()
        self._cms.append(cm)
        self.t[name] = h
        return h

    def ps(self, name, shape, dt=F32):
        cm = self.nc.psum_tensor(name, list(shape), dt)
        h = cm.__enter__()
        self._cms.append(cm)
        self.t[name] = h
        return h

    def release_to(self, mark):
        while len(self._cms) > mark:
            self._cms.pop().__exit__(None, None, None)

    def finish(self):
        self.p.emit()
        self.p.close()
        for cm in reversed(self._cms):
            cm.__exit__(None, None, None)

    def mm(self, out, lhsT, rhs, r, w, start=True, stop=True):
        return self.p.op("pe", lambda e: e.matmul(out, lhsT, rhs, start=start, stop=stop), r, w)

    def tr(self, out, in_, ident, r, w):
        return self.p.op("pe", lambda e: e.transpose(out, in_, ident), r, w)

    def act(self, out, in_, func, r, w, bias=None, scale=None, accum=None):
        kw = {}
        if bias is not None:
            kw["bias"] = bias
        if scale is not None:
            kw["scale"] = scale
        if accum is not None:
            kw["accum_out"] = accum
        return self.p.op("act", lambda e: e.activation(out, in_, func, **kw), r, w)

    def tt(self, out, in0, in1, op, r, w, eng="dve"):
        return self.p.op(eng, lambda e: e.tensor_tensor(out, in0, in1, op), r, w)

    def ts(self, out, in0, s1, s2, op0, op1, r, w, eng="dve"):
        if op1 is None:
            return self.p.op(eng, lambda e: e.tensor_scalar(out, in0, s1, None, op0), r, w)
        return self.p.op(eng, lambda e: e.tensor_scalar(out, in0, s1, s2, op0, op1), r, w)

    def stt(self, out, in0, scalar, in1, op0, op1, r, w):
        return self.p.op("dve", lambda e: e.scalar_tensor_tensor(out, in0, scalar, in1, op0, op1), r, w)

    def cp(self, out, in_, r, w, eng="dve"):
        if eng == "act":
            return self.p.op("act", lambda e: e.copy(out, in_), r, w)
        return self.p.op(eng, lambda e: e.tensor_copy(out, in_), r, w)

    def red(self, out, in_, r, w):
        return self.p.op("dve", lambda e: e.tensor_reduce(out, in_, AX.X, ALU.add), r, w)

    def mset(self, ap, val, w, eng="pool"):
        return self.p.op(eng, lambda e: e.memset(ap, val), (), w)

    def dma(self, out, in_, r, w, q="sp", final=False, slow=False):
        if slow:
            return self.p.dma(q, lambda e: e.dma_start(out=out, in_=in_, allow_slow_non_contiguous=True), r, w, final)
        return self.p.dma(q, lambda e: e.dma_start(out=out, in_=in_), r, w, final)

    def rstd(self, ssq, n, key):
        self.ts(ssq, ssq, 1.0 / n, EPS, ALU.mult, ALU.add, [key], [key])
        self.act(ssq, ssq, AF.Ln, [key], [key])
        self.act(ssq, ssq, AF.Exp, [key], [key], scale=-0.5)


def consts_np():
    c = {}
    c["ident_bf"] = np.eye(128, dtype=np.float32).astype(ml_dtypes.bfloat16)
    c["ident_f"] = np.eye(128, dtype=np.float32)
    u = np.triu(np.ones((128, 128), np.float32))
    c["u_f"] = u
    c["u4_f"] = np.tile(u, (1, 4))
    return c


def load_weights_l0(b, win, wup, ba, gain, wout, onorm):
    nc = b.nc
    Wa = b.sb("Wa", [128, 8, GIN], BF16)
    Wo = b.sb("Wo", [128, 8, D], BF16)
    stg = [b.t["wstg0"], b.t["wstg1"]]
    gcol = b.sb("gcol0", [128, 8], F32)
    b.dma(gcol[:], gain.rearrange("(c p) -> p c", p=128), [], ["gcol0"], slow=True)
    i = 0
    for c in range(8):
        for c0 in range(0, GIN, 1024):
            n = min(1024, GIN - c0)
            s = stg[i % 2]; k = "wstg%d" % (i % 2)
            b.dma(s[:, 0:n], win[c * 128:(c + 1) * 128, c0:c0 + n], [], [k], q="sp" if i % 2 == 0 else "pool")
            b.ts(Wa[:, c, c0:c0 + n], s[:, 0:n], gcol[:, c:c + 1], None, ALU.mult, None, [k, "gcol0"], ["Wa"],
                 eng="dve" if i % 2 == 0 else "pool")
            i += 1
    for c in range(8):
        s = stg[i % 2]; k = "wstg%d" % (i % 2)
        b.dma(s[:, 0:D], wout[c * 128:(c + 1) * 128, :], [], [k], q="sp" if i % 2 == 0 else "pool")
        b.cp(Wo[:, c, :], s[:, 0:D], [k], ["Wo"], eng="dve" if i % 2 == 0 else "pool")
        i += 1
    wupf = b.sb("wupf", [17, 512], F32)
    b.dma(wupf[0:16, :], wup, [], ["wupf"])
    b.dma(wupf[16:17, :], ba.rearrange("(o n) -> o n", o=1), [], ["wupf"])
    wupb = b.sb("wupb", [17, 512], BF16)
    b.cp(wupb[:], wupf[:], ["wupf"], ["wupb"])
    ong = b.sb("ong", [128, GH, GDV], F32)
    for h in range(GH):
        b.dma(ong[:, h, :], onorm.rearrange("(o n) -> o n", o=1).partition_broadcast(128), [], ["ong"], slow=True)
    return Wa, Wo, wupb, ong


def gla_tile(b, T, x_src, x1_dst, S, Sbf, W, cst, tag=""):
    Wa, Wo, wupb, ong = W
    t = b.t
    nc = b.nc
    ident_bf, ident_f, u_f, u4_f = cst
    xt, xn, hT = t["xt"], t["xn"], t["hT"]
    PA, PB, PC, PD = t["PA"], t["PB"], t["PC"], t["PD"]
    junk = t["junk"]
    b.dma(xt[0:T, :], x_src, [], ["xt"])
    b.act(junk[0:T, :], xt[0:T, :], AF.Square, ["xt"], ["junk", "ssq"], accum=t["ssq"][0:T, 0:1])
    b.rstd(t["ssq"][0:T, 0:1], D, "ssq")
    b.ts(xn[0:T, :], xt[0:T, :], t["ssq"][0:T, 0:1], None, ALU.mult, None, ["xt", "ssq"], ["xn"])
    PAb = PA[:].bitcast(BF16)
    for c in range(8):
        b.tr(PAb[:, c * 128:c * 128 + T], xn[0:T, c * 128:(c + 1) * 128], ident_bf[0:T, 0:T], ["xn", "ident_bf"], ["PA"])
    b.cp(hT[:, :, 0:T], PAb[:, 0:1024].rearrange("p (c t) -> p c t", c=8)[:, :, 0:T], ["PA"], ["hT"], eng="act")
    for j in range(8):
        for c in range(8):
            b.mm(PB[:, j * 128:j * 128 + T], Wa[:, c, j * 128:(j + 1) * 128], hT[:, c, 0:T],
                 ["Wa", "hT"], ["PB"], start=(c == 0), stop=(c == 7))
    for c in range(8):
        b.mm(PA[0:16, 512:512 + T], Wa[:, c, 3072:3088], hT[:, c, 0:T], ["Wa", "hT"], ["PA"],
             start=(c == 0), stop=(c == 7))
    acT = t["acT"]
    b.cp(acT[0:16, 0:T], PA[0:16, 512:512 + T], ["PA"], ["acT"])
    for n in range(2):
        for c in range(8):
            b.mm(PC[0:T, n * 512:(n + 1) * 512], hT[:, c, 0:T], Wa[:, c, 1024 + n * 512:1024 + (n + 1) * 512],
                 ["Wa", "hT"], ["PC"], start=(c == 0), stop=(c == 7))
    vbf = t["vbf"]
    b.cp(vbf[0:T, :], PC[0:T, :], ["PC"], ["vbf"], eng="act")
    for n in range(2):
        for c in range(8):
            b.mm(PD[0:T, n * 512:(n + 1) * 512], hT[:, c, 0:T], Wa[:, c, 2048 + n * 512:2048 + (n + 1) * 512],
                 ["Wa", "hT"], ["PD"], start=(c == 0), stop=(c == 7))
    sg = t["sg"]
    b.act(sg[0:T, :], PD[0:T, :], AF.Exp, ["PD"], ["sg"], scale=-1.0)
    b.ts(sg[0:T, :], sg[0:T, :], 1.0, None, ALU.add, None, ["sg"], ["sg"])
    b.p.op("dve", lambda e: e.reciprocal(sg[0:T, :], sg[0:T, :]), ["sg"], ["sg"])
    b.tt(sg[0:T, :], sg[0:T, :], PD[0:T, :], ALU.mult, ["sg", "PD"], ["sg"])
    b.mm(PA[0:T, 0:512], acT[0:17, 0:T], wupb[0:17, :], ["acT", "wupb"], ["PA"])
    spt = t["spt"]
    b.act(spt[0:T, :], PA[0:T, 0:512], AF.Exp, ["PA"], ["spt"], scale=-1.0)
    b.act(spt[0:T, :], spt[0:T, :], AF.Ln, ["spt", "one"], ["spt"], bias=t["one"][0:T, 0:1])
    for h in range(GH):
        b.mm(PA[:, h * 128:h * 128 + T], spt[0:T, h * 128:(h + 1) * 128], u_f[0:T, 0:T], ["spt", "u_f"], ["PA"])
    E1, E2, E3, nbl, dec = t["E1"], t["E2"], t["E3"], t["nbl"], t["dec"]
    PA4 = PA[:, 0:512].rearrange("p (h t) -> p h t", h=GH)
    b.act(E1[:].rearrange("p (h t) -> p h t", h=GH)[:, :, 0:T], PA4[:, :, 0:T], AF.Exp, ["PA"], ["E1"], scale=-1.0 / 16)
    b.act(E2[:].rearrange("p (h t) -> p h t", h=GH)[:, :, 0:T], PA4[:, :, 0:T], AF.Exp, ["PA"], ["E2"], scale=1.0 / 16)
    b.ts(nbl[:, :], PA4[:, :, T - 1], -1.0 / 16, None, ALU.mult, None, ["PA"], ["nbl"])
    for h in range(GH):
        b.act(E3[:, h * 128:h * 128 + T], PA[:, h * 128:h * 128 + T], AF.Exp, ["PA", "nbl"], ["E3"],
              scale=1.0 / 16, bias=nbl[:, h:h + 1])
    b.act(dec[:, :], nbl[:, :], AF.Exp, ["nbl"], ["dec"])
    qd, kd, kb = t["qd"], t["kd"], t["kb"]
    def v4(ap):
        return ap.rearrange("p (h t) -> p h t", h=GH)[:, :, 0:T]
    b.stt(v4(qd[:]), v4(PB[:, 0:512]), float(GDK) ** -0.5, v4(E1[:]), ALU.mult, ALU.mult, ["PB", "E1"], ["qd"])
    b.tt(v4(kd[:]), v4(PB[:, 512:1024]), v4(E2[:]), ALU.mult, ["PB", "E2"], ["kd"])
    b.tt(v4(kb[:]), v4(PB[:, 512:1024]), v4(E3[:]), ALU.mult, ["PB", "E3"], ["kb"])
    for h in range(GH):
        b.mm(PB[0:T, h * 128:h * 128 + T], kd[:, h * 128:h * 128 + T], qd[:, h * 128:h * 128 + T],
             ["kd", "qd"], ["PB"])
    attm = t["attm"]
    b.tt(v4(attm[0:T, :]), v4(PB[0:T, 0:512]), v4(u4_f[0:T, :]), ALU.mult, ["PB", "u4_f"], ["attm"])
    for h in range(GH):
        b.tr(PAb[0:T, h * 128:(h + 1) * 128], kb[:, h * 128:h * 128 + T], ident_bf[:, :], ["kb", "ident_bf"], ["PA"])
    kbt = t["kbt"]
    b.cp(kbt[0:T, :], PAb[0:T, 0:512], ["PA"], ["kbt"])
    for h in range(GH):
        b.mm(PC[0:T, h * 256:(h + 1) * 256], attm[0:T, h * 128:h * 128 + T], vbf[0:T, h * 256:(h + 1) * 256],
             ["attm", "vbf"], ["PC"], start=True, stop=False)
        b.mm(PC[0:T, h * 256:(h + 1) * 256], qd[:, h * 128:h * 128 + T], Sbf[:, h * 256:(h + 1) * 256],
             ["qd", "Sbf"], ["PC"], start=False, stop=True)
    for h in range(GH):
        b.mm(PD[:, h * 256:(h + 1) * 256], kbt[0:T, h * 128:(h + 1) * 128], vbf[0:T, h * 256:(h + 1) * 256],
             ["kbt", "vbf"], ["PD"])
    for h in range(GH):
        b.stt(S[:, h * 256:(h + 1) * 256], S[:, h * 256:(h + 1) * 256], dec[:, h:h + 1],
              PD[:, h * 256:(h + 1) * 256], ALU.mult, ALU.add, ["S", "dec", "PD"], ["S"])
    b.cp(Sbf[:], S[:], ["S"], ["Sbf"], eng="pool")
    sso = t["sso"]
    for h in range(GH):
        b.act(junk[0:T, 0:256], PC[0:T, h * 256:(h + 1) * 256], AF.Square, ["PC"], ["junk", "sso"],
              accum=sso[0:T, h:h + 1])
    b.rstd(sso[0:T, 0:GH], GDV, "sso")
    on = t["on"]
    for h in range(GH):
        b.stt(on[0:T, h * 256:(h + 1) * 256], PC[0:T, h * 256:(h + 1) * 256], sso[0:T, h:h + 1],
              ong[0:T, h, :], ALU.mult, ALU.mult, ["PC", "sso", "ong"], ["on"])
    og = t["og"]
    b.tt(og[0:T, :], on[0:T, :], sg[0:T, :], ALU.mult, ["on", "sg"], ["og"], eng="pool")
    for c in range(8):
        b.tr(PAb[:, c * 128:c * 128 + T], og[0:T, c * 128:(c + 1) * 128], ident_bf[0:T, 0:T], ["og", "ident_bf"], ["PA"])
    ogT = t["ogT"]
    b.cp(ogT[:, :, 0:T], PAb[:, 0:1024].rearrange("p (c t) -> p c t", c=8)[:, :, 0:T], ["PA"], ["ogT"], eng="act")
    for n in range(2):
        for c in range(8):
            b.mm(PD[0:T, n * 512:(n + 1) * 512], ogT[:, c, 0:T], Wo[:, c, n * 512:(n + 1) * 512],
                 ["ogT", "Wo"], ["PD"], start=(c == 0), stop=(c == 7))
    x1t = t["x1t"]
    b.tt(x1t[0:T, :], PD[0:T, :], xt[0:T, :], ALU.add, ["PD", "xt"], ["x1t"])
    if x1_dst is not None:
        b.dma(x1_dst, x1t[0:T, :], ["x1t"], ["x1dram" + tag], q="pool")
    return x1t


def alloc_common(b, gla=True):
    b.sb("xt", [128, D], F32)
    b.sb("xn", [128, D], BF16)
    b.sb("hT", [128, 8, 128], BF16)
    b.sb("junk", [128, D], BF16)
    b.sb("ssq", [128, 1], F32)
    b.sb("sg", [128, D], F32)
    b.sb("one", [128, 1], F32)
    b.sb("on", [128, D], F32)
    if gla:
        b.sb("acT", [17, 128], BF16)
        b.sb("vbf", [128, D], BF16)
        b.sb("spt", [128, 512], F32)
        b.sb("E1", [128, 512], F32)
        b.sb("E2", [128, 512], F32)
        b.sb("E3", [128, 512], F32)
        b.sb("nbl", [128, 4], F32)
        b.sb("dec", [128, 4], F32)
        b.sb("qd", [128, 512], BF16)
        b.sb("kd", [128, 512], BF16)
        b.sb("kb", [128, 512], BF16)
        b.sb("attm", [128, 512], BF16)
        b.sb("kbt", [128, 512], BF16)
        b.sb("sso", [128, 4], F32)
        b.sb("og", [128, D], BF16)
        b.sb("ogT", [128, 8, 128], BF16)
    b.sb("x1t", [128, D], F32)
    b.ps("PA", [128, 1024], F32)
    b.ps("PB", [128, 1024], F32)
    b.ps("PC", [128, 1024], F32)
    b.ps("PD", [128, 1024], F32)
    b.mset(b.t["one"][:], 1.0, ["one"])
    if gla:
        b.mset(b.t["acT"][:], 1.0, ["acT"])


def load_consts(b, d_ident_bf, d_ident_f, d_u, d_u4):
    ib = b.sb("ident_bf", [128, 128], BF16)
    i_f = b.sb("ident_f", [128, 128], F32)
    u = b.sb("u_f", [128, 128], F32)
    u4 = b.sb("u4_f", [128, 512], F32)
    b.dma(ib[:], d_ident_bf, [], ["ident_bf"])
    b.dma(i_f[:], d_ident_f, [], ["ident_f"])
    b.dma(u[:], d_u, [], ["u_f"])
    b.dma(u4[:], d_u4, [], ["u4_f"])
    return ib, i_f, u, u4


def consts_sb_np(j, ns):
    nk = 4 * ns
    kpos = (np.arange(nk)[:, None] * 128 + np.arange(128)[None, :])
    qpos = ((ns * np.arange(4)[:, None] + j) * 128 + np.arange(128)[None, :]).reshape(-1)
    m = (kpos[:, :, None] < qpos[None, None, :]).astype(np.float32)
    c = {}
    c["mask"] = np.ascontiguousarray(m.transpose(1, 0, 2)).astype(ml_dtypes.bfloat16)
    tl = np.tril(np.ones((128, 128), np.float32))
    c["tri"] = tl.astype(ml_dtypes.bfloat16)
    c["omt"] = (1.0 - tl).astype(ml_dtypes.bfloat16)
    return c


def load_weights_l1(b, winb, gain1, woutb, qn, kn, sbias, which="all"):
    if which == "kv":
        segs = [(1024, 2048)]; cb = {"k": 0, "v": 1024}
    elif which == "qg":
        segs = [(0, 1024), (3072, 1024)]; cb = {"q": 0, "g": 1024}
    else:
        segs = [(0, 2048), (2048, 2048)]; cb = {"q": 0, "k": 1024, "v": 2048, "g": 3072}
    ncol = sum(n for _, n in segs)
    W1 = b.sb("W1" + which, [128, 8, ncol], BF16)
    wk = "W1" + which
    gcol = b.sb("gcol1" + which, [128, 8], F32)
    b.dma(gcol[:], gain1.rearrange("(c p) -> p c", p=128), [], ["gcol1" + which], slow=True)
    stg = [b.t["wstg0"], b.t["wstg1"]]
    i = 0
    for c in range(8):
        lo = 0
        for (c0, n) in segs:
            for s0 in range(0, n, 1024):
                s = stg[i % 2]; k = "wstg%d" % (i % 2)
                b.dma(s[:, 0:1024], winb[c * 128:(c + 1) * 128, c0 + s0:c0 + s0 + 1024], [], [k],
                      q="sp" if i % 2 == 0 else "pool")
                b.ts(W1[:, c, lo + s0:lo + s0 + 1024], s[:, 0:1024], gcol[:, c:c + 1], None, ALU.mult, None,
                     [k, "gcol1" + which], [wk], eng="dve" if i % 2 == 0 else "pool")
                i += 1
            lo += n
    Wo1 = None
    if which != "kv":
        Wo1 = b.sb("Wo1", [128, 8, D], BF16)
        for c in range(8):
            s = stg[i % 2]; k = "wstg%d" % (i % 2)
            b.dma(s[:, 0:D], woutb[c * 128:(c + 1) * 128, :], [], [k], q="sp" if i % 2 == 0 else "pool")
            b.cp(Wo1[:, c, :], s[:, 0:D], [k], ["Wo1"], eng="dve" if i % 2 == 0 else "pool")
            i += 1
    qng = b.sb("qng" + which, [128, SH, SDH], F32)
    kng = b.sb("kng" + which, [128, SH, SDH], F32)
    for h in range(SH):
        b.dma(qng[:, h, :], qn.rearrange("(o n) -> o n", o=1).partition_broadcast(128), [], ["qng"], slow=True)
        b.dma(kng[:, h, :], kn.rearrange("(o n) -> o n", o=1).partition_broadcast(128), [], ["kng"], slow=True,
              q="pool")
    b.ts(qng[:], qng[:], float(SDH) ** -0.5, None, ALU.mult, None, ["qng"], ["qng"])
    biasb = b.sb("biasb" + which, [128, SH], F32)
    b.dma(biasb[:], sbias.rearrange("(o n) -> o n", o=1).partition_broadcast(128), [], ["biasb"], slow=True)
    return (W1, wk, cb), Wo1, qng, kng, biasb


def l1_norm_T(b, T, xsb, cst, xkey="x1t"):
    t = b.t
    ident_bf = cst[0]
    junk, ssq, xn, hT, PA = t["junk"], t["ssq"], t["xn"], t["hT"], t["PA"]
    b.act(junk[0:T, :], xsb, AF.Square, [xkey], ["junk", "ssq"], accum=ssq[0:T, 0:1])
    b.rstd(ssq[0:T, 0:1], D, "ssq")
    b.ts(xn[0:T, :], xsb, ssq[0:T, 0:1], None, ALU.mult, None, [xkey, "ssq"], ["xn"])
    PAb = PA[:].bitcast(BF16)
    for c in range(8):
        b.tr(PAb[:, c * 128:c * 128 + T], xn[0:T, c * 128:(c + 1) * 128], ident_bf[0:T, 0:T], ["xn", "ident_bf"], ["PA"])
    b.cp(hT[:, :, 0:T], PAb[:, 0:1024].rearrange("p (c t) -> p c t", c=8)[:, :, 0:T], ["PA"], ["hT"], eng="act")
    return hT


def proj_tok(b, T, PS, pskey, W1k, which):
    hT = b.t["hT"]
    W1, wkey, cb = W1k
    col0 = cb[which]
    for n in range(2):
        for c in range(8):
            b.mm(PS[0:T, n * 512:(n + 1) * 512], hT[:, c, 0:T], W1[:, c, col0 + n * 512:col0 + (n + 1) * 512],
                 [wkey, "hT"], [pskey], start=(c == 0), stop=(c == 7))


def headnorm(b, T, PS, pskey, gain_t, gkey, out_f, okey):
    t = b.t
    sq, ssh = t["on"], t["ssh"]
    b.act(sq[0:T, :], PS[0:T, :], AF.Square, [pskey], ["on"])
    b.red(ssh[0:T, :], sq[0:T, :].rearrange("p (h d) -> p h d", h=SH), ["on"], ["ssh"])
    b.rstd(ssh[0:T, :], SDH, "ssh")
    o3 = out_f[0:T, :].rearrange("p (h d) -> p h d", h=SH)
    b.tt(o3, PS[0:T, :].rearrange("p (h d) -> p h d", h=SH),
         ssh[0:T, :].unsqueeze(2).to_broadcast([T, SH, SDH]), ALU.mult, [pskey, "ssh"], [okey])
    b.tt(o3, o3, gain_t[0:T, :, :], ALU.mult, [okey, gkey], [okey], eng="pool")


def to_pairT(b, T, src_bf, skey, dst, dkey, col0, cst):
    PA = b.t["PA"]
    PAb = PA[:].bitcast(BF16)
    ident_bf = cst[0]
    for c in range(8):
        b.tr(PAb[:, c * 128:c * 128 + T], src_bf[0:T, c * 128:(c + 1) * 128], ident_bf[0:T, 0:T],
             [skey, "ident_bf"], ["PA"])
    b.cp(dst[:, :, col0:col0 + T], PAb[:, 0:1024].rearrange("p (c t) -> p c t", c=8)[:, :, 0:T], ["PA"], [dkey])


def l1_kv_tile(b, T, xsb, W1, L1W, cst, k_out, v_out, KT_d, V_d, tok0, tag, xkey="x1t", ktkey="KT_d", vkey="V_d"):
    t = b.t
    W1_, Wo1, qng, kng, biasb = L1W
    PC, PD = t["PC"], t["PD"]
    l1_norm_T(b, T, xsb, cst, xkey)
    proj_tok(b, T, PC, "PC", W1, "k")
    proj_tok(b, T, PD, "PD", W1, "v")
    kf, vf, kbf, vb2, ktt = t["kf"], t["vf"], t["kbf"], t["vb2"], t["ktt"]
    headnorm(b, T, PC, "PC", kng, "kng", kf, "kf")
    if k_out is not None:
        b.dma(k_out, kf[0:T, :], ["kf"], ["kout"], q="pool", final=True)
    b.cp(kbf[0:T, :], kf[0:T, :], ["kf"], ["kbf"], eng="pool")
    to_pairT(b, T, kbf, "kbf", ktt, "ktt", 0, cst)
    b.dma(KT_d[:, :, tok0:tok0 + T].rearrange("c p t -> p c t"), ktt[:, :, 0:T], ["ktt"], [ktkey], slow=True)
    b.cp(vf[0:T, :], PD[0:T, :], ["PD"], ["vf"], eng="act")
    if v_out is not None:
        b.dma(v_out, vf[0:T, :], ["vf"], ["vout"], q="pool", final=True)
    b.cp(vb2[0:T, :], vf[0:T, :], ["vf"], ["vb2"], eng="pool")
    b.dma(V_d[tok0:tok0 + T, :], vb2[0:T, :], ["vb2"], [vkey])


def l1_qg_tile(b, T, xsb, W1, L1W, cst, qT, sgT, col0, xkey="x1own"):
    t = b.t
    W1_, Wo1, qng, kng, biasb = L1W
    PC, PD = t["PC"], t["PD"]
    l1_norm_T(b, T, xsb, cst, xkey)
    proj_tok(b, T, PC, "PC", W1, "q")
    proj_tok(b, T, PD, "PD", W1, "g")
    kf, kbf, sg, vb2 = t["kf"], t["kbf"], t["sg"], t["vb2"]
    headnorm(b, T, PC, "PC", qng, "qng", kf, "kf")
    b.cp(kbf[0:T, :], kf[0:T, :], ["kf"], ["kbf"], eng="pool")
    to_pairT(b, T, kbf, "kbf", qT, "qT", col0, cst)
    b.act(sg[0:T, :], PD[0:T, :], AF.Exp, ["PD"], ["sg"], scale=-1.0)
    b.ts(sg[0:T, :], sg[0:T, :], 1.0, None, ALU.add, None, ["sg"], ["sg"])
    b.p.op("dve", lambda e: e.reciprocal(sg[0:T, :], sg[0:T, :]), ["sg"], ["sg"])
    b.tt(vb2[0:T, :], sg[0:T, :], PD[0:T, :], ALU.mult, ["sg", "PD"], ["vb2"])
    to_pairT(b, T, vb2, "vb2", sgT, "sgT", col0, cst)


def sb_unit(b, hp, KTb, Vb, kcol, TQ, qT, mask_ap, first, last, Z, ACC, O, biasb, h, tri, omt):
    t = b.t
    e_t, L_t, P_t, A_t = t["e_t"], t["L_t"], t["P_t"], t["A_t"]
    p0 = 64 * hp
    b.mm(Z[:, 0:TQ], KTb[p0:p0 + 64, :], qT[p0:p0 + 64, 0:TQ], ["KTs", "qT"], ["Z"])
    b.act(e_t[:, 0:TQ], Z[:, 0:TQ], AF.Exp, ["Z", "biasb"], ["e_t"], bias=biasb[:, h:h + 1])
    if mask_ap is not None:
        b.tt(e_t[:, 0:TQ], e_t[:, 0:TQ], mask_ap, ALU.mult, ["e_t", "mask", "masks"], ["e_t"])
    b.act(L_t[:, 0:TQ], e_t[:, 0:TQ], AF.Ln, ["e_t", "one"], ["L_t"], bias=t["one"][:, 0:1])
    b.mm(ACC[:, 0:TQ], tri[:, :], L_t[:, 0:TQ], ["L_t", "tri"], ["ACC"], start=first, stop=False)
    b.act(P_t[:, 0:TQ], ACC[:, 0:TQ], AF.Exp, ["ACC"], ["P_t"], scale=-1.0)
    b.mm(ACC[:, 0:TQ], omt[:, :], L_t[:, 0:TQ], ["L_t", "omt"], ["ACC"], start=False, stop=last)
    b.tt(A_t[:, 0:TQ], e_t[:, 0:TQ], P_t[:, 0:TQ], ALU.mult, ["e_t", "P_t"], ["A_t"])
    b.mm(O[:, 0:TQ], Vb, A_t[:, 0:TQ], ["Vs", "A_t"], ["O"], start=first, stop=last)


def alloc_l1(b, nkeys_max, TQ=512):
    b.sb("ssh", [128, SH], F32)
    b.sb("kf", [128, D], F32)
    b.sb("vf", [128, D], F32)
    b.sb("kbf", [128, D], BF16)
    b.sb("vb2", [128, D], BF16)
    b.sb("ktt", [128, 8, 128], BF16)
    b.sb("e_t", [128, TQ], F32)
    b.sb("L_t", [128, TQ], BF16)
    b.sb("P_t", [128, TQ], F32)
    b.sb("A_t", [128, TQ], BF16)
    b.sb("KTs", [128, nkeys_max], BF16)
    b.sb("Vs", [128, nkeys_max // 128, 128], BF16)
    b.sb("qT", [128, 8, TQ], BF16)
    b.sb("sgT", [128, 8, TQ], BF16)
    b.sb("ogT", [128, 8, TQ], BF16) if "ogT" not in b.t else None
    b.sb("ogT1", [128, 8, TQ], BF16)
    b.sb("x1own", [128, TQ // 128, D], F32)
    b.sb("yt", [128, D], F32)


def sb_group(b, L1W, cst, sbc, nkb, KT_d, V_d, qT, sgT, TQ, y_dst_tiles, x1own, ktkey="KT_d", vkey="V_d"):
    t = b.t
    W1_, Wo1, qng, kng, biasb = L1W
    maskt, tri, omt = sbc
    nmask = maskt.shape[1]
    KTs, Vs, ogT1 = t["KTs"], t["Vs"], t["ogT1"]
    PA, PB = t["PA"], t["PB"]
    Z, ACC, O = PB[:, 0:512], PB[:, 512:1024], PA[:, 512:1024]
    for pr in range(8):
        b.dma(KTs[:, 0:nkb * 128], KT_d[pr, :, 0:nkb * 128], [ktkey], ["KTs"])
        b.dma(Vs[:, 0:nkb, :], V_d[0:nkb * 128, pr * 128:(pr + 1) * 128].rearrange("(k p) c -> p k c", p=128),
              [vkey], ["Vs"], q="pool")
        for hp in range(2):
            h = 2 * pr + hp
            for i, kb in enumerate(range(nkb - 1, -1, -1)):
                mrel = kb - (nkb - nmask)
                m_ap = maskt[:, mrel, 0:TQ] if mrel >= 0 else None
                sb_unit(b, hp, KTs[:, kb * 128:(kb + 1) * 128], Vs[:, kb, :], kb, TQ, qT[:, pr, :], m_ap,
                        i == 0, kb == 0, Z, ACC, O, biasb, h, tri, omt)
            p0 = 64 * hp
            b.tt(ogT1[p0:p0 + 64, pr, 0:TQ], O[p0:p0 + 64, 0:TQ], sgT[p0:p0 + 64, pr, 0:TQ], ALU.mult,
                 ["O", "sgT"], ["ogT1"])
    PD = t["PD"]
    yt = t["yt"]
    Tt = min(128, TQ)
    for ti in range(max(1, TQ // 128)):
        for n in range(2):
            for c in range(8):
                b.mm(PD[0:Tt, n * 512:(n + 1) * 512], ogT1[:, c, ti * 128:ti * 128 + Tt], Wo1[:, c, n * 512:(n + 1) * 512],
                     ["ogT1", "Wo1"], ["PD"], start=(c == 0), stop=(c == 7))
        b.tt(yt[0:Tt, :], PD[0:Tt, :], x1own[0:Tt, ti, :], ALU.add, ["PD", "x1own"], ["yt"])
        b.dma(y_dst_tiles[ti], yt[0:Tt, :], ["yt"], ["ydst"], q="pool", final=True)


def l1_kv_rows(b, T, xsb, W1, L1W, cst, k_out, v_out, tag, xkey):
    t = b.t
    _, Wo1, qng, kng, biasb = L1W
    PC, PD = t["PC"], t["PD"]
    l1_norm_T(b, T, xsb, cst, xkey)
    proj_tok(b, T, PC, "PC", W1, "k")
    proj_tok(b, T, PD, "PD", W1, "v")
    kf, vf = t["kf"], t["vf"]
    headnorm(b, T, PC, "PC", kng, "kng", kf, "kf")
    b.dma(k_out, kf[0:T, :], ["kf"], ["kout"], q="pool", final=True)
    b.cp(vf[0:T, :], PD[0:T, :], ["PD"], ["vf"], eng="act")
    b.dma(v_out, vf[0:T, :], ["vf"], ["vout"], q="pool", final=True)


def build_program(nc, cfg):
    NTOK, NS, NSAMP, NPG, NPHYS = cfg["NTOK"], cfg["NS"], cfg["NSAMP"], cfg["NPG"], cfg["NPHYS"]
    NOWN = NTOK // NS
    NG = NOWN // 512
    NT = NTOK // 128
    NKS = (NPG + 1) * 128
    def din(n, s, dt=F32): return nc.dram_tensor(n, list(s), dt, kind="ExternalInput").ap()
    def dout(n, s, dt=F32): return nc.dram_tensor(n, list(s), dt, kind="ExternalOutput").ap()
    xb = din("xb", [NTOK, D]); own_rows = din("own_rows", [128, NOWN // 128], I32)
    xs = din("xs", [NSAMP * 8, D]); state = din("state", [NSAMP, GH, GDK, GDV])
    ck = din("ck", [NPHYS * 128, D]); cv = din("cv", [NPHYS * 128, D]); pt = din("pt", [NSAMP * NPG], I32)
    gain = din("gain", [2, D]); win_a = din("win_a", [D, GIN]); wup = din("wup", [16, 512]); ba = din("ba", [512])
    onorm = din("onorm", [256]); wout_a = din("wout_a", [D, D]); win_b = din("win_b", [D, 4096])
    qn = din("qn", [64]); kn = din("kn", [64]); sbias = din("sbias", [16]); wout_b = din("wout_b", [D, D])
    c_ib = din("c_ib", [128, 128], BF16); c_if = din("c_if", [128, 128]); c_u = din("c_u", [128, 128]); c_u4 = din("c_u4", [128, 512])
    NM = 4 * NS
    c_mask = din("c_mask", [128, NM, 512], BF16); c_tri = din("c_tri", [128, 128], BF16); c_omt = din("c_omt", [128, 128], BF16)
    c_masknew = din("c_masknew", [128, 128], BF16); c_iota = din("c_iota", [128, 1])
    y_own = dout("y_own", [NOWN, D]); ys = dout("ys", [NSAMP * 8, D])
    st_p = dout("st_p", [GH, GDK, GDV]); st_s = dout("st_s", [NSAMP, GH, GDK, GDV])
    k_all = dout("k_all", [NTOK, D]); v_all = dout("v_all", [NTOK, D])
    k_s = dout("k_s", [NSAMP * 8, D]); v_s = dout("v_s", [NSAMP * 8, D])
    x1_d = nc.dram_tensor("x1_d", [NTOK, D], F32).ap()
    xs1_d = nc.dram_tensor("xs1_d", [NSAMP * 8, D], F32).ap()
    KT_d = nc.dram_tensor("KT_d", [8, 128, NTOK], BF16).ap()
    V_d = nc.dram_tensor("V_d", [NTOK, D], BF16).ap()
    KTs_d = [nc.dram_tensor("KTs_d%d" % s, [8, 128, NKS], BF16).ap() for s in range(NSAMP)]
    Vs_d = [nc.dram_tensor("Vs_d%d" % s, [NKS, D], BF16).ap() for s in range(NSAMP)]

    b = Bld(nc)
    alloc_common(b, gla=False)
    b.sb("wstg0", [128, 1024], F32); b.sb("wstg1", [128, 1024], F32)
    b.sb("ssh", [128, SH], F32); b.sb("kf", [128, D], F32); b.sb("vf", [128, D], F32)
    b.sb("kbf", [128, D], BF16); b.sb("vb2", [128, D], BF16); b.sb("ktt", [128, 8, 128], BF16)
    cst = load_consts(b, c_ib, c_if, c_u, c_u4)
    zt = b.sb("zt", [128, D], BF16)
    b.mset(zt[:], 0.0, ["zt"])
    mark = len(b._cms)
    alloc_gla(b)
    S = b.sb("S", [128, 1024], F32); Sbf = b.sb("Sbf", [128, 1024], BF16)
    W = load_weights_l0(b, win_a, wup, ba, gain[0, :], wout_a, onorm)
    L1kv = load_weights_l1(b, win_b, gain[1, :], wout_b, qn, kn, sbias, which="kv")
    b.mset(S[:], 0.0, ["S"]); b.mset(Sbf[:], 0.0, ["Sbf"])
    for i in range(NT):
        x1t = gla_tile(b, 128, xb[i * 128:(i + 1) * 128, :], x1_d[i * 128:(i + 1) * 128, :], S, Sbf, W, cst, tag="")
        l1_kv_tile(b, 128, x1t[:, :], L1kv[0], L1kv, cst, k_all[i * 128:(i + 1) * 128, :], v_all[i * 128:(i + 1) * 128, :], KT_d, V_d, i * 128, "", xkey="x1t")
    b.dma(st_p.rearrange("h d v -> d h v"), S[:].rearrange("p (h v) -> p h v", h=GH), ["S"], ["st_p"], final=True)
    for s in range(NSAMP):
        b.dma(S[:].rearrange("p (h v) -> p h v", h=GH), state[s].rearrange("h d v -> d h v"), [], ["S"])
        b.cp(Sbf[:], S[:], ["S"], ["Sbf"], eng="pool")
        x1t = gla_tile(b, 8, xs[s * 8:(s + 1) * 8, :], xs1_d[s * 8:(s + 1) * 8, :], S, Sbf, W, cst, tag="s")
        b.dma(st_s[s].rearrange("h d v -> d h v"), S[:].rearrange("p (h v) -> p h v", h=GH), ["S"], ["st_s"], final=True)
        b.dma(KTs_d[s][:, :, NPG * 128:NKS].rearrange("c p t -> p c t"),
              zt[:].rearrange("p (c t) -> p c t", c=8), ["zt"], ["KTs_dS"], slow=True)
        b.dma(Vs_d[s][NPG * 128:NKS, :], zt[:], ["zt"], ["Vs_dS"])
        l1_kv_tile_s(b, 8, x1t[0:8, :], L1kv[0], L1kv, cst, k_s[s * 8:(s + 1) * 8, :], v_s[s * 8:(s + 1) * 8, :],
                     KTs_d[s], Vs_d[s], NPG * 128, "s%d" % s)
    b.p.barrier()
    b.release_to(mark)
    TQ = 512
    b.sb("qT", [128, 8, TQ], BF16); b.sb("sgT", [128, 8, TQ], BF16); b.sb("ogT1", [128, 8, TQ], BF16)
    b.sb("x1own", [128, TQ // 128, D], F32)
    tri = b.sb("tri", [128, 128], BF16); omt = b.sb("omt", [128, 128], BF16)
    b.dma(tri[:], c_tri, [], ["tri"]); b.dma(omt[:], c_omt, [], ["omt"])
    L1qg = load_weights_l1(b, win_b, gain[1, :], wout_b, qn, kn, sbias, which="qg")
    biasb = L1qg[4]
    x1own = b.t["x1own"]
    orow = b.sb("orow", [128, NOWN // 128], I32)
    b.dma(orow[:], own_rows, [], ["orow"])
    mark2 = len(b._cms)
    b.sb("e_t", [128, 128], F32); b.sb("L_t", [128, 128], BF16); b.sb("P_t", [128, 128], F32); b.sb("A_t", [128, 128], BF16)
    masknew = b.sb("masknew", [128, 128], BF16)
    b.dma(masknew[:], c_masknew, [], ["masknew"])
    biasfull = b.sb("biasfull", [128, SH, 8], F32)
    b.cp(biasfull[:], biasb[:, :].unsqueeze(2).to_broadcast([128, SH, 8]), ["biasb"], ["biasfull"])
    biasfull2 = biasfull[:].rearrange("p h q -> p (h q)")
    Qbd = b.sb("Qbd", [128, 8, 16], BF16)
    b.mset(Qbd[:], 0.0, ["Qbd"])
    for par in range(2):
        b.sb("kpg%d" % par, [128, D], F32); b.sb("vpg%d" % par, [128, D], F32)
        b.sb("kbfp%d" % par, [128, D], BF16); b.sb("vbp%d" % par, [128, D], BF16); b.sb("kttp%d" % par, [128, 8, 128], BF16)
    ptb = b.sb("ptb", [128, NSAMP * NPG], I32); ptf = b.sb("ptf", [128, NSAMP * NPG], F32)
    idx = b.sb("idx", [128, NSAMP * NPG], I32); iot = b.sb("iot", [128, 1], F32)
    b.dma(ptb[:], pt.rearrange("(o n) -> o n", o=1).partition_broadcast(128), [], ["ptb"], slow=True)
    b.dma(iot[:], c_iota, [], ["iot"])
    b.cp(ptf[:], ptb[:], ["ptb"], ["ptf"])
    b.ts(ptf[:], ptf[:], 128.0, iot[:, 0:1], ALU.mult, ALU.add, ["ptf", "iot"], ["ptf"])
    b.cp(idx[:], ptf[:], ["ptf"], ["idx"])
    for s in range(NSAMP):
        b.dma(x1own[0:8, 0, :], xs1_d[s * 8:(s + 1) * 8, :], ["x1drams"], ["x1own"])
        l1_qg_tile(b, 8, x1own[0:8, 0, :], L1qg[0], L1qg, cst, b.t["qT"], b.t["sgT"], 0)
        sample_attn(b, s, L1qg, cst, (masknew, tri, omt, biasfull2), NPG, ck, cv, idx,
                    KTs_d[s][:, :, NPG * 128:NKS], Vs_d[s][NPG * 128:NKS, :], b.t["qT"], b.t["sgT"],
                    ys[s * 8:(s + 1) * 8, :], x1own, zt)
    b.p.barrier()
    b.release_to(mark2)
    b.sb("KTs", [128, NTOK], BF16); b.sb("Vs", [128, NTOK // 128, 128], BF16)
    for nm_ in ("pe00", "pe01", "pe10", "pe11", "pL00", "pL01", "pL10", "pL11", "pP0", "pP1", "pA0", "pA1"):
        b.sb(nm_, [128, TQ], BF16)
    maskt = b.sb("mask", [128, NM, 512], BF16)
    b.dma(maskt[:], c_mask, [], ["mask"])
    for g in range(NG):
        for ti in range(4):
            lt = g * 4 + ti
            b.p.dma("pool", lambda e, lt=lt, ti=ti: e.indirect_dma_start(
                out=x1own[:, ti, :], out_offset=None, in_=x1_d[:, :],
                in_offset=bass.IndirectOffsetOnAxis(ap=orow[:, lt:lt + 1], axis=0)), ["orow", "x1dram"], ["x1own"])
            l1_qg_tile(b, 128, x1own[:, ti, :], L1qg[0], L1qg, cst, b.t["qT"], b.t["sgT"], ti * 128)
        b.p.barrier()
        nkb = 4 * NS * (g + 1)
        sb_group3(b, L1qg, cst, (maskt, tri, omt), nkb, KT_d, V_d, b.t["qT"], b.t["sgT"], 512,
                  [y_own[(g * 4 + ti) * 128:(g * 4 + ti + 1) * 128, :] for ti in range(4)], x1own)
        b.p.barrier()
    b.finish()
    return b


def l1_kv_tile_s(b, T, xsb, W1, L1W, cst, k_out, v_out, KT_d, V_d, tok0, tag):
    l1_kv_tile(b, T, xsb, W1, L1W, cst, k_out, v_out, KT_d, V_d, tok0, tag, xkey="x1t",
               ktkey="KTs_dS", vkey="Vs_dS")


def alloc_l1b(b, nkeys_max, TQ=512):
    b.sb("e_t", [128, TQ], F32)
    b.sb("L_t", [128, TQ], BF16)
    b.sb("P_t", [128, TQ], F32)
    b.sb("A_t", [128, TQ], BF16)
    b.sb("KTs", [128, nkeys_max], BF16)
    b.sb("Vs", [128, nkeys_max // 128, 128], BF16)
    b.sb("qT", [128, 8, TQ], BF16)
    b.sb("sgT", [128, 8, TQ], BF16)
    b.sb("ogT1", [128, 8, TQ], BF16)
    b.sb("x1own", [128, TQ // 128, D], F32)
    b.sb("yt", [128, D], F32)


def alloc_gla(b):
    b.sb("acT", [17, 128], BF16)
    b.sb("vbf", [128, D], BF16)
    b.sb("spt", [128, 512], F32)
    b.sb("E1", [128, 512], F32)
    b.sb("E2", [128, 512], F32)
    b.sb("E3", [128, 512], F32)
    b.sb("nbl", [128, 4], F32)
    b.sb("dec", [128, 4], F32)
    b.sb("qd", [128, 512], BF16)
    b.sb("kd", [128, 512], BF16)
    b.sb("kb", [128, 512], BF16)
    b.sb("attm", [128, 512], BF16)
    b.sb("kbt", [128, 512], BF16)
    b.sb("sso", [128, 4], F32)
    b.sb("og", [128, D], BF16)
    b.sb("ogT", [128, 8, 128], BF16)
    b.mset(b.t["acT"][:], 1.0, ["acT"])


def sb_outproj(b, L1W, TQ, y_dst_tiles, x1own):
    t = b.t
    Wo1 = L1W[1]
    PD, yt, ogT1 = t["PD"], t["on"], t["ogT1"]
    Tt = min(128, TQ)
    for ti in range(max(1, TQ // 128)):
        for n in range(2):
            for c in range(8):
                b.mm(PD[0:Tt, n * 512:(n + 1) * 512], ogT1[:, c, ti * 128:ti * 128 + Tt], Wo1[:, c, n * 512:(n + 1) * 512],
                     ["ogT1", "Wo1"], ["PD"], start=(c == 0), stop=(c == 7))
        b.tt(yt[0:Tt, :], PD[0:Tt, :], x1own[0:Tt, ti, :], ALU.add, ["PD", "x1own"], ["on"])
        b.dma(y_dst_tiles[ti], yt[0:Tt, :], ["on"], ["ydst"], q="pool", final=True)


def sb_group2(b, L1W, cst, sbc, nkb, KT_d, V_d, qT, sgT, TQ, y_dst_tiles, x1own):
    t = b.t
    _, Wo1, qng, kng, biasb = L1W
    maskt, tri, omt = sbc
    nmask = maskt.shape[1]
    KTs, Vs, ogT1 = t["KTs"], t["Vs"], t["ogT1"]
    PA, PB, PC, PD = t["PA"], t["PB"], t["PC"], t["PD"]
    Zs = [PB[:, 0:512], PC[:, 0:512]]
    ACCs = [PB[:, 512:1024], PC[:, 512:1024]]
    Os = [PA[:, 512:1024], PD[:, 0:512]]
    one = t["one"]
    for pr in range(8):
        b.dma(KTs[:, 0:nkb * 128], KT_d[pr, :, 0:nkb * 128], ["KT_d"], ["KTs"])
        b.dma(Vs[:, 0:nkb, :], V_d[0:nkb * 128, pr * 128:(pr + 1) * 128].rearrange("(k p) c -> p k c", p=128),
              ["V_d"], ["Vs"], q="pool")
        for i, kb in enumerate(range(nkb - 1, -1, -1)):
            mrel = kb - (nkb - nmask)
            m_ap = maskt[:, mrel, 0:TQ] if mrel >= 0 else None
            first, last = (i == 0), (kb == 0)
            KTb = KTs[:, kb * 128:(kb + 1) * 128]
            Vb = Vs[:, kb, :]
            L2 = range(2)
            e = [t["e_t"], t["e_t2"]]; Lt = [t["L_t"], t["L_t2"]]; P = [t["P_t"], t["P_t2"]]; A = [t["A_t"], t["A_t2"]]
            k = lambda n, l: n + str(l)
            for l in L2:
                p0 = 64 * l
                b.mm(Zs[l][:, 0:TQ], KTb[p0:p0 + 64, :], qT[p0:p0 + 64, pr, 0:TQ], ["KTs", "qT"], [k("Z", l)])
            for l in L2:
                h = 2 * pr + l
                b.act(e[l][:, 0:TQ], Zs[l][:, 0:TQ], AF.Exp, [k("Z", l), "biasb"], [k("e", l)], bias=biasb[:, h:h + 1])
            if m_ap is not None:
                for l in L2:
                    b.tt(e[l][:, 0:TQ], e[l][:, 0:TQ], m_ap, ALU.mult, [k("e", l), "mask"], [k("e", l)],
                         eng="dve" if l == 0 else "pool")
            for l in L2:
                b.act(Lt[l][:, 0:TQ], e[l][:, 0:TQ], AF.Ln, [k("e", l), "one"], [k("L", l)], bias=one[:, 0:1])
            for l in L2:
                b.mm(ACCs[l][:, 0:TQ], tri[:, :], Lt[l][:, 0:TQ], [k("L", l), "tri"], [k("ACC", l)], start=first, stop=False)
            for l in L2:
                b.act(P[l][:, 0:TQ], ACCs[l][:, 0:TQ], AF.Exp, [k("ACC", l)], [k("P", l)], scale=-1.0)
            for l in L2:
                b.mm(ACCs[l][:, 0:TQ], omt[:, :], Lt[l][:, 0:TQ], [k("L", l), "omt"], [k("ACC", l)], start=False, stop=last)
            for l in L2:
                b.tt(A[l][:, 0:TQ], e[l][:, 0:TQ], P[l][:, 0:TQ], ALU.mult, [k("e", l), k("P", l)], [k("A", l)])
            for l in L2:
                b.mm(Os[l][:, 0:TQ], Vb, A[l][:, 0:TQ], ["Vs", k("A", l)], [k("O", l)], start=first, stop=last)
        for l in range(2):
            p0 = 64 * l
            b.tt(ogT1[p0:p0 + 64, pr, 0:TQ], Os[l][p0:p0 + 64, 0:TQ], sgT[p0:p0 + 64, pr, 0:TQ], ALU.mult,
                 ["O" + str(l), "sgT"], ["ogT1"])
    b.p.barrier()
    sb_outproj(b, L1W, TQ, y_dst_tiles, x1own)


def sample_attn(b, s, L1W, cst, smc, NPG, ck, cv, idx, KTn_d, Vn_d, qT, sgT, y_dst, x1own, zt):
    t = b.t
    _, Wo1, qng, kng, biasb = L1W
    masknew, tri, omt, biasfull = smc
    ident_bf = cst[0]
    PA, PB, PC = t["PA"], t["PB"], t["PC"]
    PAb = PA[:].bitcast(BF16)
    Z, ACC, O = PB[:, 0:128], PB[:, 512:640], PC[:, 0:128]
    Qbd, ogT1 = t["Qbd"], t["ogT1"]
    one = t["one"]
    b.cp(Qbd[0:64, :, 0:8], qT[0:64, :, 0:8], ["qT"], ["Qbd"])
    b.cp(Qbd[64:128, :, 8:16], qT[64:128, :, 0:8], ["qT"], ["Qbd"])
    b.mm(O, zt[:, 0:128], zt[:, 0:128], ["zt"], ["O"], start=True, stop=False)
    nblk = NPG + 1
    kbl = list(range(nblk - 1, -1, -1))

    def prep(i):
        kb = kbl[i]
        par = i % 2
        kpg, vpg, kbf, vbp, ktt = t["kpg%d" % par], t["vpg%d" % par], t["kbfp%d" % par], t["vbp%d" % par], t["kttp%d" % par]
        kk = lambda n: n + str(par)
        if kb == NPG:
            b.dma(ktt[:, :, :], KTn_d.rearrange("c p t -> p c t"), ["KTs_dS"], [kk("ktt")], slow=True)
            b.dma(vbp[:, :], Vn_d, ["Vs_dS"], [kk("vbp")])
        else:
            c = s * NPG + kb
            b.p.dma("pool", lambda e_, c=c, kpg=kpg: e_.indirect_dma_start(
                out=kpg[:, :], out_offset=None, in_=ck[:, :],
                in_offset=bass.IndirectOffsetOnAxis(ap=idx[:, c:c + 1], axis=0)), ["idx"], [kk("kpg")])
            b.p.dma("pool", lambda e_, c=c, vpg=vpg: e_.indirect_dma_start(
                out=vpg[:, :], out_offset=None, in_=cv[:, :],
                in_offset=bass.IndirectOffsetOnAxis(ap=idx[:, c:c + 1], axis=0)), ["idx"], [kk("vpg")])
            b.cp(kbf[:, :], kpg[:, :], [kk("kpg")], [kk("kbf")], eng="dve")
            for c8 in range(8):
                b.tr(PAb[:, c8 * 128:(c8 + 1) * 128], kbf[:, c8 * 128:(c8 + 1) * 128], ident_bf[:, :],
                     [kk("kbf"), "ident_bf"], ["PA"])
            b.cp(ktt[:, :, :], PAb[:, 0:1024].rearrange("p (c t) -> p c t", c=8), ["PA"], [kk("ktt")])
            b.cp(vbp[:, :], vpg[:, :], [kk("vpg")], [kk("vbp")], eng="act")

    def unit(i):
        kb = kbl[i]
        par = i % 2
        ktt, vbp = t["kttp%d" % par], t["vbp%d" % par]
        kk = lambda n: n + str(par)
        first, last = (i == 0), (kb == 0)
        for pr in range(8):
            b.mm(Z[:, pr * 16:(pr + 1) * 16], ktt[:, pr, :], Qbd[:, pr, :], [kk("ktt"), "Qbd"], ["Zs"])
        e, Lt, P, A = t["e_t"], t["L_t"], t["P_t"], t["A_t"]
        b.tt(e[:, 0:128], Z, biasfull[:, :], ALU.add, ["Zs", "biasfull"], ["e0"])
        b.act(e[:, 0:128], e[:, 0:128], AF.Exp, ["e0"], ["e0"])
        if kb == NPG:
            b.tt(e[:, 0:128], e[:, 0:128], masknew[:, :], ALU.mult, ["e0", "masknew"], ["e0"])
        b.act(Lt[:, 0:128], e[:, 0:128], AF.Ln, ["e0", "one"], ["L0"], bias=one[:, 0:1])
        b.mm(ACC, tri[:, :], Lt[:, 0:128], ["L0", "tri"], ["ACCs"], start=first, stop=False)
        b.act(P[:, 0:128], ACC, AF.Exp, ["ACCs"], ["P0"], scale=-1.0)
        b.mm(ACC, omt[:, :], Lt[:, 0:128], ["L0", "omt"], ["ACCs"], start=False, stop=last)
        b.tt(A[:, 0:128], e[:, 0:128], P[:, 0:128], ALU.mult, ["e0", "P0"], ["A0"])
        for pr in range(8):
            b.mm(O[:, pr * 16:(pr + 1) * 16], vbp[:, pr * 128:(pr + 1) * 128], A[:, pr * 16:(pr + 1) * 16],
                 [kk("vbp"), "A0"], ["O"], start=False, stop=(last and pr == 7))

    prep(0)
    for i in range(nblk):
        if i + 1 < nblk:
            prep(i + 1)
        unit(i)
    O3 = O.rearrange("p (c x) -> p c x", c=8)
    b.tt(ogT1[0:64, :, 0:8], O3[0:64, :, 0:8], sgT[0:64, :, 0:8], ALU.mult, ["O", "sgT"], ["ogT1"])
    b.tt(ogT1[64:128, :, 0:8], O3[64:128, :, 8:16], sgT[64:128, :, 0:8], ALU.mult, ["O", "sgT"], ["ogT1"])
    sb_outproj(b, L1W, 8, [y_dst], x1own)


def sb_group3(b, L1W, cst, sbc, nkb, KT_d, V_d, qT, sgT, TQ, y_dst_tiles, x1own):
    t = b.t
    _, Wo1, qng, kng, biasb = L1W
    maskt, tri, omt = sbc
    nmask = maskt.shape[1]
    KTs, Vs, ogT1 = t["KTs"], t["Vs"], t["ogT1"]
    PA, PB, PC, PD = t["PA"], t["PB"], t["PC"], t["PD"]
    Zs = [[PA[:, 0:512], PA[:, 512:1024]], [PC[:, 0:512], PC[:, 512:1024]]]
    ACCs = [PB[:, 0:512], PD[:, 0:512]]
    Os = [PB[:, 512:1024], PD[:, 512:1024]]
    one = t["one"]
    e = [[t["pe00"], t["pe01"]], [t["pe10"], t["pe11"]]]
    Lt = [[t["pL00"], t["pL01"]], [t["pL10"], t["pL11"]]]
    P = [t["pP0"], t["pP1"]]; A = [t["pA0"], t["pA1"]]
    for pr in range(8):
        b.dma(KTs[:, 0:nkb * 128], KT_d[pr, :, 0:nkb * 128], ["KT_d"], ["KTs"])
        b.dma(Vs[:, 0:nkb, :], V_d[0:nkb * 128, pr * 128:(pr + 1) * 128].rearrange("(k p) c -> p k c", p=128),
              ["V_d"], ["Vs"], q="pool")
        kbs = list(range(nkb - 1, -1, -1))

        def front(i):
            kb = kbs[i]; par = i % 2
            mrel = kb - (nkb - nmask)
            m_ap = maskt[:, mrel, 0:TQ] if mrel >= 0 else None
            KTb = KTs[:, kb * 128:(kb + 1) * 128]
            for l in range(2):
                p0 = 64 * l
                b.mm(Zs[l][par][:, 0:TQ], KTb[p0:p0 + 64, :], qT[p0:p0 + 64, pr, 0:TQ], ["KTs", "qT"], ["Z%d%d" % (l, par)])
            for l in range(2):
                h = 2 * pr + l
                b.act(e[l][par][:, 0:TQ], Zs[l][par][:, 0:TQ], AF.Exp, ["Z%d%d" % (l, par), "biasb"], ["e%d%d" % (l, par)],
                      bias=biasb[:, h:h + 1])
            if m_ap is not None:
                for l in range(2):
                    b.tt(e[l][par][:, 0:TQ], e[l][par][:, 0:TQ], m_ap, ALU.mult, ["e%d%d" % (l, par), "mask"],
                         ["e%d%d" % (l, par)], eng="dve" if l == 0 else "pool")
            for l in range(2):
                b.act(Lt[l][par][:, 0:TQ], e[l][par][:, 0:TQ], AF.Ln, ["e%d%d" % (l, par), "one"], ["L%d%d" % (l, par)],
                      bias=one[:, 0:1])

        def back_a(i):
            kb = kbs[i]; par = i % 2
            first = (i == 0)
            for l in range(2):
                b.mm(ACCs[l][:, 0:TQ], tri[:, :], Lt[l][par][:, 0:TQ], ["L%d%d" % (l, par), "tri"], ["ACC%d" % l],
                     start=first, stop=False)
            for l in range(2):
                b.act(P[l][:, 0:TQ], ACCs[l][:, 0:TQ], AF.Exp, ["ACC%d" % l], ["P%d" % l], scale=-1.0)
            for l in range(2):
                b.tt(A[l][:, 0:TQ], e[l][par][:, 0:TQ], P[l][:, 0:TQ], ALU.mult, ["e%d%d" % (l, par), "P%d" % l], ["A%d" % l])

        def back_b(i):
            kb = kbs[i]; par = i % 2
            first, last = (i == 0), (kb == 0)
            Vb = Vs[:, kb, :]
            for l in range(2):
                b.mm(ACCs[l][:, 0:TQ], omt[:, :], Lt[l][par][:, 0:TQ], ["L%d%d" % (l, par), "omt"], ["ACC%d" % l],
                     start=False, stop=last)
            for l in range(2):
                b.mm(Os[l][:, 0:TQ], Vb, A[l][:, 0:TQ], ["Vs", "A%d" % l], ["O%d" % l], start=first, stop=last)

        n = len(kbs)
        front(0)
        for i in range(n):
            back_a(i)
            if i + 1 < n:
                front(i + 1)
            back_b(i)
        for l in range(2):
            p0 = 64 * l
            b.tt(ogT1[p0:p0 + 64, pr, 0:TQ], Os[l][p0:p0 + 64, 0:TQ], sgT[p0:p0 + 64, pr, 0:TQ], ALU.mult,
                 ["O%d" % l, "sgT"], ["ogT1"])
    b.p.barrier()
    sb_outproj(b, L1W, TQ, y_dst_tiles, x1own)


def _run(inp, cfg, ncores=8):
    NTOK, NS, NSAMP, NPG, NPHYS = cfg["NTOK"], cfg["NS"], cfg["NSAMP"], cfg["NPG"], cfg["NPHYS"]
    NOWN = NTOK // NS
    nc = bass.Bass("TRN2", target_bir_lowering=False)
    b = build_program(nc, cfg)
    c = consts_np()
    f32 = np.float32
    cs = np.zeros((128, 16, 8), f32)
    for s in range(8):
        cs[s, :, :] = (s < np.arange(8))[None, :]
    cs = cs.reshape(128, 128)
    import ml_dtypes
    ck = np.ascontiguousarray(inp["cache_k"][0].reshape(NPHYS * 128, 1024))
    cv = np.ascontiguousarray(inp["cache_v"][0].reshape(NPHYS * 128, 1024))
    in_maps = []
    for core in range(ncores):
        bb, j = core // NS, core % NS
        c2 = consts_sb_np(j, NS)
        own_tiles = NS * np.arange(NOWN // 128) + j
        own_rows = (own_tiles[None, :] * 128 + np.arange(128)[:, None]).astype(np.int32)
        m = {
            "xb": np.ascontiguousarray(inp["x_prompt"][bb]), "own_rows": own_rows,
            "xs": np.ascontiguousarray(inp["x_sample"][core * NSAMP:(core + 1) * NSAMP].reshape(NSAMP * 8, 1024)),
            "state": np.ascontiguousarray(inp["state_gla"][0, core * NSAMP:(core + 1) * NSAMP]),
            "ck": ck, "cv": cv,
            "pt": np.ascontiguousarray(inp["page_table"][core * NSAMP:(core + 1) * NSAMP].reshape(-1)).astype(np.int32),
            "gain": inp["norm_gain"], "win_a": inp["w_in_a"][0], "wup": inp["w_alpha_up"][0], "ba": inp["b_alpha"][0],
            "onorm": inp["onorm_a"][0], "wout_a": inp["w_out_a"][0], "win_b": inp["w_in_b"][0], "qn": inp["qnorm_b"][0],
            "kn": inp["knorm_b"][0], "sbias": inp["sb_bias"][0], "wout_b": inp["w_out_b"][0],
            "c_ib": c["ident_bf"], "c_if": c["ident_f"], "c_u": c["u_f"], "c_u4": c["u4_f"],
            "c_mask": c2["mask"], "c_tri": c2["tri"], "c_omt": c2["omt"],
            "c_masknew": cs.astype(ml_dtypes.bfloat16), "c_iota": np.arange(128, dtype=f32).reshape(128, 1),
        }
        in_maps.append({k: np.ascontiguousarray(v) for k, v in m.items()})
    res = run_bass_kernel_spmd(nc, in_maps, core_ids=list(range(ncores))).results
    NB = ncores // NS
    DB = ncores * NSAMP
    y_p = np.zeros((NB, NTOK, 1024), f32); y_s = np.zeros((DB, 8, 1024), f32)
    sp = np.zeros((1, NB, 4, 128, 256), f32); ss = np.zeros((1, DB, 4, 128, 256), f32)
    kp = np.zeros((1, NB, NTOK, 16, 64), f32); vp = np.zeros_like(kp)
    ks = np.zeros((1, DB, 8, 16, 64), f32); vs = np.zeros_like(ks)
    for core in range(ncores):
        bb, j = core // NS, core % NS
        r = res[core]
        yo = r["y_own"].reshape(NOWN // 128, 128, 1024)
        for i in range(NOWN // 128):
            t = NS * i + j
            y_p[bb, t * 128:(t + 1) * 128] = yo[i]
        y_s[core * NSAMP:(core + 1) * NSAMP] = r["ys"].reshape(NSAMP, 8, 1024)
        ss[0, core * NSAMP:(core + 1) * NSAMP] = r["st_s"]
        ks[0, core * NSAMP:(core + 1) * NSAMP] = r["k_s"].reshape(NSAMP, 8, 16, 64)
        vs[0, core * NSAMP:(core + 1) * NSAMP] = r["v_s"].reshape(NSAMP, 8, 16, 64)
        if j == 0:
            sp[0, bb] = r["st_p"]
            kp[0, bb] = r["k_all"].reshape(NTOK, 16, 64)
            vp[0, bb] = r["v_all"].reshape(NTOK, 16, 64)
    return (y_p, y_s, sp, ss, kp, vp, ks, vs)


CFG = dict(NTOK=8192, NS=4, NSAMP=16, NPG=16, NPHYS=2560)


def kernel(**inputs):
    inp = {k: np.asarray(v) for k, v in inputs.items()}
    return _run(inp, CFG, 8)
```

```python
import numpy as np
import ml_dtypes
import concourse.bass as bass
import concourse.mybir as mybir
from concourse.bass_utils import run_bass_kernel_spmd


ENGS = ("pe", "act", "dve", "pool", "sp")


class Prog:
    def __init__(self, nc, same_engine_sync=True):
        self.nc = nc
        self.ops = {e: [] for e in ENGS}
        self.count = {e: 0 for e in ENGS}
        self.esem = {}
        self.dsem = {}
        self.last_w = {}
        self.reads = {}
        self.known = {e: {} for e in ENGS}
        self.sems = {}
        self.same_engine_sync = same_engine_sync
        self._ctx = []
        self.final_tokens = []
        self.pending = {e: {} for e in ENGS}

    def _new_sem(self, name):
        cm = self.nc.semaphore(name)
        h = cm.__enter__()
        self._ctx.append(cm)
        sid = len(self.sems)
        self.sems[sid] = h
        return sid

    def _eng_sem(self, e):
        if e not in self.esem:
            self.esem[e] = self._new_sem("es_" + e)
        return self.esem[e]

    def _collect(self, e, reads, writes, is_dma):
        need = dict(self.pending[e])
        self.pending[e] = {}
        def add(tok):
            s, v = tok
            if need.get(s, 0) < v:
                need[s] = v
        for k in reads:
            for t in self.last_w.get(k, ()):
                add(t)
        for k in writes:
            for t in self.last_w.get(k, ()):
                add(t)
            for t in self.reads.get(k, ()):
                add(t)
        waits = []
        own = self.esem.get(e)
        for s, v in need.items():
            if self.known[e].get(s, 0) >= v:
                continue
            if (not is_dma) and s == own:
                if e == "pe" or not self.same_engine_sync:
                    continue
            waits.append((s, v))
            self.known[e][s] = v
        return waits

    def op(self, e, fn, reads=(), writes=()):
        waits = self._collect(e, reads, writes, False)
        s = self._eng_sem(e)
        self.count[e] += 1
        tok = (s, self.count[e])
        self.ops[e].append((waits, fn, tok, 1))
        for k in reads:
            self.reads.setdefault(k, []).append(tok)
        for k in writes:
            self.last_w[k] = [tok]
            self.reads[k] = []
        return tok

    def dma(self, e, fn, reads=(), writes=(), final=False):
        assert writes
        waits = self._collect(e, reads, writes, True)
        wk = writes[0]
        if wk not in self.dsem:
            self.dsem[wk] = [self._new_sem("ds%d" % len(self.dsem)), 0]
        ent = self.dsem[wk]
        ent[1] += 16
        tok = (ent[0], ent[1])
        self.ops[e].append((waits, fn, tok, 16))
        for k in reads:
            self.reads.setdefault(k, []).append(tok)
        for k in writes:
            self.last_w[k] = [tok]
            self.reads[k] = []
        if final:
            self.final_tokens.append(tok)
        return tok

    def barrier(self):
        toks = [(self.esem[e], self.count[e]) for e in self.esem if self.count[e] > 0]
        toks += [(s, c) for (s, c) in self.dsem.values()]
        for e in ENGS:
            for s, v in toks:
                if self.pending[e].get(s, 0) < v:
                    self.pending[e][s] = v

    def emit(self):
        nc = self.nc
        with nc.Block() as block:
            def run(e, eng):
                for waits, fn, tok, inc in self.ops[e]:
                    for s, v in waits:
                        eng.wait_ge(self.sems[s], v)
                    fn(eng).then_inc(self.sems[tok[0]], inc)
                if e == "sp":
                    fin = {}
                    for s, v in self.final_tokens:
                        fin[s] = max(fin.get(s, 0), v)
                    for s, v in fin.items():
                        eng.wait_ge(self.sems[s], v)

            @block.tensor
            def _(eng):
                run("pe", eng)

            @block.scalar
            def _(eng):
                run("act", eng)

            @block.vector
            def _(eng):
                run("dve", eng)

            @block.gpsimd
            def _(eng):
                run("pool", eng)

            @block.sync
            def _(eng):
                run("sp", eng)

    def close(self):
        for cm in reversed(self._ctx):
            cm.__exit__(None, None, None)
        self._ctx = []


F32 = mybir.dt.float32
BF16 = mybir.dt.bfloat16
I32 = mybir.dt.int32
AF = mybir.ActivationFunctionType
ALU = mybir.AluOpType
AX = mybir.AxisListType

D = 1024
EPS = 1e-6
GH, GDK, GDV = 4, 128, 256
GIN = 3088
SH, SDH = 16, 64


class Bld:
    def __init__(self, nc):
        self.nc = nc
        self.p = Prog(nc)
        self._cms = []
        self.t = {}

    def sb(self, name, shape, dt):
        cm = self.nc.sbuf_tensor(name, list(shape), dt)
        h = cm.__enter__()
        self._cms.append(cm)
        self.t[name] = h
        return h

    def ps(self, name, shape, dt=F32):
        cm = self.nc.psum_tensor(name, list(shape), dt)
        h = cm.__enter__()
        self._cms.append(cm)
        self.t[name] = h
        return h

    def release_to(self, mark):
        while len(self._cms) > mark:
            self._cms.pop().__exit__(None, None, None)

    def finish(self):
        self.p.emit()
        self.p.close()
        for cm in reversed(self._cms):
            cm.__exit__(None, None, None)

    def mm(self, out, lhsT, rhs, r, w, start=True, stop=True):
        return self.p.op("pe", lambda e: e.matmul(out, lhsT, rhs, start=start, stop=stop), r, w)

    def tr(self, out, in_, ident, r, w):
        return self.p.op("pe", lambda e: e.transpose(out, in_, ident), r, w)

    def act(self, out, in_, func, r, w, bias=None, scale=None, accum=None):
        kw = {}
        if bias is not None:
            kw["bias"] = bias
        if scale is not None:
            kw["scale"] = scale
        if accum is not None:
            kw["accum_out"] = accum
        return self.p.op("act", lambda e: e.activation(out, in_, func, **kw), r, w)

    def tt(self, out, in0, in1, op, r, w, eng="dve"):
        return self.p.op(eng, lambda e: e.tensor_tensor(out, in0, in1, op), r, w)

    def ts(self, out, in0, s1, s2, op0, op1, r, w, eng="dve"):
        if op1 is None:
            return self.p.op(eng, lambda e: e.tensor_scalar(out, in0, s1, None, op0), r, w)
        return self.p.op(eng, lambda e: e.tensor_scalar(out, in0, s1, s2, op0, op1), r, w)

    def stt(self, out, in0, scalar, in1, op0, op1, r, w):
        return self.p.op("dve", lambda e: e.scalar_tensor_tensor(out, in0, scalar, in1, op0, op1), r, w)

    def cp(self, out, in_, r, w, eng="dve"):
        if eng == "act":
            return self.p.op("act", lambda e: e.copy(out, in_), r, w)
        return self.p.op(eng, lambda e: e.tensor_copy(out, in_), r, w)

    def red(self, out, in_, r, w):
        return self.p.op("dve", lambda e: e.tensor_reduce(out, in_, AX.X, ALU.add), r, w)

    def mset(self, ap, val, w, eng="pool"):
        return self.p.op(eng, lambda e: e.memset(ap, val), (), w)

    def dma(self, out, in_, r, w, q="sp", final=False, slow=False):
        if slow:
            return self.p.dma(q, lambda e: e.dma_start(out=out, in_=in_, allow_slow_non_contiguous=True), r, w, final)
        return self.p.dma(q, lambda e: e.dma_start(out=out, in_=in_), r, w, final)

    def rstd(self, ssq, n, key):
        self.ts(ssq, ssq, 1.0 / n, EPS, ALU.mult, ALU.add, [key], [key])
        self.act(ssq, ssq, AF.Ln, [key], [key])
        self.act(ssq, ssq, AF.Exp, [key], [key], scale=-0.5)


def consts_np():
    c = {}
    c["ident_bf"] = np.eye(128, dtype=np.float32).astype(ml_dtypes.bfloat16)
    c["ident_f"] = np.eye(128, dtype=np.float32)
    u = np.triu(np.ones((128, 128), np.float32))
    c["u_f"] = u
    c["u4_f"] = np.tile(u, (1, 4))
    return c


def load_weights_l0(b, win, wup, ba, gain, wout, onorm):
    nc = b.nc
    Wa = b.sb("Wa", [128, 8, GIN], BF16)
    Wo = b.sb("Wo", [128, 8, D], BF16)
    stg = [b.t["wstg0"], b.t["wstg1"]]
    gcol = b.sb("gcol0", [128, 8], F32)
    b.dma(gcol[:], gain.rearrange("(c p) -> p c", p=128), [], ["gcol0"], slow=True)
    i = 0
    for c in range(8):
        for c0 in range(0, GIN, 1024):
            n = min(1024, GIN - c0)
            s = stg[i % 2]; k = "wstg%d" % (i % 2)
            b.dma(s[:, 0:n], win[c * 128:(c + 1) * 128, c0:c0 + n], [], [k], q="sp" if i % 2 == 0 else "pool")
            b.ts(Wa[:, c, c0:c0 + n], s[:, 0:n], gcol[:, c:c + 1], None, ALU.mult, None, [k, "gcol0"], ["Wa"],
                 eng="dve" if i % 2 == 0 else "pool")
            i += 1
    for c in range(8):
        s = stg[i % 2]; k = "wstg%d" % (i % 2)
        b.dma(s[:, 0:D], wout[c * 128:(c + 1) * 128, :], [], [k], q="sp" if i % 2 == 0 else "pool")
        b.cp(Wo[:, c, :], s[:, 0:D], [k], ["Wo"], eng="dve" if i % 2 == 0 else "pool")
        i += 1
    wupf = b.sb("wupf", [17, 512], F32)
    b.dma(wupf[0:16, :], wup, [], ["wupf"])
    b.dma(wupf[16:17, :], ba.rearrange("(o n) -> o n", o=1), [], ["wupf"])
    wupb = b.sb("wupb", [17, 512], BF16)
    b.cp(wupb[:], wupf[:], ["wupf"], ["wupb"])
    ong = b.sb("ong", [128, GH, GDV], F32)
    for h in range(GH):
        b.dma(ong[:, h, :], onorm.rearrange("(o n) -> o n", o=1).partition_broadcast(128), [], ["ong"], slow=True)
    return Wa, Wo, wupb, ong


def gla_tile(b, T, x_src, x1_dst, S, Sbf, W, cst, tag=""):
    Wa, Wo, wupb, ong = W
    t = b.t
    nc = b.nc
    ident_bf, ident_f, u_f, u4_f = cst
    xt, xn, hT = t["xt"], t["xn"], t["hT"]
    PA, PB, PC, PD = t["PA"], t["PB"], t["PC"], t["PD"]
    junk = t["junk"]
    b.dma(xt[0:T, :], x_src, [], ["xt"])
    b.act(junk[0:T, :], xt[0:T, :], AF.Square, ["xt"], ["junk", "ssq"], accum=t["ssq"][0:T, 0:1])
    b.rstd(t["ssq"][0:T, 0:1], D, "ssq")
    b.ts(xn[0:T, :], xt[0:T, :], t["ssq"][0:T, 0:1], None, ALU.mult, None, ["xt", "ssq"], ["xn"])
    PAb = PA[:].bitcast(BF16)
    for c in range(8):
        b.tr(PAb[:, c * 128:c * 128 + T], xn[0:T, c * 128:(c + 1) * 128], ident_bf[0:T, 0:T], ["xn", "ident_bf"], ["PA"])
    b.cp(hT[:, :, 0:T], PAb[:, 0:1024].rearrange("p (c t) -> p c t", c=8)[:, :, 0:T], ["PA"], ["hT"], eng="act")
    for j in range(8):
        for c in range(8):
            b.mm(PB[:, j * 128:j * 128 + T], Wa[:, c, j * 128:(j + 1) * 128], hT[:, c, 0:T],
                 ["Wa", "hT"], ["PB"], start=(c == 0), stop=(c == 7))
    for c in range(8):
        b.mm(PA[0:16, 512:512 + T], Wa[:, c, 3072:3088], hT[:, c, 0:T], ["Wa", "hT"], ["PA"],
             start=(c == 0), stop=(c == 7))
    acT = t["acT"]
    b.cp(acT[0:16, 0:T], PA[0:16, 512:512 + T], ["PA"], ["acT"])
    for n in range(2):
        for c in range(8):
            b.mm(PC[0:T, n * 512:(n + 1) * 512], hT[:, c, 0:T], Wa[:, c, 1024 + n * 512:1024 + (n + 1) * 512],
                 ["Wa", "hT"], ["PC"], start=(c == 0), stop=(c == 7))
    vbf = t["vbf"]
    b.cp(vbf[0:T, :], PC[0:T, :], ["PC"], ["vbf"], eng="act")
    for n in range(2):
        for c in range(8):
            b.mm(PD[0:T, n * 512:(n + 1) * 512], hT[:, c, 0:T], Wa[:, c, 2048 + n * 512:2048 + (n + 1) * 512],
                 ["Wa", "hT"], ["PD"], start=(c == 0), stop=(c == 7))
    sg = t["sg"]
    b.act(sg[0:T, :], PD[0:T, :], AF.Exp, ["PD"], ["sg"], scale=-1.0)
    b.ts(sg[0:T, :], sg[0:T, :], 1.0, None, ALU.add, None, ["sg"], ["sg"])
    b.p.op("dve", lambda e: e.reciprocal(sg[0:T, :], sg[0:T, :]), ["sg"], ["sg"])
    b.tt(sg[0:T, :], sg[0:T, :], PD[0:T, :], ALU.mult, ["sg", "PD"], ["sg"])
    b.mm(PA[0:T, 0:512], acT[0:17, 0:T], wupb[0:17, :], ["acT", "wupb"], ["PA"])
    spt = t["spt"]
    b.act(spt[0:T, :], PA[0:T, 0:512], AF.Exp, ["PA"], ["spt"], scale=-1.0)
    b.act(spt[0:T, :], spt[0:T, :], AF.Ln, ["spt", "one"], ["spt"], bias=t["one"][0:T, 0:1])
    for h in range(GH):
        b.mm(PA[:, h * 128:h * 128 + T], spt[0:T, h * 128:(h + 1) * 128], u_f[0:T, 0:T], ["spt", "u_f"], ["PA"])
    E1, E2, E3, nbl, dec = t["E1"], t["E2"], t["E3"], t["nbl"], t["dec"]
    PA4 = PA[:, 0:512].rearrange("p (h t) -> p h t", h=GH)
    b.act(E1[:].rearrange("p (h t) -> p h t", h=GH)[:, :, 0:T], PA4[:, :, 0:T], AF.Exp, ["PA"], ["E1"], scale=-1.0 / 16)
    b.act(E2[:].rearrange("p (h t) -> p h t", h=GH)[:, :, 0:T], PA4[:, :, 0:T], AF.Exp, ["PA"], ["E2"], scale=1.0 / 16)
    b.ts(nbl[:, :], PA4[:, :, T - 1], -1.0 / 16, None, ALU.mult, None, ["PA"], ["nbl"])
    for h in range(GH):
        b.act(E3[:, h * 128:h * 128 + T], PA[:, h * 128:h * 128 + T], AF.Exp, ["PA", "nbl"], ["E3"],
              scale=1.0 / 16, bias=nbl[:, h:h + 1])
    b.act(dec[:, :], nbl[:, :], AF.Exp, ["nbl"], ["dec"])
    qd, kd, kb = t["qd"], t["kd"], t["kb"]
    def v4(ap):
        return ap.rearrange("p (h t) -> p h t", h=GH)[:, :, 0:T]
    b.stt(v4(qd[:]), v4(PB[:, 0:512]), float(GDK) ** -0.5, v4(E1[:]), ALU.mult, ALU.mult, ["PB", "E1"], ["qd"])
    b.tt(v4(kd[:]), v4(PB[:, 512:1024]), v4(E2[:]), ALU.mult, ["PB", "E2"], ["kd"])
    b.tt(v4(kb[:]), v4(PB[:, 512:1024]), v4(E3[:]), ALU.mult, ["PB", "E3"], ["kb"])
    for h in range(GH):
        b.mm(PB[0:T, h * 128:h * 128 + T], kd[:, h * 128:h * 128 + T], qd[:, h * 128:h * 128 + T],
             ["kd", "qd"], ["PB"])
    attm = t["attm"]
    b.tt(v4(attm[0:T, :]), v4(PB[0:T, 0:512]), v4(u4_f[0:T, :]), ALU.mult, ["PB", "u4_f"], ["attm"])
    for h in range(GH):
        b.tr(PAb[0:T, h * 128:(h + 1) * 128], kb[:, h * 128:h * 128 + T], ident_bf[:, :], ["kb", "ident_bf"], ["PA"])
    kbt = t["kbt"]
    b.cp(kbt[0:T, :], PAb[0:T, 0:512], ["PA"], ["kbt"])
    for h in range(GH):
        b.mm(PC[0:T, h * 256:(h + 1) * 256], attm[0:T, h * 128:h * 128 + T], vbf[0:T, h * 256:(h + 1) * 256],
             ["attm", "vbf"], ["PC"], start=True, stop=False)
        b.mm(PC[0:T, h * 256:(h + 1) * 256], qd[:, h * 128:h * 128 + T], Sbf[:, h * 256:(h + 1) * 256],
             ["qd", "Sbf"], ["PC"], start=False, stop=True)
    for h in range(GH):
        b.mm(PD[:, h * 256:(h + 1) * 256], kbt[0:T, h * 128:(h + 1) * 128], vbf[0:T, h * 256:(h + 1) * 256],
             ["kbt", "vbf"], ["PD"])
    for h in range(GH):
        b.stt(S[:, h * 256:(h + 1) * 256], S[:, h * 256:(h + 1) * 256], dec[:, h:h + 1],
              PD[:, h * 256:(h + 1) * 256], ALU.mult, ALU.add, ["S", "dec", "PD"], ["S"])
    b.cp(Sbf[:], S[:], ["S"], ["Sbf"], eng="act")
    sso = t["sso"]
    for h in range(GH):
        b.act(junk[0:T, 0:256], PC[0:T, h * 256:(h + 1) * 256], AF.Square, ["PC"], ["junk", "sso"],
              accum=sso[0:T, h:h + 1])
    b.rstd(sso[0:T, 0:GH], GDV, "sso")
    on = t["on"]
    for h in range(GH):
        b.stt(on[0:T, h * 256:(h + 1) * 256], PC[0:T, h * 256:(h + 1) * 256], sso[0:T, h:h + 1],
              ong[0:T, h, :], ALU.mult, ALU.mult, ["PC", "sso", "ong"], ["on"])
    og = t["og"]
    b.tt(og[0:T, :], on[0:T, :], sg[0:T, :], ALU.mult, ["on", "sg"], ["og"])
    for c in range(8):
        b.tr(PAb[:, c * 128:c * 128 + T], og[0:T, c * 128:(c + 1) * 128], ident_bf[0:T, 0:T], ["og", "ident_bf"], ["PA"])
    ogT = t["ogT"]
    b.cp(ogT[:, :, 0:T], PAb[:, 0:1024].rearrange("p (c t) -> p c t", c=8)[:, :, 0:T], ["PA"], ["ogT"], eng="act")
    for n in range(2):
        for c in range(8):
            b.mm(PD[0:T, n * 512:(n + 1) * 512], ogT[:, c, 0:T], Wo[:, c, n * 512:(n + 1) * 512],
                 ["ogT", "Wo"], ["PD"], start=(c == 0), stop=(c == 7))
    x1t = t["x1t"]
    b.tt(x1t[0:T, :], PD[0:T, :], xt[0:T, :], ALU.add, ["PD", "xt"], ["x1t"])
    if x1_dst is not None:
        b.dma(x1_dst, x1t[0:T, :], ["x1t"], ["x1dram" + tag], q="pool")
    return x1t


def alloc_common(b, gla=True):
    b.sb("xt", [128, D], F32)
    b.sb("xn", [128, D], BF16)
    b.sb("hT", [128, 8, 128], BF16)
    b.sb("junk", [128, D], BF16)
    b.sb("ssq", [128, 1], F32)
    b.sb("sg", [128, D], F32)
    b.sb("one", [128, 1], F32)
    b.sb("on", [128, D], F32)
    if gla:
        b.sb("acT", [17, 128], BF16)
        b.sb("vbf", [128, D], BF16)
        b.sb("spt", [128, 512], F32)
        b.sb("E1", [128, 512], F32)
        b.sb("E2", [128, 512], F32)
        b.sb("E3", [128, 512], F32)
        b.sb("nbl", [128, 4], F32)
        b.sb("dec", [128, 4], F32)
        b.sb("qd", [128, 512], BF16)
        b.sb("kd", [128, 512], BF16)
        b.sb("kb", [128, 512], BF16)
        b.sb("attm", [128, 512], BF16)
        b.sb("kbt", [128, 512], BF16)
        b.sb("sso", [128, 4], F32)
        b.sb("og", [128, D], BF16)
        b.sb("ogT", [128, 8, 128], BF16)
    b.sb("x1t", [128, D], F32)
    b.ps("PA", [128, 1024], F32)
    b.ps("PB", [128, 1024], F32)
    b.ps("PC", [128, 1024], F32)
    b.ps("PD", [128, 1024], F32)
    b.mset(b.t["one"][:], 1.0, ["one"])
    if gla:
        b.mset(b.t["acT"][:], 1.0, ["acT"])


def load_consts(b, d_ident_bf, d_ident_f, d_u, d_u4):
    ib = b.sb("ident_bf", [128, 128], BF16)
    i_f = b.sb("ident_f", [128, 128], F32)
    u = b.sb("u_f", [128, 128], F32)
    u4 = b.sb("u4_f", [128, 512], F32)
    b.dma(ib[:], d_ident_bf, [], ["ident_bf"])
    b.dma(i_f[:], d_ident_f, [], ["ident_f"])
    b.dma(u[:], d_u, [], ["u_f"])
    b.dma(u4[:], d_u4, [], ["u4_f"])
    return ib, i_f, u, u4


def consts_sb_np(j, ns):
    nk = 4 * ns
    kpos = (np.arange(nk)[:, None] * 128 + np.arange(128)[None, :])
    qpos = ((ns * np.arange(4)[:, None] + j) * 128 + np.arange(128)[None, :]).reshape(-1)
    m = (kpos[:, :, None] < qpos[None, None, :]).astype(np.float32)
    c = {}
    c["mask"] = np.ascontiguousarray(m.transpose(1, 0, 2)).astype(ml_dtypes.bfloat16)
    tl = np.tril(np.ones((128, 128), np.float32))
    c["tri"] = tl.astype(ml_dtypes.bfloat16)
    c["omt"] = (1.0 - tl).astype(ml_dtypes.bfloat16)
    return c


def load_weights_l1(b, winb, gain1, woutb, qn, kn, sbias, which="all"):
    if which == "kv":
        segs = [(1024, 2048)]; cb = {"k": 0, "v": 1024}
    elif which == "qg":
        segs = [(0, 1024), (3072, 1024)]; cb = {"q": 0, "g": 1024}
    else:
        segs = [(0, 2048), (2048, 2048)]; cb = {"q": 0, "k": 1024, "v": 2048, "g": 3072}
    ncol = sum(n for _, n in segs)
    W1 = b.sb("W1" + which, [128, 8, ncol], BF16)
    wk = "W1" + which
    gcol = b.sb("gcol1" + which, [128, 8], F32)
    b.dma(gcol[:], gain1.rearrange("(c p) -> p c", p=128), [], ["gcol1" + which], slow=True)
    stg = [b.t["wstg0"], b.t["wstg1"]]
    i = 0
    for c in range(8):
        lo = 0
        for (c0, n) in segs:
            for s0 in range(0, n, 1024):
                s = stg[i % 2]; k = "wstg%d" % (i % 2)
                b.dma(s[:, 0:1024], winb[c * 128:(c + 1) * 128, c0 + s0:c0 + s0 + 1024], [], [k],
                      q="sp" if i % 2 == 0 else "pool")
                b.ts(W1[:, c, lo + s0:lo + s0 + 1024], s[:, 0:1024], gcol[:, c:c + 1], None, ALU.mult, None,
                     [k, "gcol1" + which], [wk], eng="dve" if i % 2 == 0 else "pool")
                i += 1
            lo += n
    Wo1 = None
    if which != "kv":
        Wo1 = b.sb("Wo1", [128, 8, D], BF16)
        for c in range(8):
            s = stg[i % 2]; k = "wstg%d" % (i % 2)
            b.dma(s[:, 0:D], woutb[c * 128:(c + 1) * 128, :], [], [k], q="sp" if i % 2 == 0 else "pool")
            b.cp(Wo1[:, c, :], s[:, 0:D], [k], ["Wo1"], eng="dve" if i % 2 == 0 else "pool")
            i += 1
    qng = b.sb("qng" + which, [128, SH, SDH], F32)
    kng = b.sb("kng" + which, [128, SH, SDH], F32)
    for h in range(SH):
        b.dma(qng[:, h, :], qn.rearrange("(o n) -> o n", o=1).partition_broadcast(128), [], ["qng"], slow=True)
        b.dma(kng[:, h, :], kn.rearrange("(o n) -> o n", o=1).partition_broadcast(128), [], ["kng"], slow=True,
              q="pool")
    b.ts(qng[:], qng[:], float(SDH) ** -0.5, None, ALU.mult, None, ["qng"], ["qng"])
    biasb = b.sb("biasb" + which, [128, SH], F32)
    b.dma(biasb[:], sbias.rearrange("(o n) -> o n", o=1).partition_broadcast(128), [], ["biasb"], slow=True)
    return (W1, wk, cb), Wo1, qng, kng, biasb


def l1_norm_T(b, T, xsb, cst, xkey="x1t"):
    t = b.t
    ident_bf = cst[0]
    junk, ssq, xn, hT, PA = t["junk"], t["ssq"], t["xn"], t["hT"], t["PA"]
    b.act(junk[0:T, :], xsb, AF.Square, [xkey], ["junk", "ssq"], accum=ssq[0:T, 0:1])
    b.rstd(ssq[0:T, 0:1], D, "ssq")
    b.ts(xn[0:T, :], xsb, ssq[0:T, 0:1], None, ALU.mult, None, [xkey, "ssq"], ["xn"])
    PAb = PA[:].bitcast(BF16)
    for c in range(8):
        b.tr(PAb[:, c * 128:c * 128 + T], xn[0:T, c * 128:(c + 1) * 128], ident_bf[0:T, 0:T], ["xn", "ident_bf"], ["PA"])
    b.cp(hT[:, :, 0:T], PAb[:, 0:1024].rearrange("p (c t) -> p c t", c=8)[:, :, 0:T], ["PA"], ["hT"], eng="act")
    return hT


def proj_tok(b, T, PS, pskey, W1k, which):
    hT = b.t["hT"]
    W1, wkey, cb = W1k
    col0 = cb[which]
    for n in range(2):
        for c in range(8):
            b.mm(PS[0:T, n * 512:(n + 1) * 512], hT[:, c, 0:T], W1[:, c, col0 + n * 512:col0 + (n + 1) * 512],
                 [wkey, "hT"], [pskey], start=(c == 0), stop=(c == 7))


def headnorm(b, T, PS, pskey, gain_t, gkey, out_f, okey):
    t = b.t
    sq, ssh = t["on"], t["ssh"]
    b.act(sq[0:T, :], PS[0:T, :], AF.Square, [pskey], ["on"])
    b.red(ssh[0:T, :], sq[0:T, :].rearrange("p (h d) -> p h d", h=SH), ["on"], ["ssh"])
    b.rstd(ssh[0:T, :], SDH, "ssh")
    o3 = out_f[0:T, :].rearrange("p (h d) -> p h d", h=SH)
    b.tt(o3, PS[0:T, :].rearrange("p (h d) -> p h d", h=SH),
         ssh[0:T, :].unsqueeze(2).to_broadcast([T, SH, SDH]), ALU.mult, [pskey, "ssh"], [okey])
    b.tt(o3, o3, gain_t[0:T, :, :], ALU.mult, [okey, gkey], [okey])


def to_pairT(b, T, src_bf, skey, dst, dkey, col0, cst):
    PA = b.t["PA"]
    PAb = PA[:].bitcast(BF16)
    ident_bf = cst[0]
    for c in range(8):
        b.tr(PAb[:, c * 128:c * 128 + T], src_bf[0:T, c * 128:(c + 1) * 128], ident_bf[0:T, 0:T],
             [skey, "ident_bf"], ["PA"])
    b.cp(dst[:, :, col0:col0 + T], PAb[:, 0:1024].rearrange("p (c t) -> p c t", c=8)[:, :, 0:T], ["PA"], [dkey])


def l1_kv_tile(b, T, xsb, W1, L1W, cst, k_out, v_out, KT_d, V_d, tok0, tag, xkey="x1t", ktkey="KT_d", vkey="V_d"):
    t = b.t
    W1_, Wo1, qng, kng, biasb = L1W
    PC, PD = t["PC"], t["PD"]
    l1_norm_T(b, T, xsb, cst, xkey)
    proj_tok(b, T, PC, "PC", W1, "k")
    proj_tok(b, T, PD, "PD", W1, "v")
    kf, vf, kbf, vb2, ktt = t["kf"], t["vf"], t["kbf"], t["vb2"], t["ktt"]
    headnorm(b, T, PC, "PC", kng, "kng", kf, "kf")
    if k_out is not None:
        b.dma(k_out, kf[0:T, :], ["kf"], ["kout"], q="pool", final=True)
    b.cp(kbf[0:T, :], kf[0:T, :], ["kf"], ["kbf"], eng="act")
    to_pairT(b, T, kbf, "kbf", ktt, "ktt", 0, cst)
    b.dma(KT_d[:, :, tok0:tok0 + T].rearrange("c p t -> p c t"), ktt[:, :, 0:T], ["ktt"], [ktkey], slow=True)
    b.cp(vf[0:T, :], PD[0:T, :], ["PD"], ["vf"], eng="act")
    if v_out is not None:
        b.dma(v_out, vf[0:T, :], ["vf"], ["vout"], q="pool", final=True)
    b.cp(vb2[0:T, :], vf[0:T, :], ["vf"], ["vb2"])
    b.dma(V_d[tok0:tok0 + T, :], vb2[0:T, :], ["vb2"], [vkey])


def l1_qg_tile(b, T, xsb, W1, L1W, cst, qT, sgT, col0, xkey="x1own"):
    t = b.t
    W1_, Wo1, qng, kng, biasb = L1W
    PC, PD = t["PC"], t["PD"]
    l1_norm_T(b, T, xsb, cst, xkey)
    proj_tok(b, T, PC, "PC", W1, "q")
    proj_tok(b, T, PD, "PD", W1, "g")
    kf, kbf, sg, vb2 = t["kf"], t["kbf"], t["sg"], t["vb2"]
    headnorm(b, T, PC, "PC", qng, "qng", kf, "kf")
    b.cp(kbf[0:T, :], kf[0:T, :], ["kf"], ["kbf"], eng="pool")
    to_pairT(b, T, kbf, "kbf", qT, "qT", col0, cst)
    b.act(sg[0:T, :], PD[0:T, :], AF.Exp, ["PD"], ["sg"], scale=-1.0)
    b.ts(sg[0:T, :], sg[0:T, :], 1.0, None, ALU.add, None, ["sg"], ["sg"])
    b.p.op("dve", lambda e: e.reciprocal(sg[0:T, :], sg[0:T, :]), ["sg"], ["sg"])
    b.tt(vb2[0:T, :], sg[0:T, :], PD[0:T, :], ALU.mult, ["sg", "PD"], ["vb2"])
    to_pairT(b, T, vb2, "vb2", sgT, "sgT", col0, cst)


def sb_unit(b, hp, KTb, Vb, kcol, TQ, qT, mask_ap, first, last, Z, ACC, O, biasb, h, tri, omt):
    t = b.t
    e_t, L_t, P_t, A_t = t["e_t"], t["L_t"], t["P_t"], t["A_t"]
    p0 = 64 * hp
    b.mm(Z[:, 0:TQ], KTb[p0:p0 + 64, :], qT[p0:p0 + 64, 0:TQ], ["KTs", "qT"], ["Z"])
    b.act(e_t[:, 0:TQ], Z[:, 0:TQ], AF.Exp, ["Z", "biasb"], ["e_t"], bias=biasb[:, h:h + 1])
    if mask_ap is not None:
        b.tt(e_t[:, 0:TQ], e_t[:, 0:TQ], mask_ap, ALU.mult, ["e_t", "mask", "masks"], ["e_t"])
    b.act(L_t[:, 0:TQ], e_t[:, 0:TQ], AF.Ln, ["e_t", "one"], ["L_t"], bias=t["one"][:, 0:1])
    b.mm(ACC[:, 0:TQ], tri[:, :], L_t[:, 0:TQ], ["L_t", "tri"], ["ACC"], start=first, stop=False)
    b.act(P_t[:, 0:TQ], ACC[:, 0:TQ], AF.Exp, ["ACC"], ["P_t"], scale=-1.0)
    b.mm(ACC[:, 0:TQ], omt[:, :], L_t[:, 0:TQ], ["L_t", "omt"], ["ACC"], start=False, stop=last)
    b.tt(A_t[:, 0:TQ], e_t[:, 0:TQ], P_t[:, 0:TQ], ALU.mult, ["e_t", "P_t"], ["A_t"])
    b.mm(O[:, 0:TQ], Vb, A_t[:, 0:TQ], ["Vs", "A_t"], ["O"], start=first, stop=last)


def alloc_l1(b, nkeys_max, TQ=512):
    b.sb("ssh", [128, SH], F32)
    b.sb("kf", [128, D], F32)
    b.sb("vf", [128, D], F32)
    b.sb("kbf", [128, D], BF16)
    b.sb("vb2", [128, D], BF16)
    b.sb("ktt", [128, 8, 128], BF16)
    b.sb("e_t", [128, TQ], F32)
    b.sb("L_t", [128, TQ], BF16)
    b.sb("P_t", [128, TQ], F32)
    b.sb("A_t", [128, TQ], BF16)
    b.sb("KTs", [128, nkeys_max], BF16)
    b.sb("Vs", [128, nkeys_max // 128, 128], BF16)
    b.sb("qT", [128, 8, TQ], BF16)
    b.sb("sgT", [128, 8, TQ], BF16)
    b.sb("ogT", [128, 8, TQ], BF16) if "ogT" not in b.t else None
    b.sb("ogT1", [128, 8, TQ], BF16)
    b.sb("x1own", [128, TQ // 128, D], F32)
    b.sb("yt", [128, D], F32)


def sb_group(b, L1W, cst, sbc, nkb, KT_d, V_d, qT, sgT, TQ, y_dst_tiles, x1own, ktkey="KT_d", vkey="V_d"):
    t = b.t
    W1_, Wo1, qng, kng, biasb = L1W
    maskt, tri, omt = sbc
    nmask = maskt.shape[1]
    KTs, Vs, ogT1 = t["KTs"], t["Vs"], t["ogT1"]
    PA, PB = t["PA"], t["PB"]
    Z, ACC, O = PB[:, 0:512], PB[:, 512:1024], PA[:, 512:1024]
    for pr in range(8):
        b.dma(KTs[:, 0:nkb * 128], KT_d[pr, :, 0:nkb * 128], [ktkey], ["KTs"])
        b.dma(Vs[:, 0:nkb, :], V_d[0:nkb * 128, pr * 128:(pr + 1) * 128].rearrange("(k p) c -> p k c", p=128),
              [vkey], ["Vs"], q="pool")
        for hp in range(2):
            h = 2 * pr + hp
            for i, kb in enumerate(range(nkb - 1, -1, -1)):
                mrel = kb - (nkb - nmask)
                m_ap = maskt[:, mrel, 0:TQ] if mrel >= 0 else None
                sb_unit(b, hp, KTs[:, kb * 128:(kb + 1) * 128], Vs[:, kb, :], kb, TQ, qT[:, pr, :], m_ap,
                        i == 0, kb == 0, Z, ACC, O, biasb, h, tri, omt)
            p0 = 64 * hp
            b.tt(ogT1[p0:p0 + 64, pr, 0:TQ], O[p0:p0 + 64, 0:TQ], sgT[p0:p0 + 64, pr, 0:TQ], ALU.mult,
                 ["O", "sgT"], ["ogT1"])
    PD = t["PD"]
    yt = t["yt"]
    Tt = min(128, TQ)
    for ti in range(max(1, TQ // 128)):
        for n in range(2):
            for c in range(8):
                b.mm(PD[0:Tt, n * 512:(n + 1) * 512], ogT1[:, c, ti * 128:ti * 128 + Tt], Wo1[:, c, n * 512:(n + 1) * 512],
                     ["ogT1", "Wo1"], ["PD"], start=(c == 0), stop=(c == 7))
        b.tt(yt[0:Tt, :], PD[0:Tt, :], x1own[0:Tt, ti, :], ALU.add, ["PD", "x1own"], ["yt"])
        b.dma(y_dst_tiles[ti], yt[0:Tt, :], ["yt"], ["ydst"], q="pool", final=True)


def l1_kv_rows(b, T, xsb, W1, L1W, cst, k_out, v_out, tag, xkey):
    t = b.t
    _, Wo1, qng, kng, biasb = L1W
    PC, PD = t["PC"], t["PD"]
    l1_norm_T(b, T, xsb, cst, xkey)
    proj_tok(b, T, PC, "PC", W1, "k")
    proj_tok(b, T, PD, "PD", W1, "v")
    kf, vf = t["kf"], t["vf"]
    headnorm(b, T, PC, "PC", kng, "kng", kf, "kf")
    b.dma(k_out, kf[0:T, :], ["kf"], ["kout"], q="pool", final=True)
    b.cp(vf[0:T, :], PD[0:T, :], ["PD"], ["vf"], eng="act")
    b.dma(v_out, vf[0:T, :], ["vf"], ["vout"], q="pool", final=True)


def build_program(nc, cfg):
    NTOK, NS, NSAMP, NPG, NPHYS = cfg["NTOK"], cfg["NS"], cfg["NSAMP"], cfg["NPG"], cfg["NPHYS"]
    NOWN = NTOK // NS
    NG = NOWN // 512
    NT = NTOK // 128
    NKS = (NPG + 1) * 128
    def din(n, s, dt=F32): return nc.dram_tensor(n, list(s), dt, kind="ExternalInput").ap()
    def dout(n, s, dt=F32): return nc.dram_tensor(n, list(s), dt, kind="ExternalOutput").ap()
    xb = din("xb", [NTOK, D]); own_rows = din("own_rows", [128, NOWN // 128], I32)
    xs = din("xs", [NSAMP * 8, D]); state = din("state", [NSAMP, GH, GDK, GDV])
    ck = din("ck", [NPHYS * 128, D]); cv = din("cv", [NPHYS * 128, D]); pt = din("pt", [NSAMP * NPG], I32)
    gain = din("gain", [2, D]); win_a = din("win_a", [D, GIN]); wup = din("wup", [16, 512]); ba = din("ba", [512])
    onorm = din("onorm", [256]); wout_a = din("wout_a", [D, D]); win_b = din("win_b", [D, 4096])
    qn = din("qn", [64]); kn = din("kn", [64]); sbias = din("sbias", [16]); wout_b = din("wout_b", [D, D])
    c_ib = din("c_ib", [128, 128], BF16); c_if = din("c_if", [128, 128]); c_u = din("c_u", [128, 128]); c_u4 = din("c_u4", [128, 512])
    NM = 4 * NS
    c_mask = din("c_mask", [128, NM, 512], BF16); c_tri = din("c_tri", [128, 128], BF16); c_omt = din("c_omt", [128, 128], BF16)
    c_masknew = din("c_masknew", [128, 128], BF16); c_iota = din("c_iota", [128, 1])
    y_own = dout("y_own", [NOWN, D]); ys = dout("ys", [NSAMP * 8, D])
    st_p = dout("st_p", [GH, GDK, GDV]); st_s = dout("st_s", [NSAMP, GH, GDK, GDV])
    k_all = dout("k_all", [NTOK, D]); v_all = dout("v_all", [NTOK, D])
    k_s = dout("k_s", [NSAMP * 8, D]); v_s = dout("v_s", [NSAMP * 8, D])
    x1_d = nc.dram_tensor("x1_d", [NTOK, D], F32).ap()
    xs1_d = nc.dram_tensor("xs1_d", [NSAMP * 8, D], F32).ap()
    KT_d = nc.dram_tensor("KT_d", [8, 128, NTOK], BF16).ap()
    V_d = nc.dram_tensor("V_d", [NTOK, D], BF16).ap()
    KTs_d = [nc.dram_tensor("KTs_d%d" % s, [8, 128, NKS], BF16).ap() for s in range(NSAMP)]
    Vs_d = [nc.dram_tensor("Vs_d%d" % s, [NKS, D], BF16).ap() for s in range(NSAMP)]

    b = Bld(nc)
    alloc_common(b, gla=False)
    b.sb("wstg0", [128, 1024], F32); b.sb("wstg1", [128, 1024], F32)
    b.sb("ssh", [128, SH], F32); b.sb("kf", [128, D], F32); b.sb("vf", [128, D], F32)
    b.sb("kbf", [128, D], BF16); b.sb("vb2", [128, D], BF16); b.sb("ktt", [128, 8, 128], BF16)
    cst = load_consts(b, c_ib, c_if, c_u, c_u4)
    zt = b.sb("zt", [128, D], BF16)
    b.mset(zt[:], 0.0, ["zt"])
    mark = len(b._cms)
    alloc_gla(b)
    S = b.sb("S", [128, 1024], F32); Sbf = b.sb("Sbf", [128, 1024], BF16)
    W = load_weights_l0(b, win_a, wup, ba, gain[0, :], wout_a, onorm)
    L1kv = load_weights_l1(b, win_b, gain[1, :], wout_b, qn, kn, sbias, which="kv")
    b.mset(S[:], 0.0, ["S"]); b.mset(Sbf[:], 0.0, ["Sbf"])
    for i in range(NT):
        x1t = gla_tile(b, 128, xb[i * 128:(i + 1) * 128, :], x1_d[i * 128:(i + 1) * 128, :], S, Sbf, W, cst, tag="")
        l1_kv_tile(b, 128, x1t[:, :], L1kv[0], L1kv, cst, k_all[i * 128:(i + 1) * 128, :], v_all[i * 128:(i + 1) * 128, :], KT_d, V_d, i * 128, "", xkey="x1t")
    b.dma(st_p.rearrange("h d v -> d h v"), S[:].rearrange("p (h v) -> p h v", h=GH), ["S"], ["st_p"], final=True)
    for s in range(NSAMP):
        b.dma(S[:].rearrange("p (h v) -> p h v", h=GH), state[s].rearrange("h d v -> d h v"), [], ["S"])
        b.cp(Sbf[:], S[:], ["S"], ["Sbf"], eng="pool")
        x1t = gla_tile(b, 8, xs[s * 8:(s + 1) * 8, :], xs1_d[s * 8:(s + 1) * 8, :], S, Sbf, W, cst, tag="s")
        b.dma(st_s[s].rearrange("h d v -> d h v"), S[:].rearrange("p (h v) -> p h v", h=GH), ["S"], ["st_s"], final=True)
        b.dma(KTs_d[s][:, :, NPG * 128:NKS].rearrange("c p t -> p c t"),
              zt[:].rearrange("p (c t) -> p c t", c=8), ["zt"], ["KTs_dS"], slow=True)
        b.dma(Vs_d[s][NPG * 128:NKS, :], zt[:], ["zt"], ["Vs_dS"])
        l1_kv_tile_s(b, 8, x1t[0:8, :], L1kv[0], L1kv, cst, k_s[s * 8:(s + 1) * 8, :], v_s[s * 8:(s + 1) * 8, :],
                     KTs_d[s], Vs_d[s], NPG * 128, "s%d" % s)
    b.p.barrier()
    b.release_to(mark)
    TQ = 512
    b.sb("qT", [128, 8, TQ], BF16); b.sb("sgT", [128, 8, TQ], BF16); b.sb("ogT1", [128, 8, TQ], BF16)
    b.sb("x1own", [128, TQ // 128, D], F32)
    tri = b.sb("tri", [128, 128], BF16); omt = b.sb("omt", [128, 128], BF16)
    b.dma(tri[:], c_tri, [], ["tri"]); b.dma(omt[:], c_omt, [], ["omt"])
    L1qg = load_weights_l1(b, win_b, gain[1, :], wout_b, qn, kn, sbias, which="qg")
    biasb = L1qg[4]
    x1own = b.t["x1own"]
    orow = b.sb("orow", [128, NOWN // 128], I32)
    b.dma(orow[:], own_rows, [], ["orow"])
    mark2 = len(b._cms)
    b.sb("e_t", [128, 128], F32); b.sb("L_t", [128, 128], BF16); b.sb("P_t", [128, 128], F32); b.sb("A_t", [128, 128], BF16)
    masknew = b.sb("masknew", [128, 128], BF16)
    b.dma(masknew[:], c_masknew, [], ["masknew"])
    biasfull = b.sb("biasfull", [128, SH, 8], F32)
    b.cp(biasfull[:], biasb[:, :].unsqueeze(2).to_broadcast([128, SH, 8]), ["biasb"], ["biasfull"])
    biasfull2 = biasfull[:].rearrange("p h q -> p (h q)")
    Qbd = b.sb("Qbd", [128, 8, 16], BF16)
    b.mset(Qbd[:], 0.0, ["Qbd"])
    for par in range(2):
        b.sb("kpg%d" % par, [128, D], F32); b.sb("vpg%d" % par, [128, D], F32)
        b.sb("kbfp%d" % par, [128, D], BF16); b.sb("vbp%d" % par, [128, D], BF16); b.sb("kttp%d" % par, [128, 8, 128], BF16)
    ptb = b.sb("ptb", [128, NSAMP * NPG], I32); ptf = b.sb("ptf", [128, NSAMP * NPG], F32)
    idx = b.sb("idx", [128, NSAMP * NPG], I32); iot = b.sb("iot", [128, 1], F32)
    b.dma(ptb[:], pt.rearrange("(o n) -> o n", o=1).partition_broadcast(128), [], ["ptb"], slow=True)
    b.dma(iot[:], c_iota, [], ["iot"])
    b.cp(ptf[:], ptb[:], ["ptb"], ["ptf"])
    b.ts(ptf[:], ptf[:], 128.0, iot[:, 0:1], ALU.mult, ALU.add, ["ptf", "iot"], ["ptf"])
    b.cp(idx[:], ptf[:], ["ptf"], ["idx"])
    for s in range(NSAMP):
        b.dma(x1own[0:8, 0, :], xs1_d[s * 8:(s + 1) * 8, :], ["x1drams"], ["x1own"])
        l1_qg_tile(b, 8, x1own[0:8, 0, :], L1qg[0], L1qg, cst, b.t["qT"], b.t["sgT"], 0)
        sample_attn(b, s, L1qg, cst, (masknew, tri, omt, biasfull2), NPG, ck, cv, idx,
                    KTs_d[s][:, :, NPG * 128:NKS], Vs_d[s][NPG * 128:NKS, :], b.t["qT"], b.t["sgT"],
                    ys[s * 8:(s + 1) * 8, :], x1own, zt)
    b.p.barrier()
    b.release_to(mark2)
    b.sb("KTs", [128, NTOK], BF16); b.sb("Vs", [128, NTOK // 128, 128], BF16)
    for nm_ in ("pe00", "pe01", "pe10", "pe11", "pL00", "pL01", "pL10", "pL11", "pP0", "pP1", "pA0", "pA1"):
        b.sb(nm_, [128, TQ], BF16)
    maskt = b.sb("mask", [128, NM, 512], BF16)
    b.dma(maskt[:], c_mask, [], ["mask"])
    for g in range(NG):
        for ti in range(4):
            lt = g * 4 + ti
            b.p.dma("pool", lambda e, lt=lt, ti=ti: e.indirect_dma_start(
                out=x1own[:, ti, :], out_offset=None, in_=x1_d[:, :],
                in_offset=bass.IndirectOffsetOnAxis(ap=orow[:, lt:lt + 1], axis=0)), ["orow", "x1dram"], ["x1own"])
            l1_qg_tile(b, 128, x1own[:, ti, :], L1qg[0], L1qg, cst, b.t["qT"], b.t["sgT"], ti * 128)
        b.p.barrier()
        nkb = 4 * NS * (g + 1)
        sb_group3(b, L1qg, cst, (maskt, tri, omt), nkb, KT_d, V_d, b.t["qT"], b.t["sgT"], 512,
                  [y_own[(g * 4 + ti) * 128:(g * 4 + ti + 1) * 128, :] for ti in range(4)], x1own)
        b.p.barrier()
    b.finish()
    return b


def l1_kv_tile_s(b, T, xsb, W1, L1W, cst, k_out, v_out, KT_d, V_d, tok0, tag):
    l1_kv_tile(b, T, xsb, W1, L1W, cst, k_out, v_out, KT_d, V_d, tok0, tag, xkey="x1t",
               ktkey="KTs_dS", vkey="Vs_dS")


def alloc_l1b(b, nkeys_max, TQ=512):
    b.sb("e_t", [128, TQ], F32)
    b.sb("L_t", [128, TQ], BF16)
    b.sb("P_t", [128, TQ], F32)
    b.sb("A_t", [128, TQ], BF16)
    b.sb("KTs", [128, nkeys_max], BF16)
    b.sb("Vs", [128, nkeys_max // 128, 128], BF16)
    b.sb("qT", [128, 8, TQ], BF16)
    b.sb("sgT", [128, 8, TQ], BF16)
    b.sb("ogT1", [128, 8, TQ], BF16)
    b.sb("x1own", [128, TQ // 128, D], F32)
    b.sb("yt", [128, D], F32)


def alloc_gla(b):
    b.sb("acT", [17, 128], BF16)
    b.sb("vbf", [128, D], BF16)
    b.sb("spt", [128, 512], F32)
    b.sb("E1", [128, 512], F32)
    b.sb("E2", [128, 512], F32)
    b.sb("E3", [128, 512], F32)
    b.sb("nbl", [128, 4], F32)
    b.sb("dec", [128, 4], F32)
    b.sb("qd", [128, 512], BF16)
    b.sb("kd", [128, 512], BF16)
    b.sb("kb", [128, 512], BF16)
    b.sb("attm", [128, 512], BF16)
    b.sb("kbt", [128, 512], BF16)
    b.sb("sso", [128, 4], F32)
    b.sb("og", [128, D], BF16)
    b.sb("ogT", [128, 8, 128], BF16)
    b.mset(b.t["acT"][:], 1.0, ["acT"])


def sb_outproj(b, L1W, TQ, y_dst_tiles, x1own):
    t = b.t
    Wo1 = L1W[1]
    PD, yt, ogT1 = t["PD"], t["on"], t["ogT1"]
    Tt = min(128, TQ)
    for ti in range(max(1, TQ // 128)):
        for n in range(2):
            for c in range(8):
                b.mm(PD[0:Tt, n * 512:(n + 1) * 512], ogT1[:, c, ti * 128:ti * 128 + Tt], Wo1[:, c, n * 512:(n + 1) * 512],
                     ["ogT1", "Wo1"], ["PD"], start=(c == 0), stop=(c == 7))
        b.tt(yt[0:Tt, :], PD[0:Tt, :], x1own[0:Tt, ti, :], ALU.add, ["PD", "x1own"], ["on"])
        b.dma(y_dst_tiles[ti], yt[0:Tt, :], ["on"], ["ydst"], q="pool", final=True)


def sb_group2(b, L1W, cst, sbc, nkb, KT_d, V_d, qT, sgT, TQ, y_dst_tiles, x1own):
    t = b.t
    _, Wo1, qng, kng, biasb = L1W
    maskt, tri, omt = sbc
    nmask = maskt.shape[1]
    KTs, Vs, ogT1 = t["KTs"], t["Vs"], t["ogT1"]
    PA, PB, PC, PD = t["PA"], t["PB"], t["PC"], t["PD"]
    Zs = [PB[:, 0:512], PC[:, 0:512]]
    ACCs = [PB[:, 512:1024], PC[:, 512:1024]]
    Os = [PA[:, 512:1024], PD[:, 0:512]]
    one = t["one"]
    for pr in range(8):
        b.dma(KTs[:, 0:nkb * 128], KT_d[pr, :, 0:nkb * 128], ["KT_d"], ["KTs"])
        b.dma(Vs[:, 0:nkb, :], V_d[0:nkb * 128, pr * 128:(pr + 1) * 128].rearrange("(k p) c -> p k c", p=128),
              ["V_d"], ["Vs"], q="pool")
        for i, kb in enumerate(range(nkb - 1, -1, -1)):
            mrel = kb - (nkb - nmask)
            m_ap = maskt[:, mrel, 0:TQ] if mrel >= 0 else None
            first, last = (i == 0), (kb == 0)
            KTb = KTs[:, kb * 128:(kb + 1) * 128]
            Vb = Vs[:, kb, :]
            L2 = range(2)
            e = [t["e_t"], t["e_t2"]]; Lt = [t["L_t"], t["L_t2"]]; P = [t["P_t"], t["P_t2"]]; A = [t["A_t"], t["A_t2"]]
            k = lambda n, l: n + str(l)
            for l in L2:
                p0 = 64 * l
                b.mm(Zs[l][:, 0:TQ], KTb[p0:p0 + 64, :], qT[p0:p0 + 64, pr, 0:TQ], ["KTs", "qT"], [k("Z", l)])
            for l in L2:
                h = 2 * pr + l
                b.act(e[l][:, 0:TQ], Zs[l][:, 0:TQ], AF.Exp, [k("Z", l), "biasb"], [k("e", l)], bias=biasb[:, h:h + 1])
            if m_ap is not None:
                for l in L2:
                    b.tt(e[l][:, 0:TQ], e[l][:, 0:TQ], m_ap, ALU.mult, [k("e", l), "mask"], [k("e", l)],
                         eng="dve" if l == 0 else "pool")
            for l in L2:
                b.act(Lt[l][:, 0:TQ], e[l][:, 0:TQ], AF.Ln, [k("e", l), "one"], [k("L", l)], bias=one[:, 0:1])
            for l in L2:
                b.mm(ACCs[l][:, 0:TQ], tri[:, :], Lt[l][:, 0:TQ], [k("L", l), "tri"], [k("ACC", l)], start=first, stop=False)
            for l in L2:
                b.act(P[l][:, 0:TQ], ACCs[l][:, 0:TQ], AF.Exp, [k("ACC", l)], [k("P", l)], scale=-1.0)
            for l in L2:
                b.mm(ACCs[l][:, 0:TQ], omt[:, :], Lt[l][:, 0:TQ], [k("L", l), "omt"], [k("ACC", l)], start=False, stop=last)
            for l in L2:
                b.tt(A[l][:, 0:TQ], e[l][:, 0:TQ], P[l][:, 0:TQ], ALU.mult, [k("e", l), k("P", l)], [k("A", l)])
            for l in L2:
                b.mm(Os[l][:, 0:TQ], Vb, A[l][:, 0:TQ], ["Vs", k("A", l)], [k("O", l)], start=first, stop=last)
        for l in range(2):
            p0 = 64 * l
            b.tt(ogT1[p0:p0 + 64, pr, 0:TQ], Os[l][p0:p0 + 64, 0:TQ], sgT[p0:p0 + 64, pr, 0:TQ], ALU.mult,
                 ["O" + str(l), "sgT"], ["ogT1"])
    b.p.barrier()
    sb_outproj(b, L1W, TQ, y_dst_tiles, x1own)


def sample_attn(b, s, L1W, cst, smc, NPG, ck, cv, idx, KTn_d, Vn_d, qT, sgT, y_dst, x1own, zt):
    t = b.t
    _, Wo1, qng, kng, biasb = L1W
    masknew, tri, omt, biasfull = smc
    ident_bf = cst[0]
    PA, PB, PC = t["PA"], t["PB"], t["PC"]
    PAb = PA[:].bitcast(BF16)
    Z, ACC, O = PB[:, 0:128], PB[:, 512:640], PC[:, 0:128]
    Qbd, ogT1 = t["Qbd"], t["ogT1"]
    one = t["one"]
    b.cp(Qbd[0:64, :, 0:8], qT[0:64, :, 0:8], ["qT"], ["Qbd"])
    b.cp(Qbd[64:128, :, 8:16], qT[64:128, :, 0:8], ["qT"], ["Qbd"])
    b.mm(O, zt[:, 0:128], zt[:, 0:128], ["zt"], ["O"], start=True, stop=False)
    nblk = NPG + 1
    kbl = list(range(nblk - 1, -1, -1))

    def prep(i):
        kb = kbl[i]
        par = i % 2
        kpg, vpg, kbf, vbp, ktt = t["kpg%d" % par], t["vpg%d" % par], t["kbfp%d" % par], t["vbp%d" % par], t["kttp%d" % par]
        kk = lambda n: n + str(par)
        if kb == NPG:
            b.dma(ktt[:, :, :], KTn_d.rearrange("c p t -> p c t"), ["KTs_dS"], [kk("ktt")], slow=True)
            b.dma(vbp[:, :], Vn_d, ["Vs_dS"], [kk("vbp")])
        else:
            c = s * NPG + kb
            b.p.dma("pool", lambda e_, c=c, kpg=kpg: e_.indirect_dma_start(
                out=kpg[:, :], out_offset=None, in_=ck[:, :],
                in_offset=bass.IndirectOffsetOnAxis(ap=idx[:, c:c + 1], axis=0)), ["idx"], [kk("kpg")])
            b.p.dma("pool", lambda e_, c=c, vpg=vpg: e_.indirect_dma_start(
                out=vpg[:, :], out_offset=None, in_=cv[:, :],
                in_offset=bass.IndirectOffsetOnAxis(ap=idx[:, c:c + 1], axis=0)), ["idx"], [kk("vpg")])
            b.cp(kbf[:, :], kpg[:, :], [kk("kpg")], [kk("kbf")], eng="dve")
            for c8 in range(8):
                b.tr(PAb[:, c8 * 128:(c8 + 1) * 128], kbf[:, c8 * 128:(c8 + 1) * 128], ident_bf[:, :],
                     [kk("kbf"), "ident_bf"], ["PA"])
            b.cp(ktt[:, :, :], PAb[:, 0:1024].rearrange("p (c t) -> p c t", c=8), ["PA"], [kk("ktt")])
            b.cp(vbp[:, :], vpg[:, :], [kk("vpg")], [kk("vbp")], eng="act")

    def unit(i):
        kb = kbl[i]
        par = i % 2
        ktt, vbp = t["kttp%d" % par], t["vbp%d" % par]
        kk = lambda n: n + str(par)
        first, last = (i == 0), (kb == 0)
        for pr in range(8):
            b.mm(Z[:, pr * 16:(pr + 1) * 16], ktt[:, pr, :], Qbd[:, pr, :], [kk("ktt"), "Qbd"], ["Zs"])
        e, Lt, P, A = t["e_t"], t["L_t"], t["P_t"], t["A_t"]
        b.tt(e[:, 0:128], Z, biasfull[:, :], ALU.add, ["Zs", "biasfull"], ["e0"])
        b.act(e[:, 0:128], e[:, 0:128], AF.Exp, ["e0"], ["e0"])
        if kb == NPG:
            b.tt(e[:, 0:128], e[:, 0:128], masknew[:, :], ALU.mult, ["e0", "masknew"], ["e0"])
        b.act(Lt[:, 0:128], e[:, 0:128], AF.Ln, ["e0", "one"], ["L0"], bias=one[:, 0:1])
        b.mm(ACC, tri[:, :], Lt[:, 0:128], ["L0", "tri"], ["ACCs"], start=first, stop=False)
        b.act(P[:, 0:128], ACC, AF.Exp, ["ACCs"], ["P0"], scale=-1.0)
        b.mm(ACC, omt[:, :], Lt[:, 0:128], ["L0", "omt"], ["ACCs"], start=False, stop=last)
        b.tt(A[:, 0:128], e[:, 0:128], P[:, 0:128], ALU.mult, ["e0", "P0"], ["A0"])
        for pr in range(8):
            b.mm(O[:, pr * 16:(pr + 1) * 16], vbp[:, pr * 128:(pr + 1) * 128], A[:, pr * 16:(pr + 1) * 16],
                 [kk("vbp"), "A0"], ["O"], start=False, stop=(last and pr == 7))

    prep(0)
    for i in range(nblk):
        if i + 1 < nblk:
            prep(i + 1)
        unit(i)
    O3 = O.rearrange("p (c x) -> p c x", c=8)
    b.tt(ogT1[0:64, :, 0:8], O3[0:64, :, 0:8], sgT[0:64, :, 0:8], ALU.mult, ["O", "sgT"], ["ogT1"])
    b.tt(ogT1[64:128, :, 0:8], O3[64:128, :, 8:16], sgT[64:128, :, 0:8], ALU.mult, ["O", "sgT"], ["ogT1"])
    sb_outproj(b, L1W, 8, [y_dst], x1own)


def sb_group3(b, L1W, cst, sbc, nkb, KT_d, V_d, qT, sgT, TQ, y_dst_tiles, x1own):
    t = b.t
    _, Wo1, qng, kng, biasb = L1W
    maskt, tri, omt = sbc
    nmask = maskt.shape[1]
    KTs, Vs, ogT1 = t["KTs"], t["Vs"], t["ogT1"]
    PA, PB, PC, PD = t["PA"], t["PB"], t["PC"], t["PD"]
    Zs = [[PA[:, 0:512], PA[:, 512:1024]], [PC[:, 0:512], PC[:, 512:1024]]]
    ACCs = [PB[:, 0:512], PD[:, 0:512]]
    Os = [PB[:, 512:1024], PD[:, 512:1024]]
    one = t["one"]
    e = [[t["pe00"], t["pe01"]], [t["pe10"], t["pe11"]]]
    Lt = [[t["pL00"], t["pL01"]], [t["pL10"], t["pL11"]]]
    P = [t["pP0"], t["pP1"]]; A = [t["pA0"], t["pA1"]]
    for pr in range(8):
        b.dma(KTs[:, 0:nkb * 128], KT_d[pr, :, 0:nkb * 128], ["KT_d"], ["KTs"])
        b.dma(Vs[:, 0:nkb, :], V_d[0:nkb * 128, pr * 128:(pr + 1) * 128].rearrange("(k p) c -> p k c", p=128),
              ["V_d"], ["Vs"], q="pool")
        kbs = list(range(nkb - 1, -1, -1))
        NSg = nmask // 4

        def col0(kb):
            kbrel = kb - (nkb - nmask)
            if kbrel < 0:
                return 0
            return 128 * max(0, min(3, -((-(kbrel - NSg + 1)) // NSg)))

        for l in range(2):
            b.mm(ACCs[l][:, 0:TQ], t["zt"][:, 0:128], t["zt"][:, 0:TQ], ["zt"], ["ACC%d" % l], start=True, stop=False)
            b.mm(Os[l][:, 0:TQ], t["zt"][:, 0:128], t["zt"][:, 0:TQ], ["zt"], ["O%d" % l], start=True, stop=False)

        def front(i):
            kb = kbs[i]; par = i % 2
            mrel = kb - (nkb - nmask)
            c0 = col0(kb)
            m_ap = maskt[:, mrel, c0:TQ] if mrel >= 0 else None
            KTb = KTs[:, kb * 128:(kb + 1) * 128]
            for l in range(2):
                p0 = 64 * l
                b.mm(Zs[l][par][:, c0:TQ], KTb[p0:p0 + 64, :], qT[p0:p0 + 64, pr, c0:TQ], ["KTs", "qT"], ["Z%d%d" % (l, par)])
            for l in range(2):
                h = 2 * pr + l
                b.act(e[l][par][:, c0:TQ], Zs[l][par][:, c0:TQ], AF.Exp, ["Z%d%d" % (l, par), "biasb"], ["e%d%d" % (l, par)],
                      bias=biasb[:, h:h + 1])
            if m_ap is not None:
                for l in range(2):
                    b.tt(e[l][par][:, c0:TQ], e[l][par][:, c0:TQ], m_ap, ALU.mult, ["e%d%d" % (l, par), "mask"],
                         ["e%d%d" % (l, par)], eng="dve" if l == 0 else "pool")
            for l in range(2):
                b.act(Lt[l][par][:, c0:TQ], e[l][par][:, c0:TQ], AF.Ln, ["e%d%d" % (l, par), "one"], ["L%d%d" % (l, par)],
                      bias=one[:, 0:1])

        def back_a(i):
            kb = kbs[i]; par = i % 2
            c0 = col0(kb)
            for l in range(2):
                b.mm(ACCs[l][:, c0:TQ], tri[:, :], Lt[l][par][:, c0:TQ], ["L%d%d" % (l, par), "tri"], ["ACC%d" % l],
                     start=False, stop=False)
            for l in range(2):
                b.act(P[l][:, c0:TQ], ACCs[l][:, c0:TQ], AF.Exp, ["ACC%d" % l], ["P%d" % l], scale=-1.0)
            for l in range(2):
                b.tt(A[l][:, c0:TQ], e[l][par][:, c0:TQ], P[l][:, c0:TQ], ALU.mult, ["e%d%d" % (l, par), "P%d" % l], ["A%d" % l])

        def back_b(i):
            kb = kbs[i]; par = i % 2
            c0 = col0(kb)
            last = (kb == 0)
            Vb = Vs[:, kb, :]
            for l in range(2):
                b.mm(ACCs[l][:, c0:TQ], omt[:, :], Lt[l][par][:, c0:TQ], ["L%d%d" % (l, par), "omt"], ["ACC%d" % l],
                     start=False, stop=last)
            for l in range(2):
                b.mm(Os[l][:, c0:TQ], Vb, A[l][:, c0:TQ], ["Vs", "A%d" % l], ["O%d" % l], start=False, stop=last)

        n = len(kbs)
        front(0)
        for i in range(n):
            back_a(i)
            if i + 1 < n:
                front(i + 1)
            back_b(i)
        for l in range(2):
            p0 = 64 * l
            b.tt(ogT1[p0:p0 + 64, pr, 0:TQ], Os[l][p0:p0 + 64, 0:TQ], sgT[p0:p0 + 64, pr, 0:TQ], ALU.mult,
                 ["O%d" % l, "sgT"], ["ogT1"])
    b.p.barrier()
    sb_outproj(b, L1W, TQ, y_dst_tiles, x1own)


def _run(inp, cfg, ncores=8):
    NTOK, NS, NSAMP, NPG, NPHYS = cfg["NTOK"], cfg["NS"], cfg["NSAMP"], cfg["NPG"], cfg["NPHYS"]
    NOWN = NTOK // NS
    nc = bass.Bass("TRN2", target_bir_lowering=False)
    b = build_program(nc, cfg)
    c = consts_np()
    f32 = np.float32
    cs = np.zeros((128, 16, 8), f32)
    for s in range(8):
        cs[s, :, :] = (s < np.arange(8))[None, :]
    cs = cs.reshape(128, 128)
    import ml_dtypes
    ck = np.ascontiguousarray(inp["cache_k"][0].reshape(NPHYS * 128, 1024))
    cv = np.ascontiguousarray(inp["cache_v"][0].reshape(NPHYS * 128, 1024))
    in_maps = []
    for core in range(ncores):
        bb, j = core // NS, core % NS
        c2 = consts_sb_np(j, NS)
        own_tiles = NS * np.arange(NOWN // 128) + j
        own_rows = (own_tiles[None, :] * 128 + np.arange(128)[:, None]).astype(np.int32)
        m = {
            "xb": np.ascontiguousarray(inp["x_prompt"][bb]), "own_rows": own_rows,
            "xs": np.ascontiguousarray(inp["x_sample"][core * NSAMP:(core + 1) * NSAMP].reshape(NSAMP * 8, 1024)),
            "state": np.ascontiguousarray(inp["state_gla"][0, core * NSAMP:(core + 1) * NSAMP]),
            "ck": ck, "cv": cv,
            "pt": np.ascontiguousarray(inp["page_table"][core * NSAMP:(core + 1) * NSAMP].reshape(-1)).astype(np.int32),
            "gain": inp["norm_gain"], "win_a": inp["w_in_a"][0], "wup": inp["w_alpha_up"][0], "ba": inp["b_alpha"][0],
            "onorm": inp["onorm_a"][0], "wout_a": inp["w_out_a"][0], "win_b": inp["w_in_b"][0], "qn": inp["qnorm_b"][0],
            "kn": inp["knorm_b"][0], "sbias": inp["sb_bias"][0], "wout_b": inp["w_out_b"][0],
            "c_ib": c["ident_bf"], "c_if": c["ident_f"], "c_u": c["u_f"], "c_u4": c["u4_f"],
            "c_mask": c2["mask"], "c_tri": c2["tri"], "c_omt": c2["omt"],
            "c_masknew": cs.astype(ml_dtypes.bfloat16), "c_iota": np.arange(128, dtype=f32).reshape(128, 1),
        }
        in_maps.append({k: np.ascontiguousarray(v) for k, v in m.items()})
    res = run_bass_kernel_spmd(nc, in_maps, core_ids=list(range(ncores))).results
    NB = ncores // NS
    DB = ncores * NSAMP
    y_p = np.zeros((NB, NTOK, 1024), f32); y_s = np.zeros((DB, 8, 1024), f32)
    sp = np.zeros((1, NB, 4, 128, 256), f32); ss = np.zeros((1, DB, 4, 128, 256), f32)
    kp = np.zeros((1, NB, NTOK, 16, 64), f32); vp = np.zeros_like(kp)
    ks = np.zeros((1, DB, 8, 16, 64), f32); vs = np.zeros_like(ks)
    for core in range(ncores):
        bb, j = core // NS, core % NS
        r = res[core]
        yo = r["y_own"].reshape(NOWN // 128, 128, 1024)
        for i in range(NOWN // 128):
            t = NS * i + j
            y_p[bb, t * 128:(t + 1) * 128] = yo[i]
        y_s[core * NSAMP:(core + 1) * NSAMP] = r["ys"].reshape(NSAMP, 8, 1024)
        ss[0, core * NSAMP:(core + 1) * NSAMP] = r["st_s"]
        ks[0, core * NSAMP:(core + 1) * NSAMP] = r["k_s"].reshape(NSAMP, 8, 16, 64)
        vs[0, core * NSAMP:(core + 1) * NSAMP] = r["v_s"].reshape(NSAMP, 8, 16, 64)
        if j == 0:
            sp[0, bb] = r["st_p"]
            kp[0, bb] = r["k_all"].reshape(NTOK, 16, 64)
            vp[0, bb] = r["v_all"].reshape(NTOK, 16, 64)
    return (y_p, y_s, sp, ss, kp, vp, ks, vs)


CFG = dict(NTOK=8192, NS=4, NSAMP=16, NPG=16, NPHYS=2560)


def kernel(**inputs):
    inp = {k: np.asarray(v) for k, v in inputs.items()}
    return _run(inp, CFG, 8)
```

```python
import numpy as np
import ml_dtypes
import concourse.bass as bass
import concourse.mybir as mybir
from concourse.bass_utils import run_bass_kernel_spmd


ENGS = ("pe", "act", "dve", "pool", "sp")


class Prog:
    def __init__(self, nc, same_engine_sync=True):
        self.nc = nc
        self.ops = {e: [] for e in ENGS}
        self.count = {e: 0 for e in ENGS}
        self.esem = {}
        self.dsem = {}
        self.last_w = {}
        self.reads = {}
        self.known = {e: {} for e in ENGS}
        self.sems = {}
        self.same_engine_sync = same_engine_sync
        self._ctx = []
        self.final_tokens = []
        self.pending = {e: {} for e in ENGS}

    def _new_sem(self, name):
        cm = self.nc.semaphore(name)
        h = cm.__enter__()
        self._ctx.append(cm)
        sid = len(self.sems)
        self.sems[sid] = h
        return sid

    def _eng_sem(self, e):
        if e not in self.esem:
            self.esem[e] = self._new_sem("es_" + e)
        return self.esem[e]

    def _collect(self, e, reads, writes, is_dma):
        need = dict(self.pending[e])
        self.pending[e] = {}
        def add(tok):
            s, v = tok
            if need.get(s, 0) < v:
                need[s] = v
        for k in reads:
            for t in self.last_w.get(k, ()):
                add(t)
        for k in writes:
            for t in self.last_w.get(k, ()):
                add(t)
            for t in self.reads.get(k, ()):
                add(t)
        waits = []
        own = self.esem.get(e)
        for s, v in need.items():
            if self.known[e].get(s, 0) >= v:
                continue
            if (not is_dma) and s == own:
                if e == "pe" or not self.same_engine_sync:
                    continue
            waits.append((s, v))
            self.known[e][s] = v
        return waits

    def op(self, e, fn, reads=(), writes=()):
        waits = self._collect(e, reads, writes, False)
        s = self._eng_sem(e)
        self.count[e] += 1
        tok = (s, self.count[e])
        self.ops[e].append((waits, fn, tok, 1))
        for k in reads:
            self.reads.setdefault(k, []).append(tok)
        for k in writes:
            self.last_w[k] = [tok]
            self.reads[k] = []
        return tok

    def dma(self, e, fn, reads=(), writes=(), final=False):
        assert writes
        waits = self._collect(e, reads, writes, True)
        wk = writes[0]
        if wk not in self.dsem:
            self.dsem[wk] = [self._new_sem("ds%d" % len(self.dsem)), 0]
        ent = self.dsem[wk]
        ent[1] += 16
        tok = (ent[0], ent[1])
        self.ops[e].append((waits, fn, tok, 16))
        for k in reads:
            self.reads.setdefault(k, []).append(tok)
        for k in writes:
            self.last_w[k] = [tok]
            self.reads[k] = []
        if final:
            self.final_tokens.append(tok)
        return tok

    def barrier(self):
        toks = [(self.esem[e], self.count[e]) for e in self.esem if self.count[e] > 0]
        toks += [(s, c) for (s, c) in self.dsem.values()]
        for e in ENGS:
            for s, v in toks:
                if self.pending[e].get(s, 0) < v:
                    self.pending[e][s] = v

    def emit(self):
        nc = self.nc
        with nc.Block() as block:
            def run(e, eng):
                for waits, fn, tok, inc in self.ops[e]:
                    for s, v in waits:
                        eng.wait_ge(self.sems[s], v)
                    fn(eng).then_inc(self.sems[tok[0]], inc)
                if e == "sp":
                    fin = {}
                    for s, v in self.final_tokens:
                        fin[s] = max(fin.get(s, 0), v)
                    for s, v in fin.items():
                        eng.wait_ge(self.sems[s], v)

            @block.tensor
            def _(eng):
                run("pe", eng)

            @block.scalar
            def _(eng):
                run("act", eng)

            @block.vector
            def _(eng):
                run("dve", eng)

            @block.gpsimd
            def _(eng):
                run("pool", eng)

            @block.sync
            def _(eng):
                run("sp", eng)

    def close(self):
        for cm in reversed(self._ctx):
            cm.__exit__(None, None, None)
        self._ctx = []


F32 = mybir.dt.float32
BF16 = mybir.dt.bfloat16
I32 = mybir.dt.int32
AF = mybir.ActivationFunctionType
ALU = mybir.AluOpType
AX = mybir.AxisListType

D = 1024
EPS = 1e-6
GH, GDK, GDV = 4, 128, 256
GIN = 3088
SH, SDH = 16, 64


class Bld:
    def __init__(self, nc):
        self.nc = nc
        self.p = Prog(nc)
        self._cms = []
        self.t = {}

    def sb(self, name, shape, dt):
        cm = self.nc.sbuf_tensor(name, list(shape), dt)
        h = cm.__enter__()
        self._cms.append(cm)
        self.t[name] = h
        return h

    def ps(self, name, shape, dt=F32):
        cm = self.nc.psum_tensor(name, list(shape), dt)
        h = cm.__enter__()
        self._cms.append(cm)
        self.t[name] = h
        return h

    def release_to(self, mark):
        while len(self._cms) > mark:
            self._cms.pop().__exit__(None, None, None)

    def finish(self):
        self.p.emit()
        self.p.close()
        for cm in reversed(self._cms):
            cm.__exit__(None, None, None)

    def mm(self, out, lhsT, rhs, r, w, start=True, stop=True):
        return self.p.op("pe", lambda e: e.matmul(out, lhsT, rhs, start=start, stop=stop), r, w)

    def tr(self, out, in_, ident, r, w):
        return self.p.op("pe", lambda e: e.transpose(out, in_, ident), r, w)

    def act(self, out, in_, func, r, w, bias=None, scale=None, accum=None):
        kw = {}
        if bias is not None:
            kw["bias"] = bias
        if scale is not None:
            kw["scale"] = scale
        if accum is not None:
            kw["accum_out"] = accum
        return self.p.op("act", lambda e: e.activation(out, in_, func, **kw), r, w)

    def tt(self, out, in0, in1, op, r, w, eng="dve"):
        return self.p.op(eng, lambda e: e.tensor_tensor(out, in0, in1, op), r, w)

    def ts(self, out, in0, s1, s2, op0, op1, r, w, eng="dve"):
        if op1 is None:
            return self.p.op(eng, lambda e: e.tensor_scalar(out, in0, s1, None, op0), r, w)
        return self.p.op(eng, lambda e: e.tensor_scalar(out, in0, s1, s2, op0, op1), r, w)

    def stt(self, out, in0, scalar, in1, op0, op1, r, w):
        return self.p.op("dve", lambda e: e.scalar_tensor_tensor(out, in0, scalar, in1, op0, op1), r, w)

    def cp(self, out, in_, r, w, eng="dve"):
        if eng == "act":
            return self.p.op("act", lambda e: e.copy(out, in_), r, w)
        return self.p.op(eng, lambda e: e.tensor_copy(out, in_), r, w)

    def red(self, out, in_, r, w):
        return self.p.op("dve", lambda e: e.tensor_reduce(out, in_, AX.X, ALU.add), r, w)

    def mset(self, ap, val, w, eng="pool"):
        return self.p.op(eng, lambda e: e.memset(ap, val), (), w)

    def dma(self, out, in_, r, w, q="sp", final=False, slow=False):
        if slow:
            return self.p.dma(q, lambda e: e.dma_start(out=out, in_=in_, allow_slow_non_contiguous=True), r, w, final)
        return self.p.dma(q, lambda e: e.dma_start(out=out, in_=in_), r, w, final)

    def rstd(self, ssq, n, key):
        self.ts(ssq, ssq, 1.0 / n, EPS, ALU.mult, ALU.add, [key], [key])
        self.act(ssq, ssq, AF.Ln, [key], [key])
        self.act(ssq, ssq, AF.Exp, [key], [key], scale=-0.5)


def consts_np():
    c = {}
    c["ident_bf"] = np.eye(128, dtype=np.float32).astype(ml_dtypes.bfloat16)
    c["ident_f"] = np.eye(128, dtype=np.float32)
    u = np.triu(np.ones((128, 128), np.float32))
    c["u_f"] = u
    c["u4_f"] = np.tile(u, (1, 4))
    return c


def load_weights_l0(b, win, wup, ba, gain, wout, onorm):
    nc = b.nc
    Wa = b.sb("Wa", [128, 8, GIN], BF16)
    Wo = b.sb("Wo", [128, 8, D], BF16)
    stg = [b.t["wstg0"], b.t["wstg1"]]
    gcol = b.sb("gcol0", [128, 8], F32)
    b.dma(gcol[:], gain.rearrange("(c p) -> p c", p=128), [], ["gcol0"], slow=True)
    i = 0
    for c in range(8):
        for c0 in range(0, GIN, 1024):
            n = min(1024, GIN - c0)
            s = stg[i % 2]; k = "wstg%d" % (i % 2)
            b.dma(s[:, 0:n], win[c * 128:(c + 1) * 128, c0:c0 + n], [], [k], q="sp" if i % 2 == 0 else "pool")
            b.ts(Wa[:, c, c0:c0 + n], s[:, 0:n], gcol[:, c:c + 1], None, ALU.mult, None, [k, "gcol0"], ["Wa"],
                 eng="dve" if i % 2 == 0 else "pool")
            i += 1
    for c in range(8):
        s = stg[i % 2]; k = "wstg%d" % (i % 2)
        b.dma(s[:, 0:D], wout[c * 128:(c + 1) * 128, :], [], [k], q="sp" if i % 2 == 0 else "pool")
        b.cp(Wo[:, c, :], s[:, 0:D], [k], ["Wo"], eng="dve" if i % 2 == 0 else "pool")
        i += 1
    wupf = b.sb("wupf", [17, 512], F32)
    b.dma(wupf[0:16, :], wup, [], ["wupf"])
    b.dma(wupf[16:17, :], ba.rearrange("(o n) -> o n", o=1), [], ["wupf"])
    wupb = b.sb("wupb", [17, 512], BF16)
    b.cp(wupb[:], wupf[:], ["wupf"], ["wupb"])
    ong = b.sb("ong", [128, GH, GDV], F32)
    for h in range(GH):
        b.dma(ong[:, h, :], onorm.rearrange("(o n) -> o n", o=1).partition_broadcast(128), [], ["ong"], slow=True)
    return Wa, Wo, wupb, ong


def gla_tile(b, T, x_src, x1_dst, S, Sbf, W, cst, tag=""):
    Wa, Wo, wupb, ong = W
    t = b.t
    nc = b.nc
    ident_bf, ident_f, u_f, u4_f = cst
    xt, xn, hT = t["xt"], t["xn"], t["hT"]
    PA, PB, PC, PD = t["PA"], t["PB"], t["PC"], t["PD"]
    junk = t["junk"]
    b.dma(xt[0:T, :], x_src, [], ["xt"])
    b.act(junk[0:T, :], xt[0:T, :], AF.Square, ["xt"], ["junk", "ssq"], accum=t["ssq"][0:T, 0:1])
    b.rstd(t["ssq"][0:T, 0:1], D, "ssq")
    b.ts(xn[0:T, :], xt[0:T, :], t["ssq"][0:T, 0:1], None, ALU.mult, None, ["xt", "ssq"], ["xn"])
    PAb = PA[:].bitcast(BF16)
    for c in range(8):
        b.tr(PAb[:, c * 128:c * 128 + T], xn[0:T, c * 128:(c + 1) * 128], ident_bf[0:T, 0:T], ["xn", "ident_bf"], ["PA"])
    b.cp(hT[:, :, 0:T], PAb[:, 0:1024].rearrange("p (c t) -> p c t", c=8)[:, :, 0:T], ["PA"], ["hT"], eng="act")
    for j in range(8):
        for c in range(8):
            b.mm(PB[:, j * 128:j * 128 + T], Wa[:, c, j * 128:(j + 1) * 128], hT[:, c, 0:T],
                 ["Wa", "hT"], ["PB"], start=(c == 0), stop=(c == 7))
    for c in range(8):
        b.mm(PA[0:16, 512:512 + T], Wa[:, c, 3072:3088], hT[:, c, 0:T], ["Wa", "hT"], ["PA"],
             start=(c == 0), stop=(c == 7))
    acT = t["acT"]
    b.cp(acT[0:16, 0:T], PA[0:16, 512:512 + T], ["PA"], ["acT"])
    for n in range(2):
        for c in range(8):
            b.mm(PC[0:T, n * 512:(n + 1) * 512], hT[:, c, 0:T], Wa[:, c, 1024 + n * 512:1024 + (n + 1) * 512],
                 ["Wa", "hT"], ["PC"], start=(c == 0), stop=(c == 7))
    vbf = t["vbf"]
    b.cp(vbf[0:T, :], PC[0:T, :], ["PC"], ["vbf"], eng="act")
    for n in range(2):
        for c in range(8):
            b.mm(PD[0:T, n * 512:(n + 1) * 512], hT[:, c, 0:T], Wa[:, c, 2048 + n * 512:2048 + (n + 1) * 512],
                 ["Wa", "hT"], ["PD"], start=(c == 0), stop=(c == 7))
    sg = t["sg"]
    b.act(sg[0:T, :], PD[0:T, :], AF.Exp, ["PD"], ["sg"], scale=-1.0)
    b.ts(sg[0:T, :], sg[0:T, :], 1.0, None, ALU.add, None, ["sg"], ["sg"])
    b.p.op("dve", lambda e: e.reciprocal(sg[0:T, :], sg[0:T, :]), ["sg"], ["sg"])
    b.tt(sg[0:T, :], sg[0:T, :], PD[0:T, :], ALU.mult, ["sg", "PD"], ["sg"])
    b.mm(PA[0:T, 0:512], acT[0:17, 0:T], wupb[0:17, :], ["acT", "wupb"], ["PA"])
    spt = t["spt"]
    b.act(spt[0:T, :], PA[0:T, 0:512], AF.Exp, ["PA"], ["spt"], scale=-1.0)
    b.act(spt[0:T, :], spt[0:T, :], AF.Ln, ["spt", "one"], ["spt"], bias=t["one"][0:T, 0:1])
    for h in range(GH):
        b.mm(PA[:, h * 128:h * 128 + T], spt[0:T, h * 128:(h + 1) * 128], u_f[0:T, 0:T], ["spt", "u_f"], ["PA"])
    E1, E2, E3, nbl, dec = t["E1"], t["E2"], t["E3"], t["nbl"], t["dec"]
    PA4 = PA[:, 0:512].rearrange("p (h t) -> p h t", h=GH)
    b.act(E1[:].rearrange("p (h t) -> p h t", h=GH)[:, :, 0:T], PA4[:, :, 0:T], AF.Exp, ["PA"], ["E1"], scale=-1.0 / 16)
    b.act(E2[:].rearrange("p (h t) -> p h t", h=GH)[:, :, 0:T], PA4[:, :, 0:T], AF.Exp, ["PA"], ["E2"], scale=1.0 / 16)
    b.ts(nbl[:, :], PA4[:, :, T - 1], -1.0 / 16, None, ALU.mult, None, ["PA"], ["nbl"])
    for h in range(GH):
        b.act(E3[:, h * 128:h * 128 + T], PA[:, h * 128:h * 128 + T], AF.Exp, ["PA", "nbl"], ["E3"],
              scale=1.0 / 16, bias=nbl[:, h:h + 1])
    b.act(dec[:, :], nbl[:, :], AF.Exp, ["nbl"], ["dec"])
    qd, kd, kb = t["qd"], t["kd"], t["kb"]
    def v4(ap):
        return ap.rearrange("p (h t) -> p h t", h=GH)[:, :, 0:T]
    b.stt(v4(qd[:]), v4(PB[:, 0:512]), float(GDK) ** -0.5, v4(E1[:]), ALU.mult, ALU.mult, ["PB", "E1"], ["qd"])
    b.tt(v4(kd[:]), v4(PB[:, 512:1024]), v4(E2[:]), ALU.mult, ["PB", "E2"], ["kd"])
    b.tt(v4(kb[:]), v4(PB[:, 512:1024]), v4(E3[:]), ALU.mult, ["PB", "E3"], ["kb"])
    for h in range(GH):
        b.mm(PB[0:T, h * 128:h * 128 + T], kd[:, h * 128:h * 128 + T], qd[:, h * 128:h * 128 + T],
             ["kd", "qd"], ["PB"])
    attm = t["attm"]
    b.tt(v4(attm[0:T, :]), v4(PB[0:T, 0:512]), v4(u4_f[0:T, :]), ALU.mult, ["PB", "u4_f"], ["attm"])
    for h in range(GH):
        b.tr(PAb[0:T, h * 128:(h + 1) * 128], kb[:, h * 128:h * 128 + T], ident_bf[:, :], ["kb", "ident_bf"], ["PA"])
    kbt = t["kbt"]
    b.cp(kbt[0:T, :], PAb[0:T, 0:512], ["PA"], ["kbt"])
    for h in range(GH):
        b.mm(PC[0:T, h * 256:(h + 1) * 256], attm[0:T, h * 128:h * 128 + T], vbf[0:T, h * 256:(h + 1) * 256],
             ["attm", "vbf"], ["PC"], start=True, stop=False)
        b.mm(PC[0:T, h * 256:(h + 1) * 256], qd[:, h * 128:h * 128 + T], Sbf[:, h * 256:(h + 1) * 256],
             ["qd", "Sbf"], ["PC"], start=False, stop=True)
    for h in range(GH):
        b.mm(PD[:, h * 256:(h + 1) * 256], kbt[0:T, h * 128:(h + 1) * 128], vbf[0:T, h * 256:(h + 1) * 256],
             ["kbt", "vbf"], ["PD"])
    for h in range(GH):
        b.stt(S[:, h * 256:(h + 1) * 256], S[:, h * 256:(h + 1) * 256], dec[:, h:h + 1],
              PD[:, h * 256:(h + 1) * 256], ALU.mult, ALU.add, ["S", "dec", "PD"], ["S"])
    b.cp(Sbf[:], S[:], ["S"], ["Sbf"], eng="act")
    sso = t["sso"]
    for h in range(GH):
        b.act(junk[0:T, 0:256], PC[0:T, h * 256:(h + 1) * 256], AF.Square, ["PC"], ["junk", "sso"],
              accum=sso[0:T, h:h + 1])
    b.rstd(sso[0:T, 0:GH], GDV, "sso")
    on = t["on"]
    for h in range(GH):
        b.stt(on[0:T, h * 256:(h + 1) * 256], PC[0:T, h * 256:(h + 1) * 256], sso[0:T, h:h + 1],
              ong[0:T, h, :], ALU.mult, ALU.mult, ["PC", "sso", "ong"], ["on"])
    og = t["og"]
    b.tt(og[0:T, :], on[0:T, :], sg[0:T, :], ALU.mult, ["on", "sg"], ["og"])
    for c in range(8):
        b.tr(PAb[:, c * 128:c * 128 + T], og[0:T, c * 128:(c + 1) * 128], ident_bf[0:T, 0:T], ["og", "ident_bf"], ["PA"])
    ogT = t["ogT"]
    b.cp(ogT[:, :, 0:T], PAb[:, 0:1024].rearrange("p (c t) -> p c t", c=8)[:, :, 0:T], ["PA"], ["ogT"], eng="act")
    for n in range(2):
        for c in range(8):
            b.mm(PD[0:T, n * 512:(n + 1) * 512], ogT[:, c, 0:T], Wo[:, c, n * 512:(n + 1) * 512],
                 ["ogT", "Wo"], ["PD"], start=(c == 0), stop=(c == 7))
    x1t = t["x1t"]
    b.tt(x1t[0:T, :], PD[0:T, :], xt[0:T, :], ALU.add, ["PD", "xt"], ["x1t"])
    if x1_dst is not None:
        b.dma(x1_dst, x1t[0:T, :], ["x1t"], ["x1dram" + tag], q="pool")
    return x1t


def alloc_common(b, gla=True):
    b.sb("xt", [128, D], F32)
    b.sb("xn", [128, D], BF16)
    b.sb("hT", [128, 8, 128], BF16)
    b.sb("junk", [128, D], BF16)
    b.sb("ssq", [128, 1], F32)
    b.sb("sg", [128, D], F32)
    b.sb("one", [128, 1], F32)
    b.sb("on", [128, D], F32)
    if gla:
        b.sb("acT", [17, 128], BF16)
        b.sb("vbf", [128, D], BF16)
        b.sb("spt", [128, 512], F32)
        b.sb("E1", [128, 512], F32)
        b.sb("E2", [128, 512], F32)
        b.sb("E3", [128, 512], F32)
        b.sb("nbl", [128, 4], F32)
        b.sb("dec", [128, 4], F32)
        b.sb("qd", [128, 512], BF16)
        b.sb("kd", [128, 512], BF16)
        b.sb("kb", [128, 512], BF16)
        b.sb("attm", [128, 512], BF16)
        b.sb("kbt", [128, 512], BF16)
        b.sb("sso", [128, 4], F32)
        b.sb("og", [128, D], BF16)
        b.sb("ogT", [128, 8, 128], BF16)
    b.sb("x1t", [128, D], F32)
    b.ps("PA", [128, 1024], F32)
    b.ps("PB", [128, 1024], F32)
    b.ps("PC", [128, 1024], F32)
    b.ps("PD", [128, 1024], F32)
    b.mset(b.t["one"][:], 1.0, ["one"])
    if gla:
        b.mset(b.t["acT"][:], 1.0, ["acT"])


def load_consts(b, d_ident_bf, d_ident_f, d_u, d_u4):
    ib = b.sb("ident_bf", [128, 128], BF16)
    i_f = b.sb("ident_f", [128, 128], F32)
    u = b.sb("u_f", [128, 128], F32)
    u4 = b.sb("u4_f", [128, 512], F32)
    b.dma(ib[:], d_ident_bf, [], ["ident_bf"])
    b.dma(i_f[:], d_ident_f, [], ["ident_f"])
    b.dma(u[:], d_u, [], ["u_f"])
    b.dma(u4[:], d_u4, [], ["u4_f"])
    return ib, i_f, u, u4


def consts_sb_np(j, ns):
    nk = 4 * ns
    kpos = (np.arange(nk)[:, None] * 128 + np.arange(128)[None, :])
    qpos = ((ns * np.arange(4)[:, None] + j) * 128 + np.arange(128)[None, :]).reshape(-1)
    m = (kpos[:, :, None] < qpos[None, None, :]).astype(np.float32)
    c = {}
    c["mask"] = np.ascontiguousarray(m.transpose(1, 0, 2)).astype(ml_dtypes.bfloat16)
    tl = np.tril(np.ones((128, 128), np.float32))
    c["tri"] = tl.astype(ml_dtypes.bfloat16)
    c["omt"] = (1.0 - tl).astype(ml_dtypes.bfloat16)
    return c


def load_weights_l1(b, winb, gain1, woutb, qn, kn, sbias, which="all"):
    if which == "kv":
        segs = [(1024, 2048)]; cb = {"k": 0, "v": 1024}
    elif which == "qg":
        segs = [(0, 1024), (3072, 1024)]; cb = {"q": 0, "g": 1024}
    else:
        segs = [(0, 2048), (2048, 2048)]; cb = {"q": 0, "k": 1024, "v": 2048, "g": 3072}
    ncol = sum(n for _, n in segs)
    W1 = b.sb("W1" + which, [128, 8, ncol], BF16)
    wk = "W1" + which
    gcol = b.sb("gcol1" + which, [128, 8], F32)
    b.dma(gcol[:], gain1.rearrange("(c p) -> p c", p=128), [], ["gcol1" + which], slow=True)
    stg = [b.t["wstg0"], b.t["wstg1"]]
    i = 0
    for c in range(8):
        lo = 0
        for (c0, n) in segs:
            for s0 in range(0, n, 1024):
                s = stg[i % 2]; k = "wstg%d" % (i % 2)
                b.dma(s[:, 0:1024], winb[c * 128:(c + 1) * 128, c0 + s0:c0 + s0 + 1024], [], [k],
                      q="sp" if i % 2 == 0 else "pool")
                b.ts(W1[:, c, lo + s0:lo + s0 + 1024], s[:, 0:1024], gcol[:, c:c + 1], None, ALU.mult, None,
                     [k, "gcol1" + which], [wk], eng="dve" if i % 2 == 0 else "pool")
                i += 1
            lo += n
    Wo1 = None
    if which != "kv":
        Wo1 = b.sb("Wo1", [128, 8, D], BF16)
        for c in range(8):
            s = stg[i % 2]; k = "wstg%d" % (i % 2)
            b.dma(s[:, 0:D], woutb[c * 128:(c + 1) * 128, :], [], [k], q="sp" if i % 2 == 0 else "pool")
            b.cp(Wo1[:, c, :], s[:, 0:D], [k], ["Wo1"], eng="dve" if i % 2 == 0 else "pool")
            i += 1
    qng = b.sb("qng" + which, [128, SH, SDH], F32)
    kng = b.sb("kng" + which, [128, SH, SDH], F32)
    for h in range(SH):
        b.dma(qng[:, h, :], qn.rearrange("(o n) -> o n", o=1).partition_broadcast(128), [], ["qng"], slow=True)
        b.dma(kng[:, h, :], kn.rearrange("(o n) -> o n", o=1).partition_broadcast(128), [], ["kng"], slow=True,
              q="pool")
    b.ts(qng[:], qng[:], float(SDH) ** -0.5, None, ALU.mult, None, ["qng"], ["qng"])
    biasb = b.sb("biasb" + which, [128, SH], F32)
    b.dma(biasb[:], sbias.rearrange("(o n) -> o n", o=1).partition_broadcast(128), [], ["biasb"], slow=True)
    return (W1, wk, cb), Wo1, qng, kng, biasb


def l1_norm_T(b, T, xsb, cst, xkey="x1t"):
    t = b.t
    ident_bf = cst[0]
    junk, ssq, xn, hT, PA = t["junk"], t["ssq"], t["xn"], t["hT"], t["PA"]
    b.act(junk[0:T, :], xsb, AF.Square, [xkey], ["junk", "ssq"], accum=ssq[0:T, 0:1])
    b.rstd(ssq[0:T, 0:1], D, "ssq")
    b.ts(xn[0:T, :], xsb, ssq[0:T, 0:1], None, ALU.mult, None, [xkey, "ssq"], ["xn"])
    PAb = PA[:].bitcast(BF16)
    for c in range(8):
        b.tr(PAb[:, c * 128:c * 128 + T], xn[0:T, c * 128:(c + 1) * 128], ident_bf[0:T, 0:T], ["xn", "ident_bf"], ["PA"])
    b.cp(hT[:, :, 0:T], PAb[:, 0:1024].rearrange("p (c t) -> p c t", c=8)[:, :, 0:T], ["PA"], ["hT"], eng="act")
    return hT


def proj_tok(b, T, PS, pskey, W1k, which):
    hT = b.t["hT"]
    W1, wkey, cb = W1k
    col0 = cb[which]
    for n in range(2):
        for c in range(8):
            b.mm(PS[0:T, n * 512:(n + 1) * 512], hT[:, c, 0:T], W1[:, c, col0 + n * 512:col0 + (n + 1) * 512],
                 [wkey, "hT"], [pskey], start=(c == 0), stop=(c == 7))


def headnorm(b, T, PS, pskey, gain_t, gkey, out_f, okey):
    t = b.t
    sq, ssh = t["on"], t["ssh"]
    b.act(sq[0:T, :], PS[0:T, :], AF.Square, [pskey], ["on"])
    b.red(ssh[0:T, :], sq[0:T, :].rearrange("p (h d) -> p h d", h=SH), ["on"], ["ssh"])
    b.rstd(ssh[0:T, :], SDH, "ssh")
    o3 = out_f[0:T, :].rearrange("p (h d) -> p h d", h=SH)
    b.tt(o3, PS[0:T, :].rearrange("p (h d) -> p h d", h=SH),
         ssh[0:T, :].unsqueeze(2).to_broadcast([T, SH, SDH]), ALU.mult, [pskey, "ssh"], [okey])
    b.tt(o3, o3, gain_t[0:T, :, :], ALU.mult, [okey, gkey], [okey])


def to_pairT(b, T, src_bf, skey, dst, dkey, col0, cst):
    PA = b.t["PA"]
    PAb = PA[:].bitcast(BF16)
    ident_bf = cst[0]
    for c in range(8):
        b.tr(PAb[:, c * 128:c * 128 + T], src_bf[0:T, c * 128:(c + 1) * 128], ident_bf[0:T, 0:T],
             [skey, "ident_bf"], ["PA"])
    b.cp(dst[:, :, col0:col0 + T], PAb[:, 0:1024].rearrange("p (c t) -> p c t", c=8)[:, :, 0:T], ["PA"], [dkey])


def l1_kv_tile(b, T, xsb, W1, L1W, cst, k_out, v_out, KT_d, V_d, tok0, tag, xkey="x1t", ktkey="KT_d", vkey="V_d"):
    t = b.t
    W1_, Wo1, qng, kng, biasb = L1W
    PC, PD = t["PC"], t["PD"]
    l1_norm_T(b, T, xsb, cst, xkey)
    proj_tok(b, T, PC, "PC", W1, "k")
    proj_tok(b, T, PD, "PD", W1, "v")
    kf, vf, kbf, vb2, ktt = t["kf"], t["vf"], t["kbf"], t["vb2"], t["ktt"]
    headnorm(b, T, PC, "PC", kng, "kng", kf, "kf")
    if k_out is not None:
        b.dma(k_out, kf[0:T, :], ["kf"], ["kout"], q="pool", final=True)
    b.cp(kbf[0:T, :], kf[0:T, :], ["kf"], ["kbf"], eng="act")
    to_pairT(b, T, kbf, "kbf", ktt, "ktt", 0, cst)
    b.dma(KT_d[:, :, tok0:tok0 + T].rearrange("c p t -> p c t"), ktt[:, :, 0:T], ["ktt"], [ktkey], slow=True)
    b.cp(vf[0:T, :], PD[0:T, :], ["PD"], ["vf"], eng="act")
    if v_out is not None:
        b.dma(v_out, vf[0:T, :], ["vf"], ["vout"], q="pool", final=True)
    b.cp(vb2[0:T, :], vf[0:T, :], ["vf"], ["vb2"])
    b.dma(V_d[tok0:tok0 + T, :], vb2[0:T, :], ["vb2"], [vkey])


def l1_qg_tile(b, T, xsb, W1, L1W, cst, qT, sgT, col0, xkey="x1own"):
    t = b.t
    W1_, Wo1, qng, kng, biasb = L1W
    PC, PD = t["PC"], t["PD"]
    l1_norm_T(b, T, xsb, cst, xkey)
    proj_tok(b, T, PC, "PC", W1, "q")
    proj_tok(b, T, PD, "PD", W1, "g")
    kf, kbf, sg, vb2 = t["kf"], t["kbf"], t["sg"], t["vb2"]
    headnorm(b, T, PC, "PC", qng, "qng", kf, "kf")
    b.cp(kbf[0:T, :], kf[0:T, :], ["kf"], ["kbf"], eng="pool")
    to_pairT(b, T, kbf, "kbf", qT, "qT", col0, cst)
    b.act(sg[0:T, :], PD[0:T, :], AF.Exp, ["PD"], ["sg"], scale=-1.0)
    b.ts(sg[0:T, :], sg[0:T, :], 1.0, None, ALU.add, None, ["sg"], ["sg"])
    b.p.op("dve", lambda e: e.reciprocal(sg[0:T, :], sg[0:T, :]), ["sg"], ["sg"])
    b.tt(vb2[0:T, :], sg[0:T, :], PD[0:T, :], ALU.mult, ["sg", "PD"], ["vb2"])
    to_pairT(b, T, vb2, "vb2", sgT, "sgT", col0, cst)


def sb_unit(b, hp, KTb, Vb, kcol, TQ, qT, mask_ap, first, last, Z, ACC, O, biasb, h, tri, omt):
    t = b.t
    e_t, L_t, P_t, A_t = t["e_t"], t["L_t"], t["P_t"], t["A_t"]
    p0 = 64 * hp
    b.mm(Z[:, 0:TQ], KTb[p0:p0 + 64, :], qT[p0:p0 + 64, 0:TQ], ["KTs", "qT"], ["Z"])
    b.act(e_t[:, 0:TQ], Z[:, 0:TQ], AF.Exp, ["Z", "biasb"], ["e_t"], bias=biasb[:, h:h + 1])
    if mask_ap is not None:
        b.tt(e_t[:, 0:TQ], e_t[:, 0:TQ], mask_ap, ALU.mult, ["e_t", "mask", "masks"], ["e_t"])
    b.act(L_t[:, 0:TQ], e_t[:, 0:TQ], AF.Ln, ["e_t", "one"], ["L_t"], bias=t["one"][:, 0:1])
    b.mm(ACC[:, 0:TQ], tri[:, :], L_t[:, 0:TQ], ["L_t", "tri"], ["ACC"], start=first, stop=False)
    b.act(P_t[:, 0:TQ], ACC[:, 0:TQ], AF.Exp, ["ACC"], ["P_t"], scale=-1.0)
    b.mm(ACC[:, 0:TQ], omt[:, :], L_t[:, 0:TQ], ["L_t", "omt"], ["ACC"], start=False, stop=last)
    b.tt(A_t[:, 0:TQ], e_t[:, 0:TQ], P_t[:, 0:TQ], ALU.mult, ["e_t", "P_t"], ["A_t"])
    b.mm(O[:, 0:TQ], Vb, A_t[:, 0:TQ], ["Vs", "A_t"], ["O"], start=first, stop=last)


def alloc_l1(b, nkeys_max, TQ=512):
    b.sb("ssh", [128, SH], F32)
    b.sb("kf", [128, D], F32)
    b.sb("vf", [128, D], F32)
    b.sb("kbf", [128, D], BF16)
    b.sb("vb2", [128, D], BF16)
    b.sb("ktt", [128, 8, 128], BF16)
    b.sb("e_t", [128, TQ], F32)
    b.sb("L_t", [128, TQ], BF16)
    b.sb("P_t", [128, TQ], F32)
    b.sb("A_t", [128, TQ], BF16)
    b.sb("KTs", [128, nkeys_max], BF16)
    b.sb("Vs", [128, nkeys_max // 128, 128], BF16)
    b.sb("qT", [128, 8, TQ], BF16)
    b.sb("sgT", [128, 8, TQ], BF16)
    b.sb("ogT", [128, 8, TQ], BF16) if "ogT" not in b.t else None
    b.sb("ogT1", [128, 8, TQ], BF16)
    b.sb("x1own", [128, TQ // 128, D], F32)
    b.sb("yt", [128, D], F32)


def sb_group(b, L1W, cst, sbc, nkb, KT_d, V_d, qT, sgT, TQ, y_dst_tiles, x1own, ktkey="KT_d", vkey="V_d"):
    t = b.t
    W1_, Wo1, qng, kng, biasb = L1W
    maskt, tri, omt = sbc
    nmask = maskt.shape[1]
    KTs, Vs, ogT1 = t["KTs"], t["Vs"], t["ogT1"]
    PA, PB = t["PA"], t["PB"]
    Z, ACC, O = PB[:, 0:512], PB[:, 512:1024], PA[:, 512:1024]
    for pr in range(8):
        b.dma(KTs[:, 0:nkb * 128], KT_d[pr, :, 0:nkb * 128], [ktkey], ["KTs"])
        b.dma(Vs[:, 0:nkb, :], V_d[0:nkb * 128, pr * 128:(pr + 1) * 128].rearrange("(k p) c -> p k c", p=128),
              [vkey], ["Vs"], q="pool")
        for hp in range(2):
            h = 2 * pr + hp
            for i, kb in enumerate(range(nkb - 1, -1, -1)):
                mrel = kb - (nkb - nmask)
                m_ap = maskt[:, mrel, 0:TQ] if mrel >= 0 else None
                sb_unit(b, hp, KTs[:, kb * 128:(kb + 1) * 128], Vs[:, kb, :], kb, TQ, qT[:, pr, :], m_ap,
                        i == 0, kb == 0, Z, ACC, O, biasb, h, tri, omt)
            p0 = 64 * hp
            b.tt(ogT1[p0:p0 + 64, pr, 0:TQ], O[p0:p0 + 64, 0:TQ], sgT[p0:p0 + 64, pr, 0:TQ], ALU.mult,
                 ["O", "sgT"], ["ogT1"])
    PD = t["PD"]
    yt = t["yt"]
    Tt = min(128, TQ)
    for ti in range(max(1, TQ // 128)):
        for n in range(2):
            for c in range(8):
                b.mm(PD[0:Tt, n * 512:(n + 1) * 512], ogT1[:, c, ti * 128:ti * 128 + Tt], Wo1[:, c, n * 512:(n + 1) * 512],
                     ["ogT1", "Wo1"], ["PD"], start=(c == 0), stop=(c == 7))
        b.tt(yt[0:Tt, :], PD[0:Tt, :], x1own[0:Tt, ti, :], ALU.add, ["PD", "x1own"], ["yt"])
        b.dma(y_dst_tiles[ti], yt[0:Tt, :], ["yt"], ["ydst"], q="pool", final=True)


def l1_kv_rows(b, T, xsb, W1, L1W, cst, k_out, v_out, tag, xkey):
    t = b.t
    _, Wo1, qng, kng, biasb = L1W
    PC, PD = t["PC"], t["PD"]
    l1_norm_T(b, T, xsb, cst, xkey)
    proj_tok(b, T, PC, "PC", W1, "k")
    proj_tok(b, T, PD, "PD", W1, "v")
    kf, vf = t["kf"], t["vf"]
    headnorm(b, T, PC, "PC", kng, "kng", kf, "kf")
    b.dma(k_out, kf[0:T, :], ["kf"], ["kout"], q="pool", final=True)
    b.cp(vf[0:T, :], PD[0:T, :], ["PD"], ["vf"], eng="act")
    b.dma(v_out, vf[0:T, :], ["vf"], ["vout"], q="pool", final=True)


def build_program(nc, cfg):
    NTOK, NS, NSAMP, NPG, NPHYS = cfg["NTOK"], cfg["NS"], cfg["NSAMP"], cfg["NPG"], cfg["NPHYS"]
    NOWN = NTOK // NS
    NG = NOWN // 512
    NT = NTOK // 128
    NKS = (NPG + 1) * 128
    def din(n, s, dt=F32): return nc.dram_tensor(n, list(s), dt, kind="ExternalInput").ap()
    def dout(n, s, dt=F32): return nc.dram_tensor(n, list(s), dt, kind="ExternalOutput").ap()
    xb = din("xb", [NTOK, D]); own_rows = din("own_rows", [128, NOWN // 128], I32)
    xs = din("xs", [NSAMP * 8, D]); state = din("state", [NSAMP, GH, GDK, GDV])
    ck = din("ck", [NPHYS * 128, D]); cv = din("cv", [NPHYS * 128, D]); pt = din("pt", [NSAMP * NPG], I32)
    gain = din("gain", [2, D]); win_a = din("win_a", [D, GIN]); wup = din("wup", [16, 512]); ba = din("ba", [512])
    onorm = din("onorm", [256]); wout_a = din("wout_a", [D, D]); win_b = din("win_b", [D, 4096])
    qn = din("qn", [64]); kn = din("kn", [64]); sbias = din("sbias", [16]); wout_b = din("wout_b", [D, D])
    c_ib = din("c_ib", [128, 128], BF16); c_if = din("c_if", [128, 128]); c_u = din("c_u", [128, 128]); c_u4 = din("c_u4", [128, 512])
    NM = 4 * NS
    c_mask = din("c_mask", [128, NM, 512], BF16); c_tri = din("c_tri", [128, 128], BF16); c_omt = din("c_omt", [128, 128], BF16)
    c_masknew = din("c_masknew", [128, 128], BF16); c_iota = din("c_iota", [128, 1])
    y_own = dout("y_own", [NOWN, D]); ys = dout("ys", [NSAMP * 8, D])
    st_p = dout("st_p", [GH, GDK, GDV]); st_s = dout("st_s", [NSAMP, GH, GDK, GDV])
    k_all = dout("k_all", [NTOK, D]); v_all = dout("v_all", [NTOK, D])
    k_s = dout("k_s", [NSAMP * 8, D]); v_s = dout("v_s", [NSAMP * 8, D])
    x1_d = nc.dram_tensor("x1_d", [NTOK, D], F32).ap()
    xs1_d = nc.dram_tensor("xs1_d", [NSAMP * 8, D], F32).ap()
    KT_d = nc.dram_tensor("KT_d", [8, 128, NTOK], BF16).ap()
    V_d = nc.dram_tensor("V_d", [NTOK, D], BF16).ap()
    KTs_d = [nc.dram_tensor("KTs_d%d" % s, [8, 128, NKS], BF16).ap() for s in range(NSAMP)]
    Vs_d = [nc.dram_tensor("Vs_d%d" % s, [NKS, D], BF16).ap() for s in range(NSAMP)]

    b = Bld(nc)
    alloc_common(b, gla=False)
    b.sb("wstg0", [128, 1024], F32); b.sb("wstg1", [128, 1024], F32)
    b.sb("ssh", [128, SH], F32); b.sb("kf", [128, D], F32); b.sb("vf", [128, D], F32)
    b.sb("kbf", [128, D], BF16); b.sb("vb2", [128, D], BF16); b.sb("ktt", [128, 8, 128], BF16)
    cst = load_consts(b, c_ib, c_if, c_u, c_u4)
    zt = b.sb("zt", [128, D], BF16)
    b.mset(zt[:], 0.0, ["zt"])
    mark = len(b._cms)
    alloc_gla(b)
    S = b.sb("S", [128, 1024], F32); Sbf = b.sb("Sbf", [128, 1024], BF16)
    W = load_weights_l0(b, win_a, wup, ba, gain[0, :], wout_a, onorm)
    L1kv = load_weights_l1(b, win_b, gain[1, :], wout_b, qn, kn, sbias, which="kv")
    b.mset(S[:], 0.0, ["S"]); b.mset(Sbf[:], 0.0, ["Sbf"])
    for i in range(NT):
        x1t = gla_tile(b, 128, xb[i * 128:(i + 1) * 128, :], x1_d[i * 128:(i + 1) * 128, :], S, Sbf, W, cst, tag="")
        l1_kv_tile(b, 128, x1t[:, :], L1kv[0], L1kv, cst, k_all[i * 128:(i + 1) * 128, :], v_all[i * 128:(i + 1) * 128, :], KT_d, V_d, i * 128, "", xkey="x1t")
    b.dma(st_p.rearrange("h d v -> d h v"), S[:].rearrange("p (h v) -> p h v", h=GH), ["S"], ["st_p"], final=True)
    for s in range(NSAMP):
        b.dma(S[:].rearrange("p (h v) -> p h v", h=GH), state[s].rearrange("h d v -> d h v"), [], ["S"])
        b.cp(Sbf[:], S[:], ["S"], ["Sbf"], eng="pool")
        x1t = gla_tile(b, 8, xs[s * 8:(s + 1) * 8, :], xs1_d[s * 8:(s + 1) * 8, :], S, Sbf, W, cst, tag="s")
        b.dma(st_s[s].rearrange("h d v -> d h v"), S[:].rearrange("p (h v) -> p h v", h=GH), ["S"], ["st_s"], final=True)
        b.dma(KTs_d[s][:, :, NPG * 128:NKS].rearrange("c p t -> p c t"),
              zt[:].rearrange("p (c t) -> p c t", c=8), ["zt"], ["KTs_dS"], slow=True)
        b.dma(Vs_d[s][NPG * 128:NKS, :], zt[:], ["zt"], ["Vs_dS"])
        l1_kv_tile_s(b, 8, x1t[0:8, :], L1kv[0], L1kv, cst, k_s[s * 8:(s + 1) * 8, :], v_s[s * 8:(s + 1) * 8, :],
                     KTs_d[s], Vs_d[s], NPG * 128, "s%d" % s)
    b.p.barrier()
    b.release_to(mark)
    TQ = 512
    b.sb("qT", [128, 8, TQ], BF16); b.sb("sgT", [128, 8, TQ], BF16); b.sb("ogT1", [128, 8, TQ], BF16)
    b.sb("x1own", [128, TQ // 128, D], F32)
    tri = b.sb("tri", [128, 128], BF16); omt = b.sb("omt", [128, 128], BF16)
    b.dma(tri[:], c_tri, [], ["tri"]); b.dma(omt[:], c_omt, [], ["omt"])
    L1qg = load_weights_l1(b, win_b, gain[1, :], wout_b, qn, kn, sbias, which="qg")
    biasb = L1qg[4]
    x1own = b.t["x1own"]
    orow = b.sb("orow", [128, NOWN // 128], I32)
    b.dma(orow[:], own_rows, [], ["orow"])
    mark2 = len(b._cms)
    b.sb("e_t", [128, 128], F32); b.sb("L_t", [128, 128], BF16); b.sb("P_t", [128, 128], F32); b.sb("A_t", [128, 128], BF16)
    masknew = b.sb("masknew", [128, 128], BF16)
    b.dma(masknew[:], c_masknew, [], ["masknew"])
    biasfull = b.sb("biasfull", [128, SH, 8], F32)
    b.cp(biasfull[:], biasb[:, :].unsqueeze(2).to_broadcast([128, SH, 8]), ["biasb"], ["biasfull"])
    biasfull2 = biasfull[:].rearrange("p h q -> p (h q)")
    Qbd = b.sb("Qbd", [128, 8, 16], BF16)
    b.mset(Qbd[:], 0.0, ["Qbd"])
    for par in range(2):
        b.sb("kpg%d" % par, [128, D], F32); b.sb("vpg%d" % par, [128, D], F32)
        b.sb("kbfp%d" % par, [128, D], BF16); b.sb("vbp%d" % par, [128, D], BF16); b.sb("kttp%d" % par, [128, 8, 128], BF16)
    ptb = b.sb("ptb", [128, NSAMP * NPG], I32); ptf = b.sb("ptf", [128, NSAMP * NPG], F32)
    idx = b.sb("idx", [128, NSAMP * NPG], I32); iot = b.sb("iot", [128, 1], F32)
    b.dma(ptb[:], pt.rearrange("(o n) -> o n", o=1).partition_broadcast(128), [], ["ptb"], slow=True)
    b.dma(iot[:], c_iota, [], ["iot"])
    b.cp(ptf[:], ptb[:], ["ptb"], ["ptf"])
    b.ts(ptf[:], ptf[:], 128.0, iot[:, 0:1], ALU.mult, ALU.add, ["ptf", "iot"], ["ptf"])
    b.cp(idx[:], ptf[:], ["ptf"], ["idx"])
    for s in range(NSAMP):
        b.dma(x1own[0:8, 0, :], xs1_d[s * 8:(s + 1) * 8, :], ["x1drams"], ["x1own"])
        l1_qg_tile(b, 8, x1own[0:8, 0, :], L1qg[0], L1qg, cst, b.t["qT"], b.t["sgT"], 0)
        sample_attn(b, s, L1qg, cst, (masknew, tri, omt, biasfull2), NPG, ck, cv, idx,
                    KTs_d[s][:, :, NPG * 128:NKS], Vs_d[s][NPG * 128:NKS, :], b.t["qT"], b.t["sgT"],
                    ys[s * 8:(s + 1) * 8, :], x1own, zt)
    b.p.barrier()
    b.release_to(mark2)
    b.sb("KTs", [128, NTOK], BF16); b.sb("Vs", [128, NTOK // 128, 128], BF16)
    for nm_ in ("pe00", "pe01", "pe10", "pe11", "pL00", "pL01", "pL10", "pL11", "pP0", "pP1", "pA0", "pA1"):
        b.sb(nm_, [128, TQ], BF16)
    maskt = b.sb("mask", [128, NM, 512], BF16)
    b.dma(maskt[:], c_mask, [], ["mask"])
    for g in range(NG):
        for ti in range(4):
            lt = g * 4 + ti
            b.p.dma("pool", lambda e, lt=lt, ti=ti: e.indirect_dma_start(
                out=x1own[:, ti, :], out_offset=None, in_=x1_d[:, :],
                in_offset=bass.IndirectOffsetOnAxis(ap=orow[:, lt:lt + 1], axis=0)), ["orow", "x1dram"], ["x1own"])
            l1_qg_tile(b, 128, x1own[:, ti, :], L1qg[0], L1qg, cst, b.t["qT"], b.t["sgT"], ti * 128)
        b.p.barrier()
        nkb = 4 * NS * (g + 1)
        sb_group3(b, L1qg, cst, (maskt, tri, omt), nkb, KT_d, V_d, b.t["qT"], b.t["sgT"], 512,
                  [y_own[(g * 4 + ti) * 128:(g * 4 + ti + 1) * 128, :] for ti in range(4)], x1own)
        b.p.barrier()
    b.finish()
    return b


def l1_kv_tile_s(b, T, xsb, W1, L1W, cst, k_out, v_out, KT_d, V_d, tok0, tag):
    l1_kv_tile(b, T, xsb, W1, L1W, cst, k_out, v_out, KT_d, V_d, tok0, tag, xkey="x1t",
               ktkey="KTs_dS", vkey="Vs_dS")


def alloc_l1b(b, nkeys_max, TQ=512):
    b.sb("e_t", [128, TQ], F32)
    b.sb("L_t", [128, TQ], BF16)
    b.sb("P_t", [128, TQ], F32)
    b.sb("A_t", [128, TQ], BF16)
    b.sb("KTs", [128, nkeys_max], BF16)
    b.sb("Vs", [128, nkeys_max // 128, 128], BF16)
    b.sb("qT", [128, 8, TQ], BF16)
    b.sb("sgT", [128, 8, TQ], BF16)
    b.sb("ogT1", [128, 8, TQ], BF16)
    b.sb("x1own", [128, TQ // 128, D], F32)
    b.sb("yt", [128, D], F32)


def alloc_gla(b):
    b.sb("acT", [17, 128], BF16)
    b.sb("vbf", [128, D], BF16)
    b.sb("spt", [128, 512], F32)
    b.sb("E1", [128, 512], F32)
    b.sb("E2", [128, 512], F32)
    b.sb("E3", [128, 512], F32)
    b.sb("nbl", [128, 4], F32)
    b.sb("dec", [128, 4], F32)
    b.sb("qd", [128, 512], BF16)
    b.sb("kd", [128, 512], BF16)
    b.sb("kb", [128, 512], BF16)
    b.sb("attm", [128, 512], BF16)
    b.sb("kbt", [128, 512], BF16)
    b.sb("sso", [128, 4], F32)
    b.sb("og", [128, D], BF16)
    b.sb("ogT", [128, 8, 128], BF16)
    b.mset(b.t["acT"][:], 1.0, ["acT"])


def sb_outproj(b, L1W, TQ, y_dst_tiles, x1own):
    t = b.t
    Wo1 = L1W[1]
    PD, yt, ogT1 = t["PD"], t["on"], t["ogT1"]
    Tt = min(128, TQ)
    for ti in range(max(1, TQ // 128)):
        for n in range(2):
            for c in range(8):
                b.mm(PD[0:Tt, n * 512:(n + 1) * 512], ogT1[:, c, ti * 128:ti * 128 + Tt], Wo1[:, c, n * 512:(n + 1) * 512],
                     ["ogT1", "Wo1"], ["PD"], start=(c == 0), stop=(c == 7))
        b.tt(yt[0:Tt, :], PD[0:Tt, :], x1own[0:Tt, ti, :], ALU.add, ["PD", "x1own"], ["on"])
        b.dma(y_dst_tiles[ti], yt[0:Tt, :], ["on"], ["ydst"], q="pool", final=True)


def sb_group2(b, L1W, cst, sbc, nkb, KT_d, V_d, qT, sgT, TQ, y_dst_tiles, x1own):
    t = b.t
    _, Wo1, qng, kng, biasb = L1W
    maskt, tri, omt = sbc
    nmask = maskt.shape[1]
    KTs, Vs, ogT1 = t["KTs"], t["Vs"], t["ogT1"]
    PA, PB, PC, PD = t["PA"], t["PB"], t["PC"], t["PD"]
    Zs = [PB[:, 0:512], PC[:, 0:512]]
    ACCs = [PB[:, 512:1024], PC[:, 512:1024]]
    Os = [PA[:, 512:1024], PD[:, 0:512]]
    one = t["one"]
    for pr in range(8):
        b.dma(KTs[:, 0:nkb * 128], KT_d[pr, :, 0:nkb * 128], ["KT_d"], ["KTs"])
        b.dma(Vs[:, 0:nkb, :], V_d[0:nkb * 128, pr * 128:(pr + 1) * 128].rearrange("(k p) c -> p k c", p=128),
              ["V_d"], ["Vs"], q="pool")
        for i, kb in enumerate(range(nkb - 1, -1, -1)):
            mrel = kb - (nkb - nmask)
            m_ap = maskt[:, mrel, 0:TQ] if mrel >= 0 else None
            first, last = (i == 0), (kb == 0)
            KTb = KTs[:, kb * 128:(kb + 1) * 128]
            Vb = Vs[:, kb, :]
            L2 = range(2)
            e = [t["e_t"], t["e_t2"]]; Lt = [t["L_t"], t["L_t2"]]; P = [t["P_t"], t["P_t2"]]; A = [t["A_t"], t["A_t2"]]
            k = lambda n, l: n + str(l)
            for l in L2:
                p0 = 64 * l
                b.mm(Zs[l][:, 0:TQ], KTb[p0:p0 + 64, :], qT[p0:p0 + 64, pr, 0:TQ], ["KTs", "qT"], [k("Z", l)])
            for l in L2:
                h = 2 * pr + l
                b.act(e[l][:, 0:TQ], Zs[l][:, 0:TQ], AF.Exp, [k("Z", l), "biasb"], [k("e", l)], bias=biasb[:, h:h + 1])
            if m_ap is not None:
                for l in L2:
                    b.tt(e[l][:, 0:TQ], e[l][:, 0:TQ], m_ap, ALU.mult, [k("e", l), "mask"], [k("e", l)],
                         eng="dve" if l == 0 else "pool")
            for l in L2:
                b.act(Lt[l][:, 0:TQ], e[l][:, 0:TQ], AF.Ln, [k("e", l), "one"], [k("L", l)], bias=one[:, 0:1])
            for l in L2:
                b.mm(ACCs[l][:, 0:TQ], tri[:, :], Lt[l][:, 0:TQ], [k("L", l), "tri"], [k("ACC", l)], start=first, stop=False)
            for l in L2:
                b.act(P[l][:, 0:TQ], ACCs[l][:, 0:TQ], AF.Exp, [k("ACC", l)], [k("P", l)], scale=-1.0)
            for l in L2:
                b.mm(ACCs[l][:, 0:TQ], omt[:, :], Lt[l][:, 0:TQ], [k("L", l), "omt"], [k("ACC", l)], start=False, stop=last)
            for l in L2:
                b.tt(A[l][:, 0:TQ], e[l][:, 0:TQ], P[l][:, 0:TQ], ALU.mult, [k("e", l), k("P", l)], [k("A", l)])
            for l in L2:
                b.mm(Os[l][:, 0:TQ], Vb, A[l][:, 0:TQ], ["Vs", k("A", l)], [k("O", l)], start=first, stop=last)
        for l in range(2):
            p0 = 64 * l
            b.tt(ogT1[p0:p0 + 64, pr, 0:TQ], Os[l][p0:p0 + 64, 0:TQ], sgT[p0:p0 + 64, pr, 0:TQ], ALU.mult,
                 ["O" + str(l), "sgT"], ["ogT1"])
    b.p.barrier()
    sb_outproj(b, L1W, TQ, y_dst_tiles, x1own)


def sample_attn(b, s, L1W, cst, smc, NPG, ck, cv, idx, KTn_d, Vn_d, qT, sgT, y_dst, x1own, zt):
    t = b.t
    _, Wo1, qng, kng, biasb = L1W
    masknew, tri, omt, biasfull = smc
    ident_bf = cst[0]
    PA, PB, PC = t["PA"], t["PB"], t["PC"]
    PAb = PA[:].bitcast(BF16)
    Z, ACC, O = PB[:, 0:128], PB[:, 512:640], PC[:, 0:128]
    Qbd, ogT1 = t["Qbd"], t["ogT1"]
    one = t["one"]
    b.cp(Qbd[0:64, :, 0:8], qT[0:64, :, 0:8], ["qT"], ["Qbd"])
    b.cp(Qbd[64:128, :, 8:16], qT[64:128, :, 0:8], ["qT"], ["Qbd"])
    b.mm(O, zt[:, 0:128], zt[:, 0:128], ["zt"], ["O"], start=True, stop=False)
    nblk = NPG + 1
    kbl = list(range(nblk - 1, -1, -1))

    def prep(i):
        kb = kbl[i]
        par = i % 2
        kpg, vpg, kbf, vbp, ktt = t["kpg%d" % par], t["vpg%d" % par], t["kbfp%d" % par], t["vbp%d" % par], t["kttp%d" % par]
        kk = lambda n: n + str(par)
        if kb == NPG:
            b.dma(ktt[:, :, :], KTn_d.rearrange("c p t -> p c t"), ["KTs_dS"], [kk("ktt")], slow=True)
            b.dma(vbp[:, :], Vn_d, ["Vs_dS"], [kk("vbp")])
        else:
            c = s * NPG + kb
            b.p.dma("pool", lambda e_, c=c, kpg=kpg: e_.indirect_dma_start(
                out=kpg[:, :], out_offset=None, in_=ck[:, :],
                in_offset=bass.IndirectOffsetOnAxis(ap=idx[:, c:c + 1], axis=0)), ["idx"], [kk("kpg")])
            b.p.dma("pool", lambda e_, c=c, vpg=vpg: e_.indirect_dma_start(
                out=vpg[:, :], out_offset=None, in_=cv[:, :],
                in_offset=bass.IndirectOffsetOnAxis(ap=idx[:, c:c + 1], axis=0)), ["idx"], [kk("vpg")])
            b.cp(kbf[:, :], kpg[:, :], [kk("kpg")], [kk("kbf")], eng="dve")
            for c8 in range(8):
                b.tr(PAb[:, c8 * 128:(c8 + 1) * 128], kbf[:, c8 * 128:(c8 + 1) * 128], ident_bf[:, :],
                     [kk("kbf"), "ident_bf"], ["PA"])
            b.cp(ktt[:, :, :], PAb[:, 0:1024].rearrange("p (c t) -> p c t", c=8), ["PA"], [kk("ktt")])
            b.cp(vbp[:, :], vpg[:, :], [kk("vpg")], [kk("vbp")], eng="act")

    def unit(i):
        kb = kbl[i]
        par = i % 2
        ktt, vbp = t["kttp%d" % par], t["vbp%d" % par]
        kk = lambda n: n + str(par)
        first, last = (i == 0), (kb == 0)
        for pr in range(8):
            b.mm(Z[:, pr * 16:(pr + 1) * 16], ktt[:, pr, :], Qbd[:, pr, :], [kk("ktt"), "Qbd"], ["Zs"])
        e, Lt, P, A = t["e_t"], t["L_t"], t["P_t"], t["A_t"]
        b.tt(e[:, 0:128], Z, biasfull[:, :], ALU.add, ["Zs", "biasfull"], ["e0"])
        b.act(e[:, 0:128], e[:, 0:128], AF.Exp, ["e0"], ["e0"])
        if kb == NPG:
            b.tt(e[:, 0:128], e[:, 0:128], masknew[:, :], ALU.mult, ["e0", "masknew"], ["e0"])
        b.act(Lt[:, 0:128], e[:, 0:128], AF.Ln, ["e0", "one"], ["L0"], bias=one[:, 0:1])
        b.mm(ACC, tri[:, :], Lt[:, 0:128], ["L0", "tri"], ["ACCs"], start=first, stop=False)
        b.act(P[:, 0:128], ACC, AF.Exp, ["ACCs"], ["P0"], scale=-1.0)
        b.mm(ACC, omt[:, :], Lt[:, 0:128], ["L0", "omt"], ["ACCs"], start=False, stop=last)
        b.tt(A[:, 0:128], e[:, 0:128], P[:, 0:128], ALU.mult, ["e0", "P0"], ["A0"])
        for pr in range(8):
            b.mm(O[:, pr * 16:(pr + 1) * 16], vbp[:, pr * 128:(pr + 1) * 128], A[:, pr * 16:(pr + 1) * 16],
                 [kk("vbp"), "A0"], ["O"], start=False, stop=(last and pr == 7))

    prep(0)
    for i in range(nblk):
        if i + 1 < nblk:
            prep(i + 1)
        unit(i)
    O3 = O.rearrange("p (c x) -> p c x", c=8)
    b.tt(ogT1[0:64, :, 0:8], O3[0:64, :, 0:8], sgT[0:64, :, 0:8], ALU.mult, ["O", "sgT"], ["ogT1"])
    b.tt(ogT1[64:128, :, 0:8], O3[64:128, :, 8:16], sgT[64:128, :, 0:8], ALU.mult, ["O", "sgT"], ["ogT1"])
    sb_outproj(b, L1W, 8, [y_dst], x1own)


def sb_group3(b, L1W, cst, sbc, nkb, KT_d, V_d, qT, sgT, TQ, y_dst_tiles, x1own):
    t = b.t
    _, Wo1, qng, kng, biasb = L1W
    maskt, tri, omt = sbc
    nmask = maskt.shape[1]
    KTs, Vs, ogT1 = t["KTs"], t["Vs"], t["ogT1"]
    PA, PB, PC, PD = t["PA"], t["PB"], t["PC"], t["PD"]
    Zs = [[PA[:, 0:512], PA[:, 512:1024]], [PC[:, 0:512], PC[:, 512:1024]]]
    ACCs = [PB[:, 0:512], PD[:, 0:512]]
    Os = [PB[:, 512:1024], PD[:, 512:1024]]
    one = t["one"]
    e = [[t["pe00"], t["pe01"]], [t["pe10"], t["pe11"]]]
    Lt = [[t["pL00"], t["pL01"]], [t["pL10"], t["pL11"]]]
    P = [t["pP0"], t["pP1"]]; A = [t["pA0"], t["pA1"]]
    for pr in range(8):
        b.dma(KTs[:, 0:nkb * 128], KT_d[pr, :, 0:nkb * 128], ["KT_d"], ["KTs"])
        b.dma(Vs[:, 0:nkb, :], V_d[0:nkb * 128, pr * 128:(pr + 1) * 128].rearrange("(k p) c -> p k c", p=128),
              ["V_d"], ["Vs"], q="pool")
        kbs = list(range(nkb - 1, -1, -1))
        NSg = nmask // 4

        def col0(kb):
            kbrel = kb - (nkb - nmask)
            if kbrel < 0:
                return 0
            return 128 * max(0, min(3, -((-(kbrel - NSg + 1)) // NSg)))

        for l in range(2):
            b.mm(ACCs[l][:, 0:TQ], t["zt"][:, 0:128], t["zt"][:, 0:TQ], ["zt"], ["ACC%d" % l], start=True, stop=False)
            b.mm(Os[l][:, 0:TQ], t["zt"][:, 0:128], t["zt"][:, 0:TQ], ["zt"], ["O%d" % l], start=True, stop=False)

        def front(i):
            kb = kbs[i]; par = i % 2
            mrel = kb - (nkb - nmask)
            c0 = col0(kb)
            m_ap = maskt[:, mrel, c0:TQ] if mrel >= 0 else None
            KTb = KTs[:, kb * 128:(kb + 1) * 128]
            for l in range(2):
                p0 = 64 * l
                b.mm(Zs[l][par][:, c0:TQ], KTb[p0:p0 + 64, :], qT[p0:p0 + 64, pr, c0:TQ], ["KTs", "qT"], ["Z%d%d" % (l, par)])
            for l in range(2):
                h = 2 * pr + l
                b.act(e[l][par][:, c0:TQ], Zs[l][par][:, c0:TQ], AF.Exp, ["Z%d%d" % (l, par), "biasb"], ["e%d%d" % (l, par)],
                      bias=biasb[:, h:h + 1])
            if m_ap is not None:
                for l in range(2):
                    b.tt(e[l][par][:, c0:TQ], e[l][par][:, c0:TQ], m_ap, ALU.mult, ["e%d%d" % (l, par), "mask"],
                         ["e%d%d" % (l, par)], eng="dve")
            for l in range(2):
                b.act(Lt[l][par][:, c0:TQ], e[l][par][:, c0:TQ], AF.Ln, ["e%d%d" % (l, par), "one"], ["L%d%d" % (l, par)],
                      bias=one[:, 0:1])

        def back_a(i):
            kb = kbs[i]; par = i % 2
            c0 = col0(kb)
            for l in range(2):
                b.mm(ACCs[l][:, c0:TQ], tri[:, :], Lt[l][par][:, c0:TQ], ["L%d%d" % (l, par), "tri"], ["ACC%d" % l],
                     start=False, stop=False)
            for l in range(2):
                b.act(P[l][:, c0:TQ], ACCs[l][:, c0:TQ], AF.Exp, ["ACC%d" % l], ["P%d" % l], scale=-1.0)
            for l in range(2):
                b.tt(A[l][:, c0:TQ], e[l][par][:, c0:TQ], P[l][:, c0:TQ], ALU.mult, ["e%d%d" % (l, par), "P%d" % l], ["A%d" % l])

        def back_b(i):
            kb = kbs[i]; par = i % 2
            c0 = col0(kb)
            last = (kb == 0)
            Vb = Vs[:, kb, :]
            for l in range(2):
                b.mm(ACCs[l][:, c0:TQ], omt[:, :], Lt[l][par][:, c0:TQ], ["L%d%d" % (l, par), "omt"], ["ACC%d" % l],
                     start=False, stop=last)
            for l in range(2):
                b.mm(Os[l][:, c0:TQ], Vb, A[l][:, c0:TQ], ["Vs", "A%d" % l], ["O%d" % l], start=False, stop=last)

        n = len(kbs)
        front(0)
        for i in range(n):
            back_a(i)
            if i + 1 < n:
                front(i + 1)
            back_b(i)
        for l in range(2):
            p0 = 64 * l
            b.tt(ogT1[p0:p0 + 64, pr, 0:TQ], Os[l][p0:p0 + 64, 0:TQ], sgT[p0:p0 + 64, pr, 0:TQ], ALU.mult,
                 ["O%d" % l, "sgT"], ["ogT1"])
    b.p.barrier()
    sb_outproj(b, L1W, TQ, y_dst_tiles, x1own)


def _run(inp, cfg, ncores=8):
    NTOK, NS, NSAMP, NPG, NPHYS = cfg["NTOK"], cfg["NS"], cfg["NSAMP"], cfg["NPG"], cfg["NPHYS"]
    NOWN = NTOK // NS
    nc = bass.Bass("TRN2", target_bir_lowering=False)
    b = build_program(nc, cfg)
    c = consts_np()
    f32 = np.float32
    cs = np.zeros((128, 16, 8), f32)
    for s in range(8):
        cs[s, :, :] = (s < np.arange(8))[None, :]
    cs = cs.reshape(128, 128)
    import ml_dtypes
    ck = np.ascontiguousarray(inp["cache_k"][0].reshape(NPHYS * 128, 1024))
    cv = np.ascontiguousarray(inp["cache_v"][0].reshape(NPHYS * 128, 1024))
    in_maps = []
    for core in range(ncores):
        bb, j = core // NS, core % NS
        c2 = consts_sb_np(j, NS)
        own_tiles = NS * np.arange(NOWN // 128) + j
        own_rows = (own_tiles[None, :] * 128 + np.arange(128)[:, None]).astype(np.int32)
        m = {
            "xb": np.ascontiguousarray(inp["x_prompt"][bb]), "own_rows": own_rows,
            "xs": np.ascontiguousarray(inp["x_sample"][core * NSAMP:(core + 1) * NSAMP].reshape(NSAMP * 8, 1024)),
            "state": np.ascontiguousarray(inp["state_gla"][0, core * NSAMP:(core + 1) * NSAMP]),
            "ck": ck, "cv": cv,
            "pt": np.ascontiguousarray(inp["page_table"][core * NSAMP:(core + 1) * NSAMP].reshape(-1)).astype(np.int32),
            "gain": inp["norm_gain"], "win_a": inp["w_in_a"][0], "wup": inp["w_alpha_up"][0], "ba": inp["b_alpha"][0],
            "onorm": inp["onorm_a"][0], "wout_a": inp["w_out_a"][0], "win_b": inp["w_in_b"][0], "qn": inp["qnorm_b"][0],
            "kn": inp["knorm_b"][0], "sbias": inp["sb_bias"][0], "wout_b": inp["w_out_b"][0],
            "c_ib": c["ident_bf"], "c_if": c["ident_f"], "c_u": c["u_f"], "c_u4": c["u4_f"],
            "c_mask": c2["mask"], "c_tri": c2["tri"], "c_omt": c2["omt"],
            "c_masknew": cs.astype(ml_dtypes.bfloat16), "c_iota": np.arange(128, dtype=f32).reshape(128, 1),
        }
        in_maps.append({k: np.ascontiguousarray(v) for k, v in m.items()})
    res = run_bass_kernel_spmd(nc, in_maps, core_ids=list(range(ncores))).results
    NB = ncores // NS
    DB = ncores * NSAMP
    y_p = np.zeros((NB, NTOK, 1024), f32); y_s = np.zeros((DB, 8, 1024), f32)
    sp = np.zeros((1, NB, 4, 128, 256), f32); ss = np.zeros((1, DB, 4, 128, 256), f32)
    kp = np.zeros((1, NB, NTOK, 16, 64), f32); vp = np.zeros_like(kp)
    ks = np.zeros((1, DB, 8, 16, 64), f32); vs = np.zeros_like(ks)
    for core in range(ncores):
        bb, j = core // NS, core % NS
        r = res[core]
        yo = r["y_own"].reshape(NOWN // 128, 128, 1024)
        for i in range(NOWN // 128):
            t = NS * i + j
            y_p[bb, t * 128:(t + 1) * 128] = yo[i]
        y_s[core * NSAMP:(core + 1) * NSAMP] = r["ys"].reshape(NSAMP, 8, 1024)
        ss[0, core * NSAMP:(core + 1) * NSAMP] = r["st_s"]
        ks[0, core * NSAMP:(core + 1) * NSAMP] = r["k_s"].reshape(NSAMP, 8, 16, 64)
        vs[0, core * NSAMP:(core + 1) * NSAMP] = r["v_s"].reshape(NSAMP, 8, 16, 64)
        if j == 0:
            sp[0, bb] = r["st_p"]
            kp[0, bb] = r["k_all"].reshape(NTOK, 16, 64)
            vp[0, bb] = r["v_all"].reshape(NTOK, 16, 64)
    return (y_p, y_s, sp, ss, kp, vp, ks, vs)


CFG = dict(NTOK=8192, NS=4, NSAMP=16, NPG=16, NPHYS=2560)


def kernel(**inputs):
    inp = {k: np.asarray(v) for k, v in inputs.items()}
    return _run(inp, CFG, 8)
```

```python
import numpy as np
import ml_dtypes
import concourse.bass as bass
import concourse.mybir as mybir
from concourse.bass_utils import run_bass_kernel_spmd


ENGS = ("pe", "act", "dve", "pool", "sp")


class Prog:
    def __init__(self, nc, same_engine_sync=True):
        self.nc = nc
        self.ops = {e: [] for e in ENGS}
        self.count = {e: 0 for e in ENGS}
        self.esem = {}
        self.dsem = {}
        self.last_w = {}
        self.reads = {}
        self.known = {e: {} for e in ENGS}
        self.sems = {}
        self.same_engine_sync = same_engine_sync
        self._ctx = []
        self.final_tokens = []
        self.pending = {e: {} for e in ENGS}

    def _new_sem(self, name):
        cm = self.nc.semaphore(name)
        h = cm.__enter__()
        self._ctx.append(cm)
        sid = len(self.sems)
        self.sems[sid] = h
        return sid

    def _eng_sem(self, e):
        if e not in self.esem:
            self.esem[e] = self._new_sem("es_" + e)
        return self.esem[e]

    def _collect(self, e, reads, writes, is_dma):
        need = dict(self.pending[e])
        self.pending[e] = {}
        def add(tok):
            s, v = tok
            if need.get(s, 0) < v:
                need[s] = v
        for k in reads:
            for t in self.last_w.get(k, ()):
                add(t)
        for k in writes:
            for t in self.last_w.get(k, ()):
                add(t)
            for t in self.reads.get(k, ()):
                add(t)
        waits = []
        own = self.esem.get(e)
        for s, v in need.items():
            if self.known[e].get(s, 0) >= v:
                continue
            if (not is_dma) and s == own:
                if e == "pe" or not self.same_engine_sync:
                    continue
            waits.append((s, v))
            self.known[e][s] = v
        return waits

    def op(self, e, fn, reads=(), writes=()):
        waits = self._collect(e, reads, writes, False)
        s = self._eng_sem(e)
        self.count[e] += 1
        tok = (s, self.count[e])
        self.ops[e].append((waits, fn, tok, 1))
        for k in reads:
            self.reads.setdefault(k, []).append(tok)
        for k in writes:
            self.last_w[k] = [tok]
            self.reads[k] = []
        return tok

    def dma(self, e, fn, reads=(), writes=(), final=False):
        assert writes
        waits = self._collect(e, reads, writes, True)
        wk = writes[0]
        if wk not in self.dsem:
            self.dsem[wk] = [self._new_sem("ds%d" % len(self.dsem)), 0]
        ent = self.dsem[wk]
        ent[1] += 16
        tok = (ent[0], ent[1])
        self.ops[e].append((waits, fn, tok, 16))
        for k in reads:
            self.reads.setdefault(k, []).append(tok)
        for k in writes:
            self.last_w[k] = [tok]
            self.reads[k] = []
        if final:
            self.final_tokens.append(tok)
        return tok

    def barrier(self):
        toks = [(self.esem[e], self.count[e]) for e in self.esem if self.count[e] > 0]
        toks += [(s, c) for (s, c) in self.dsem.values()]
        for e in ENGS:
            for s, v in toks:
                if self.pending[e].get(s, 0) < v:
                    self.pending[e][s] = v

    def emit(self):
        nc = self.nc
        with nc.Block() as block:
            def run(e, eng):
                for waits, fn, tok, inc in self.ops[e]:
                    for s, v in waits:
                        eng.wait_ge(self.sems[s], v)
                    fn(eng).then_inc(self.sems[tok[0]], inc)
                if e == "sp":
                    fin = {}
                    for s, v in self.final_tokens:
                        fin[s] = max(fin.get(s, 0), v)
                    for s, v in fin.items():
                        eng.wait_ge(self.sems[s], v)

            @block.tensor
            def _(eng):
                run("pe", eng)

            @block.scalar
            def _(eng):
                run("act", eng)

            @block.vector
            def _(eng):
                run("dve", eng)

            @block.gpsimd
            def _(eng):
                run("pool", eng)

            @block.sync
            def _(eng):
                run("sp", eng)

    def close(self):
        for cm in reversed(self._ctx):
            cm.__exit__(None, None, None)
        self._ctx = []


F32 = mybir.dt.float32
BF16 = mybir.dt.bfloat16
I32 = mybir.dt.int32
AF = mybir.ActivationFunctionType
ALU = mybir.AluOpType
AX = mybir.AxisListType

D = 1024
EPS = 1e-6
GH, GDK, GDV = 4, 128, 256
GIN = 3088
SH, SDH = 16, 64


class Bld:
    def __init__(self, nc):
        self.nc = nc
        self.p = Prog(nc)
        self._cms = []
        self.t = {}

    def sb(self, name, shape, dt):
        cm = self.nc.sbuf_tensor(name, list(shape), dt)
        h = cm.__enter__()
        self._cms.append(cm)
        self.t[name] = h
        return h

    def ps(self, name, shape, dt=F32):
        cm = self.nc.psum_tensor(name, list(shape), dt)
        h = cm.__enter__()
        self._cms.append(cm)
        self.t[name] = h
        return h

    def release_to(self, mark):
        while len(self._cms) > mark:
            self._cms.pop().__exit__(None, None, None)

    def finish(self):
        self.p.emit()
        self.p.close()
        for cm in reversed(self._cms):
            cm.__exit__(None, None, None)

    def mm(self, out, lhsT, rhs, r, w, start=True, stop=True):
        return self.p.op("pe", lambda e: e.matmul(out, lhsT, rhs, start=start, stop=stop), r, w)

    def tr(self, out, in_, ident, r, w):
        return self.p.op("pe", lambda e: e.transpose(out, in_, ident), r, w)

    def act(self, out, in_, func, r, w, bias=None, scale=None, accum=None):
        kw = {}
        if bias is not None:
            kw["bias"] = bias
        if scale is not None:
            kw["scale"] = scale
        if accum is not None:
            kw["accum_out"] = accum
        return self.p.op("act", lambda e: e.activation(out, in_, func, **kw), r, w)

    def tt(self, out, in0, in1, op, r, w, eng="dve"):
        return self.p.op(eng, lambda e: e.tensor_tensor(out, in0, in1, op), r, w)

    def ts(self, out, in0, s1, s2, op0, op1, r, w, eng="dve"):
        if op1 is None:
            return self.p.op(eng, lambda e: e.tensor_scalar(out, in0, s1, None, op0), r, w)
        return self.p.op(eng, lambda e: e.tensor_scalar(out, in0, s1, s2, op0, op1), r, w)

    def stt(self, out, in0, scalar, in1, op0, op1, r, w):
        return self.p.op("dve", lambda e: e.scalar_tensor_tensor(out, in0, scalar, in1, op0, op1), r, w)

    def cp(self, out, in_, r, w, eng="dve"):
        if eng == "act":
            return self.p.op("act", lambda e: e.copy(out, in_), r, w)
        return self.p.op(eng, lambda e: e.tensor_copy(out, in_), r, w)

    def red(self, out, in_, r, w):
        return self.p.op("dve", lambda e: e.tensor_reduce(out, in_, AX.X, ALU.add), r, w)

    def mset(self, ap, val, w, eng="pool"):
        return self.p.op(eng, lambda e: e.memset(ap, val), (), w)

    def dma(self, out, in_, r, w, q="sp", final=False, slow=False):
        if slow:
            return self.p.dma(q, lambda e: e.dma_start(out=out, in_=in_, allow_slow_non_contiguous=True), r, w, final)
        return self.p.dma(q, lambda e: e.dma_start(out=out, in_=in_), r, w, final)

    def rstd(self, ssq, n, key):
        self.ts(ssq, ssq, 1.0 / n, EPS, ALU.mult, ALU.add, [key], [key])
        self.act(ssq, ssq, AF.Ln, [key], [key])
        self.act(ssq, ssq, AF.Exp, [key], [key], scale=-0.5)


def consts_np():
    c = {}
    c["ident_bf"] = np.eye(128, dtype=np.float32).astype(ml_dtypes.bfloat16)
    c["ident_f"] = np.eye(128, dtype=np.float32)
    u = np.triu(np.ones((128, 128), np.float32))
    c["u_f"] = u
    c["u4_f"] = np.tile(u, (1, 4))
    return c


def load_weights_l0(b, win, wup, ba, gain, wout, onorm):
    nc = b.nc
    Wa = b.sb("Wa", [128, 8, GIN], BF16)
    Wo = b.sb("Wo", [128, 8, D], BF16)
    stg = [b.t["wstg0"], b.t["wstg1"]]
    gcol = b.sb("gcol0", [128, 8], F32)
    b.dma(gcol[:], gain.rearrange("(c p) -> p c", p=128), [], ["gcol0"], slow=True)
    i = 0
    for c in range(8):
        for c0 in range(0, GIN, 1024):
            n = min(1024, GIN - c0)
            s = stg[i % 2]; k = "wstg%d" % (i % 2)
            b.dma(s[:, 0:n], win[c * 128:(c + 1) * 128, c0:c0 + n], [], [k], q="sp" if i % 2 == 0 else "pool")
            b.ts(Wa[:, c, c0:c0 + n], s[:, 0:n], gcol[:, c:c + 1], None, ALU.mult, None, [k, "gcol0"], ["Wa"],
                 eng="dve" if i % 2 == 0 else "pool")
            i += 1
    for c in range(8):
        s = stg[i % 2]; k = "wstg%d" % (i % 2)
        b.dma(s[:, 0:D], wout[c * 128:(c + 1) * 128, :], [], [k], q="sp" if i % 2 == 0 else "pool")
        b.cp(Wo[:, c, :], s[:, 0:D], [k], ["Wo"], eng="dve" if i % 2 == 0 else "pool")
        i += 1
    wupf = b.sb("wupf", [17, 512], F32)
    b.dma(wupf[0:16, :], wup, [], ["wupf"])
    b.dma(wupf[16:17, :], ba.rearrange("(o n) -> o n", o=1), [], ["wupf"])
    wupb = b.sb("wupb", [17, 512], BF16)
    b.cp(wupb[:], wupf[:], ["wupf"], ["wupb"])
    ong = b.sb("ong", [128, GH, GDV], F32)
    for h in range(GH):
        b.dma(ong[:, h, :], onorm.rearrange("(o n) -> o n", o=1).partition_broadcast(128), [], ["ong"], slow=True)
    return Wa, Wo, wupb, ong


def gla_tile(b, T, x_src, x1_dst, S, Sbf, W, cst, tag=""):
    Wa, Wo, wupb, ong = W
    t = b.t
    nc = b.nc
    ident_bf, ident_f, u_f, u4_f = cst
    xt, xn, hT = t["xt"], t["xn"], t["hT"]
    PA, PB, PC, PD = t["PA"], t["PB"], t["PC"], t["PD"]
    junk = t["junk"]
    b.dma(xt[0:T, :], x_src, [], ["xt"])
    b.act(junk[0:T, :], xt[0:T, :], AF.Square, ["xt"], ["junk", "ssq"], accum=t["ssq"][0:T, 0:1])
    b.rstd(t["ssq"][0:T, 0:1], D, "ssq")
    b.ts(xn[0:T, :], xt[0:T, :], t["ssq"][0:T, 0:1], None, ALU.mult, None, ["xt", "ssq"], ["xn"])
    PAb = PA[:].bitcast(BF16)
    for c in range(8):
        b.tr(PAb[:, c * 128:c * 128 + T], xn[0:T, c * 128:(c + 1) * 128], ident_bf[0:T, 0:T], ["xn", "ident_bf"], ["PA"])
    b.cp(hT[:, :, 0:T], PAb[:, 0:1024].rearrange("p (c t) -> p c t", c=8)[:, :, 0:T], ["PA"], ["hT"], eng="act")
    for j in range(8):
        for c in range(8):
            b.mm(PB[:, j * 128:j * 128 + T], Wa[:, c, j * 128:(j + 1) * 128], hT[:, c, 0:T],
                 ["Wa", "hT"], ["PB"], start=(c == 0), stop=(c == 7))
    for c in range(8):
        b.mm(PA[0:16, 512:512 + T], Wa[:, c, 3072:3088], hT[:, c, 0:T], ["Wa", "hT"], ["PA"],
             start=(c == 0), stop=(c == 7))
    acT = t["acT"]
    b.cp(acT[0:16, 0:T], PA[0:16, 512:512 + T], ["PA"], ["acT"])
    for n in range(2):
        for c in range(8):
            b.mm(PC[0:T, n * 512:(n + 1) * 512], hT[:, c, 0:T], Wa[:, c, 1024 + n * 512:1024 + (n + 1) * 512],
                 ["Wa", "hT"], ["PC"], start=(c == 0), stop=(c == 7))
    vbf = t["vbf"]
    b.cp(vbf[0:T, :], PC[0:T, :], ["PC"], ["vbf"], eng="act")
    for n in range(2):
        for c in range(8):
            b.mm(PD[0:T, n * 512:(n + 1) * 512], hT[:, c, 0:T], Wa[:, c, 2048 + n * 512:2048 + (n + 1) * 512],
                 ["Wa", "hT"], ["PD"], start=(c == 0), stop=(c == 7))
    sg = t["sg"]
    b.act(sg[0:T, :], PD[0:T, :], AF.Exp, ["PD"], ["sg"], scale=-1.0)
    b.ts(sg[0:T, :], sg[0:T, :], 1.0, None, ALU.add, None, ["sg"], ["sg"])
    b.p.op("dve", lambda e: e.reciprocal(sg[0:T, :], sg[0:T, :]), ["sg"], ["sg"])
    b.tt(sg[0:T, :], sg[0:T, :], PD[0:T, :], ALU.mult, ["sg", "PD"], ["sg"])
    b.mm(PA[0:T, 0:512], acT[0:17, 0:T], wupb[0:17, :], ["acT", "wupb"], ["PA"])
    spt = t["spt"]
    b.act(spt[0:T, :], PA[0:T, 0:512], AF.Exp, ["PA"], ["spt"], scale=-1.0)
    b.act(spt[0:T, :], spt[0:T, :], AF.Ln, ["spt", "one"], ["spt"], bias=t["one"][0:T, 0:1])
    for h in range(GH):
        b.mm(PA[:, h * 128:h * 128 + T], spt[0:T, h * 128:(h + 1) * 128], u_f[0:T, 0:T], ["spt", "u_f"], ["PA"])
    E1, E2, E3, nbl, dec = t["E1"], t["E2"], t["E3"], t["nbl"], t["dec"]
    PA4 = PA[:, 0:512].rearrange("p (h t) -> p h t", h=GH)
    b.act(E1[:].rearrange("p (h t) -> p h t", h=GH)[:, :, 0:T], PA4[:, :, 0:T], AF.Exp, ["PA"], ["E1"], scale=-1.0 / 16)
    b.act(E2[:].rearrange("p (h t) -> p h t", h=GH)[:, :, 0:T], PA4[:, :, 0:T], AF.Exp, ["PA"], ["E2"], scale=1.0 / 16)
    b.ts(nbl[:, :], PA4[:, :, T - 1], -1.0 / 16, None, ALU.mult, None, ["PA"], ["nbl"])
    for h in range(GH):
        b.act(E3[:, h * 128:h * 128 + T], PA[:, h * 128:h * 128 + T], AF.Exp, ["PA", "nbl"], ["E3"],
              scale=1.0 / 16, bias=nbl[:, h:h + 1])
    b.act(dec[:, :], nbl[:, :], AF.Exp, ["nbl"], ["dec"])
    qd, kd, kb = t["qd"], t["kd"], t["kb"]
    def v4(ap):
        return ap.rearrange("p (h t) -> p h t", h=GH)[:, :, 0:T]
    b.stt(v4(qd[:]), v4(PB[:, 0:512]), float(GDK) ** -0.5, v4(E1[:]), ALU.mult, ALU.mult, ["PB", "E1"], ["qd"])
    b.tt(v4(kd[:]), v4(PB[:, 512:1024]), v4(E2[:]), ALU.mult, ["PB", "E2"], ["kd"])
    b.tt(v4(kb[:]), v4(PB[:, 512:1024]), v4(E3[:]), ALU.mult, ["PB", "E3"], ["kb"])
    for h in range(GH):
        b.mm(PB[0:T, h * 128:h * 128 + T], kd[:, h * 128:h * 128 + T], qd[:, h * 128:h * 128 + T],
             ["kd", "qd"], ["PB"])
    attm = t["attm"]
    b.tt(v4(attm[0:T, :]), v4(PB[0:T, 0:512]), v4(u4_f[0:T, :]), ALU.mult, ["PB", "u4_f"], ["attm"])
    for h in range(GH):
        b.tr(PAb[0:T, h * 128:(h + 1) * 128], kb[:, h * 128:h * 128 + T], ident_bf[:, :], ["kb", "ident_bf"], ["PA"])
    kbt = t["kbt"]
    b.cp(kbt[0:T, :], PAb[0:T, 0:512], ["PA"], ["kbt"])
    for h in range(GH):
        b.mm(PC[0:T, h * 256:(h + 1) * 256], attm[0:T, h * 128:h * 128 + T], vbf[0:T, h * 256:(h + 1) * 256],
             ["attm", "vbf"], ["PC"], start=True, stop=False)
        b.mm(PC[0:T, h * 256:(h + 1) * 256], qd[:, h * 128:h * 128 + T], Sbf[:, h * 256:(h + 1) * 256],
             ["qd", "Sbf"], ["PC"], start=False, stop=True)
    for h in range(GH):
        b.mm(PD[:, h * 256:(h + 1) * 256], kbt[0:T, h * 128:(h + 1) * 128], vbf[0:T, h * 256:(h + 1) * 256],
             ["kbt", "vbf"], ["PD"])
    for h in range(GH):
        b.stt(S[:, h * 256:(h + 1) * 256], S[:, h * 256:(h + 1) * 256], dec[:, h:h + 1],
              PD[:, h * 256:(h + 1) * 256], ALU.mult, ALU.add, ["S", "dec", "PD"], ["S"])
    b.cp(Sbf[:], S[:], ["S"], ["Sbf"], eng="act")
    sso = t["sso"]
    for h in range(GH):
        b.act(junk[0:T, 0:256], PC[0:T, h * 256:(h + 1) * 256], AF.Square, ["PC"], ["junk", "sso"],
              accum=sso[0:T, h:h + 1])
    b.rstd(sso[0:T, 0:GH], GDV, "sso")
    on = t["on"]
    for h in range(GH):
        b.stt(on[0:T, h * 256:(h + 1) * 256], PC[0:T, h * 256:(h + 1) * 256], sso[0:T, h:h + 1],
              ong[0:T, h, :], ALU.mult, ALU.mult, ["PC", "sso", "ong"], ["on"])
    og = t["og"]
    b.tt(og[0:T, :], on[0:T, :], sg[0:T, :], ALU.mult, ["on", "sg"], ["og"])
    for c in range(8):
        b.tr(PAb[:, c * 128:c * 128 + T], og[0:T, c * 128:(c + 1) * 128], ident_bf[0:T, 0:T], ["og", "ident_bf"], ["PA"])
    ogT = t["ogT"]
    b.cp(ogT[:, :, 0:T], PAb[:, 0:1024].rearrange("p (c t) -> p c t", c=8)[:, :, 0:T], ["PA"], ["ogT"], eng="act")
    for n in range(2):
        for c in range(8):
            b.mm(PD[0:T, n * 512:(n + 1) * 512], ogT[:, c, 0:T], Wo[:, c, n * 512:(n + 1) * 512],
                 ["ogT", "Wo"], ["PD"], start=(c == 0), stop=(c == 7))
    x1t = t["x1t"]
    b.tt(x1t[0:T, :], PD[0:T, :], xt[0:T, :], ALU.add, ["PD", "xt"], ["x1t"])
    if x1_dst is not None:
        b.dma(x1_dst, x1t[0:T, :], ["x1t"], ["x1dram" + tag], q="pool")
    return x1t


def alloc_common(b, gla=True):
    b.sb("xt", [128, D], F32)
    b.sb("xn", [128, D], BF16)
    b.sb("hT", [128, 8, 128], BF16)
    b.sb("junk", [128, D], BF16)
    b.sb("ssq", [128, 1], F32)
    b.sb("sg", [128, D], F32)
    b.sb("one", [128, 1], F32)
    b.sb("on", [128, D], F32)
    if gla:
        b.sb("acT", [17, 128], BF16)
        b.sb("vbf", [128, D], BF16)
        b.sb("spt", [128, 512], F32)
        b.sb("E1", [128, 512], F32)
        b.sb("E2", [128, 512], F32)
        b.sb("E3", [128, 512], F32)
        b.sb("nbl", [128, 4], F32)
        b.sb("dec", [128, 4], F32)
        b.sb("qd", [128, 512], BF16)
        b.sb("kd", [128, 512], BF16)
        b.sb("kb", [128, 512], BF16)
        b.sb("attm", [128, 512], BF16)
        b.sb("kbt", [128, 512], BF16)
        b.sb("sso", [128, 4], F32)
        b.sb("og", [128, D], BF16)
        b.sb("ogT", [128, 8, 128], BF16)
    b.sb("x1t", [128, D], F32)
    b.ps("PA", [128, 1024], F32)
    b.ps("PB", [128, 1024], F32)
    b.ps("PC", [128, 1024], F32)
    b.ps("PD", [128, 1024], F32)
    b.mset(b.t["one"][:], 1.0, ["one"])
    if gla:
        b.mset(b.t["acT"][:], 1.0, ["acT"])


def load_consts(b, d_ident_bf, d_ident_f, d_u, d_u4):
    ib = b.sb("ident_bf", [128, 128], BF16)
    i_f = b.sb("ident_f", [128, 128], F32)
    u = b.sb("u_f", [128, 128], F32)
    u4 = b.sb("u4_f", [128, 512], F32)
    b.dma(ib[:], d_ident_bf, [], ["ident_bf"])
    b.dma(i_f[:], d_ident_f, [], ["ident_f"])
    b.dma(u[:], d_u, [], ["u_f"])
    b.dma(u4[:], d_u4, [], ["u4_f"])
    return ib, i_f, u, u4


def consts_sb_np(j, ns):
    nk = 4 * ns
    kpos = (np.arange(nk)[:, None] * 128 + np.arange(128)[None, :])
    qpos = ((ns * np.arange(4)[:, None] + j) * 128 + np.arange(128)[None, :]).reshape(-1)
    m = (kpos[:, :, None] < qpos[None, None, :]).astype(np.float32)
    c = {}
    c["mask"] = np.ascontiguousarray(m.transpose(1, 0, 2)).astype(ml_dtypes.bfloat16)
    tl = np.tril(np.ones((128, 128), np.float32))
    c["tri"] = tl.astype(ml_dtypes.bfloat16)
    c["omt"] = (1.0 - tl).astype(ml_dtypes.bfloat16)
    return c


def load_weights_l1(b, winb, gain1, woutb, qn, kn, sbias, which="all"):
    if which == "kv":
        segs = [(1024, 2048)]; cb = {"k": 0, "v": 1024}
    elif which == "qg":
        segs = [(0, 1024), (3072, 1024)]; cb = {"q": 0, "g": 1024}
    else:
        segs = [(0, 2048), (2048, 2048)]; cb = {"q": 0, "k": 1024, "v": 2048, "g": 3072}
    ncol = sum(n for _, n in segs)
    W1 = b.sb("W1" + which, [128, 8, ncol], BF16)
    wk = "W1" + which
    gcol = b.sb("gcol1" + which, [128, 8], F32)
    b.dma(gcol[:], gain1.rearrange("(c p) -> p c", p=128), [], ["gcol1" + which], slow=True)
    stg = [b.t["wstg0"], b.t["wstg1"]]
    i = 0
    for c in range(8):
        lo = 0
        for (c0, n) in segs:
            for s0 in range(0, n, 1024):
                s = stg[i % 2]; k = "wstg%d" % (i % 2)
                b.dma(s[:, 0:1024], winb[c * 128:(c + 1) * 128, c0 + s0:c0 + s0 + 1024], [], [k],
                      q="sp" if i % 2 == 0 else "pool")
                b.ts(W1[:, c, lo + s0:lo + s0 + 1024], s[:, 0:1024], gcol[:, c:c + 1], None, ALU.mult, None,
                     [k, "gcol1" + which], [wk], eng="dve" if i % 2 == 0 else "pool")
                i += 1
            lo += n
    Wo1 = None
    if which != "kv":
        Wo1 = b.sb("Wo1", [128, 8, D], BF16)
        for c in range(8):
            s = stg[i % 2]; k = "wstg%d" % (i % 2)
            b.dma(s[:, 0:D], woutb[c * 128:(c + 1) * 128, :], [], [k], q="sp" if i % 2 == 0 else "pool")
            b.cp(Wo1[:, c, :], s[:, 0:D], [k], ["Wo1"], eng="dve" if i % 2 == 0 else "pool")
            i += 1
    qng = b.sb("qng" + which, [128, SH, SDH], F32)
    kng = b.sb("kng" + which, [128, SH, SDH], F32)
    for h in range(SH):
        b.dma(qng[:, h, :], qn.rearrange("(o n) -> o n", o=1).partition_broadcast(128), [], ["qng"], slow=True)
        b.dma(kng[:, h, :], kn.rearrange("(o n) -> o n", o=1).partition_broadcast(128), [], ["kng"], slow=True,
              q="pool")
    b.ts(qng[:], qng[:], float(SDH) ** -0.5, None, ALU.mult, None, ["qng"], ["qng"])
    biasb = b.sb("biasb" + which, [128, SH], F32)
    b.dma(biasb[:], sbias.rearrange("(o n) -> o n", o=1).partition_broadcast(128), [], ["biasb"], slow=True)
    return (W1, wk, cb), Wo1, qng, kng, biasb


def l1_norm_T(b, T, xsb, cst, xkey="x1t"):
    t = b.t
    ident_bf = cst[0]
    junk, ssq, xn, hT, PA = t["junk"], t["ssq"], t["xn"], t["hT"], t["PA"]
    b.act(junk[0:T, :], xsb, AF.Square, [xkey], ["junk", "ssq"], accum=ssq[0:T, 0:1])
    b.rstd(ssq[0:T, 0:1], D, "ssq")
    b.ts(xn[0:T, :], xsb, ssq[0:T, 0:1], None, ALU.mult, None, [xkey, "ssq"], ["xn"])
    PAb = PA[:].bitcast(BF16)
    for c in range(8):
        b.tr(PAb[:, c * 128:c * 128 + T], xn[0:T, c * 128:(c + 1) * 128], ident_bf[0:T, 0:T], ["xn", "ident_bf"], ["PA"])
    b.cp(hT[:, :, 0:T], PAb[:, 0:1024].rearrange("p (c t) -> p c t", c=8)[:, :, 0:T], ["PA"], ["hT"], eng="act")
    return hT


def proj_tok(b, T, PS, pskey, W1k, which):
    hT = b.t["hT"]
    W1, wkey, cb = W1k
    col0 = cb[which]
    for n in range(2):
        for c in range(8):
            b.mm(PS[0:T, n * 512:(n + 1) * 512], hT[:, c, 0:T], W1[:, c, col0 + n * 512:col0 + (n + 1) * 512],
                 [wkey, "hT"], [pskey], start=(c == 0), stop=(c == 7))


def headnorm(b, T, PS, pskey, gain_t, gkey, out_f, okey):
    t = b.t
    sq, ssh = t["on"], t["ssh"]
    b.act(sq[0:T, :], PS[0:T, :], AF.Square, [pskey], ["on"])
    b.red(ssh[0:T, :], sq[0:T, :].rearrange("p (h d) -> p h d", h=SH), ["on"], ["ssh"])
    b.rstd(ssh[0:T, :], SDH, "ssh")
    o3 = out_f[0:T, :].rearrange("p (h d) -> p h d", h=SH)
    b.tt(o3, PS[0:T, :].rearrange("p (h d) -> p h d", h=SH),
         ssh[0:T, :].unsqueeze(2).to_broadcast([T, SH, SDH]), ALU.mult, [pskey, "ssh"], [okey])
    b.tt(o3, o3, gain_t[0:T, :, :], ALU.mult, [okey, gkey], [okey])


def to_pairT(b, T, src_bf, skey, dst, dkey, col0, cst):
    PA = b.t["PA"]
    PAb = PA[:].bitcast(BF16)
    ident_bf = cst[0]
    for c in range(8):
        b.tr(PAb[:, c * 128:c * 128 + T], src_bf[0:T, c * 128:(c + 1) * 128], ident_bf[0:T, 0:T],
             [skey, "ident_bf"], ["PA"])
    b.cp(dst[:, :, col0:col0 + T], PAb[:, 0:1024].rearrange("p (c t) -> p c t", c=8)[:, :, 0:T], ["PA"], [dkey])


def l1_kv_tile(b, T, xsb, W1, L1W, cst, k_out, v_out, KT_d, V_d, tok0, tag, xkey="x1t", ktkey="KT_d", vkey="V_d"):
    t = b.t
    W1_, Wo1, qng, kng, biasb = L1W
    PC, PD = t["PC"], t["PD"]
    l1_norm_T(b, T, xsb, cst, xkey)
    proj_tok(b, T, PC, "PC", W1, "k")
    proj_tok(b, T, PD, "PD", W1, "v")
    kf, vf, kbf, vb2, ktt = t["kf"], t["vf"], t["kbf"], t["vb2"], t["ktt"]
    headnorm(b, T, PC, "PC", kng, "kng", kf, "kf")
    if k_out is not None:
        b.dma(k_out, kf[0:T, :], ["kf"], ["kout"], q="pool", final=True)
    b.cp(kbf[0:T, :], kf[0:T, :], ["kf"], ["kbf"], eng="act")
    to_pairT(b, T, kbf, "kbf", ktt, "ktt", 0, cst)
    b.dma(KT_d[:, :, tok0:tok0 + T].rearrange("c p t -> p c t"), ktt[:, :, 0:T], ["ktt"], [ktkey], slow=True)
    b.cp(vf[0:T, :], PD[0:T, :], ["PD"], ["vf"], eng="act")
    if v_out is not None:
        b.dma(v_out, vf[0:T, :], ["vf"], ["vout"], q="pool", final=True)
    b.cp(vb2[0:T, :], vf[0:T, :], ["vf"], ["vb2"])
    b.dma(V_d[tok0:tok0 + T, :], vb2[0:T, :], ["vb2"], [vkey])


def l1_qg_tile(b, T, xsb, W1, L1W, cst, qT, sgT, col0, xkey="x1own"):
    t = b.t
    W1_, Wo1, qng, kng, biasb = L1W
    PC, PD = t["PC"], t["PD"]
    l1_norm_T(b, T, xsb, cst, xkey)
    proj_tok(b, T, PC, "PC", W1, "q")
    proj_tok(b, T, PD, "PD", W1, "g")
    kf, kbf, sg, vb2 = t["kf"], t["kbf"], t["sg"], t["vb2"]
    headnorm(b, T, PC, "PC", qng, "qng", kf, "kf")
    b.cp(kbf[0:T, :], kf[0:T, :], ["kf"], ["kbf"], eng="pool")
    to_pairT(b, T, kbf, "kbf", qT, "qT", col0, cst)
    b.act(sg[0:T, :], PD[0:T, :], AF.Exp, ["PD"], ["sg"], scale=-1.0)
    b.ts(sg[0:T, :], sg[0:T, :], 1.0, None, ALU.add, None, ["sg"], ["sg"])
    b.p.op("dve", lambda e: e.reciprocal(sg[0:T, :], sg[0:T, :]), ["sg"], ["sg"])
    b.tt(vb2[0:T, :], sg[0:T, :], PD[0:T, :], ALU.mult, ["sg", "PD"], ["vb2"])
    to_pairT(b, T, vb2, "vb2", sgT, "sgT", col0, cst)


def sb_unit(b, hp, KTb, Vb, kcol, TQ, qT, mask_ap, first, last, Z, ACC, O, biasb, h, tri, omt):
    t = b.t
    e_t, L_t, P_t, A_t = t["e_t"], t["L_t"], t["P_t"], t["A_t"]
    p0 = 64 * hp
    b.mm(Z[:, 0:TQ], KTb[p0:p0 + 64, :], qT[p0:p0 + 64, 0:TQ], ["KTs", "qT"], ["Z"])
    b.act(e_t[:, 0:TQ], Z[:, 0:TQ], AF.Exp, ["Z", "biasb"], ["e_t"], bias=biasb[:, h:h + 1])
    if mask_ap is not None:
        b.tt(e_t[:, 0:TQ], e_t[:, 0:TQ], mask_ap, ALU.mult, ["e_t", "mask", "masks"], ["e_t"])
    b.act(L_t[:, 0:TQ], e_t[:, 0:TQ], AF.Ln, ["e_t", "one"], ["L_t"], bias=t["one"][:, 0:1])
    b.mm(ACC[:, 0:TQ], tri[:, :], L_t[:, 0:TQ], ["L_t", "tri"], ["ACC"], start=first, stop=False)
    b.act(P_t[:, 0:TQ], ACC[:, 0:TQ], AF.Exp, ["ACC"], ["P_t"], scale=-1.0)
    b.mm(ACC[:, 0:TQ], omt[:, :], L_t[:, 0:TQ], ["L_t", "omt"], ["ACC"], start=False, stop=last)
    b.tt(A_t[:, 0:TQ], e_t[:, 0:TQ], P_t[:, 0:TQ], ALU.mult, ["e_t", "P_t"], ["A_t"])
    b.mm(O[:, 0:TQ], Vb, A_t[:, 0:TQ], ["Vs", "A_t"], ["O"], start=first, stop=last)


def alloc_l1(b, nkeys_max, TQ=512):
    b.sb("ssh", [128, SH], F32)
    b.sb("kf", [128, D], F32)
    b.sb("vf", [128, D], F32)
    b.sb("kbf", [128, D], BF16)
    b.sb("vb2", [128, D], BF16)
    b.sb("ktt", [128, 8, 128], BF16)
    b.sb("e_t", [128, TQ], F32)
    b.sb("L_t", [128, TQ], BF16)
    b.sb("P_t", [128, TQ], F32)
    b.sb("A_t", [128, TQ], BF16)
    b.sb("KTs", [128, nkeys_max], BF16)
    b.sb("Vs", [128, nkeys_max // 128, 128], BF16)
    b.sb("qT", [128, 8, TQ], BF16)
    b.sb("sgT", [128, 8, TQ], BF16)
    b.sb("ogT", [128, 8, TQ], BF16) if "ogT" not in b.t else None
    b.sb("ogT1", [128, 8, TQ], BF16)
    b.sb("x1own", [128, TQ // 128, D], F32)
    b.sb("yt", [128, D], F32)


def sb_group(b, L1W, cst, sbc, nkb, KT_d, V_d, qT, sgT, TQ, y_dst_tiles, x1own, ktkey="KT_d", vkey="V_d"):
    t = b.t
    W1_, Wo1, qng, kng, biasb = L1W
    maskt, tri, omt = sbc
    nmask = maskt.shape[1]
    KTs, Vs, ogT1 = t["KTs"], t["Vs"], t["ogT1"]
    PA, PB = t["PA"], t["PB"]
    Z, ACC, O = PB[:, 0:512], PB[:, 512:1024], PA[:, 512:1024]
    for pr in range(8):
        b.dma(KTs[:, 0:nkb * 128], KT_d[pr, :, 0:nkb * 128], [ktkey], ["KTs"])
        b.dma(Vs[:, 0:nkb, :], V_d[0:nkb * 128, pr * 128:(pr + 1) * 128].rearrange("(k p) c -> p k c", p=128),
              [vkey], ["Vs"], q="pool")
        for hp in range(2):
            h = 2 * pr + hp
            for i, kb in enumerate(range(nkb - 1, -1, -1)):
                mrel = kb - (nkb - nmask)
                m_ap = maskt[:, mrel, 0:TQ] if mrel >= 0 else None
                sb_unit(b, hp, KTs[:, kb * 128:(kb + 1) * 128], Vs[:, kb, :], kb, TQ, qT[:, pr, :], m_ap,
                        i == 0, kb == 0, Z, ACC, O, biasb, h, tri, omt)
            p0 = 64 * hp
            b.tt(ogT1[p0:p0 + 64, pr, 0:TQ], O[p0:p0 + 64, 0:TQ], sgT[p0:p0 + 64, pr, 0:TQ], ALU.mult,
                 ["O", "sgT"], ["ogT1"])
    PD = t["PD"]
    yt = t["yt"]
    Tt = min(128, TQ)
    for ti in range(max(1, TQ // 128)):
        for n in range(2):
            for c in range(8):
                b.mm(PD[0:Tt, n * 512:(n + 1) * 512], ogT1[:, c, ti * 128:ti * 128 + Tt], Wo1[:, c, n * 512:(n + 1) * 512],
                     ["ogT1", "Wo1"], ["PD"], start=(c == 0), stop=(c == 7))
        b.tt(yt[0:Tt, :], PD[0:Tt, :], x1own[0:Tt, ti, :], ALU.add, ["PD", "x1own"], ["yt"])
        b.dma(y_dst_tiles[ti], yt[0:Tt, :], ["yt"], ["ydst"], q="pool", final=True)


def l1_kv_rows(b, T, xsb, W1, L1W, cst, k_out, v_out, tag, xkey):
    t = b.t
    _, Wo1, qng, kng, biasb = L1W
    PC, PD = t["PC"], t["PD"]
    l1_norm_T(b, T, xsb, cst, xkey)
    proj_tok(b, T, PC, "PC", W1, "k")
    proj_tok(b, T, PD, "PD", W1, "v")
    kf, vf = t["kf"], t["vf"]
    headnorm(b, T, PC, "PC", kng, "kng", kf, "kf")
    b.dma(k_out, kf[0:T, :], ["kf"], ["kout"], q="pool", final=True)
    b.cp(vf[0:T, :], PD[0:T, :], ["PD"], ["vf"], eng="act")
    b.dma(v_out, vf[0:T, :], ["vf"], ["vout"], q="pool", final=True)


def build_program(nc, cfg):
    NTOK, NS, NSAMP, NPG, NPHYS = cfg["NTOK"], cfg["NS"], cfg["NSAMP"], cfg["NPG"], cfg["NPHYS"]
    NOWN = NTOK // NS
    NG = NOWN // 512
    NT = NTOK // 128
    NKS = (NPG + 1) * 128
    def din(n, s, dt=F32): return nc.dram_tensor(n, list(s), dt, kind="ExternalInput").ap()
    def dout(n, s, dt=F32): return nc.dram_tensor(n, list(s), dt, kind="ExternalOutput").ap()
    xb = din("xb", [NTOK, D]); own_rows = din("own_rows", [128, NOWN // 128], I32)
    xs = din("xs", [NSAMP * 8, D]); state = din("state", [NSAMP, GH, GDK, GDV])
    ck = din("ck", [NPHYS * 128, D]); cv = din("cv", [NPHYS * 128, D]); pt = din("pt", [NSAMP * NPG], I32)
    gain = din("gain", [2, D]); win_a = din("win_a", [D, GIN]); wup = din("wup", [16, 512]); ba = din("ba", [512])
    onorm = din("onorm", [256]); wout_a = din("wout_a", [D, D]); win_b = din("win_b", [D, 4096])
    qn = din("qn", [64]); kn = din("kn", [64]); sbias = din("sbias", [16]); wout_b = din("wout_b", [D, D])
    c_ib = din("c_ib", [128, 128], BF16); c_if = din("c_if", [128, 128]); c_u = din("c_u", [128, 128]); c_u4 = din("c_u4", [128, 512])
    NM = 4 * NS
    c_mask = din("c_mask", [128, NM, 512], BF16); c_tri = din("c_tri", [128, 128], BF16); c_omt = din("c_omt", [128, 128], BF16)
    c_masknew = din("c_masknew", [128, 128], BF16); c_iota = din("c_iota", [128, 1])
    y_own = dout("y_own", [NOWN, D]); ys = dout("ys", [NSAMP * 8, D])
    st_p = dout("st_p", [GH, GDK, GDV]); st_s = dout("st_s", [NSAMP, GH, GDK, GDV])
    k_all = dout("k_all", [NTOK, D]); v_all = dout("v_all", [NTOK, D])
    k_s = dout("k_s", [NSAMP * 8, D]); v_s = dout("v_s", [NSAMP * 8, D])
    x1_d = nc.dram_tensor("x1_d", [NTOK, D], F32).ap()
    xs1_d = nc.dram_tensor("xs1_d", [NSAMP * 8, D], F32).ap()
    KT_d = nc.dram_tensor("KT_d", [8, 128, NTOK], BF16).ap()
    V_d = nc.dram_tensor("V_d", [NTOK, D], BF16).ap()
    KTs_d = [nc.dram_tensor("KTs_d%d" % s, [8, 128, NKS], BF16).ap() for s in range(NSAMP)]
    Vs_d = [nc.dram_tensor("Vs_d%d" % s, [NKS, D], BF16).ap() for s in range(NSAMP)]

    b = Bld(nc)
    alloc_common(b, gla=False)
    b.sb("wstg0", [128, 1024], F32); b.sb("wstg1", [128, 1024], F32)
    b.sb("ssh", [128, SH], F32); b.sb("kf", [128, D], F32); b.sb("vf", [128, D], F32)
    b.sb("kbf", [128, D], BF16); b.sb("vb2", [128, D], BF16); b.sb("ktt", [128, 8, 128], BF16)
    cst = load_consts(b, c_ib, c_if, c_u, c_u4)
    zt = b.sb("zt", [128, D], BF16)
    b.mset(zt[:], 0.0, ["zt"])
    mark = len(b._cms)
    alloc_gla(b)
    S = b.sb("S", [128, 1024], F32); Sbf = b.sb("Sbf", [128, 1024], BF16)
    W = load_weights_l0(b, win_a, wup, ba, gain[0, :], wout_a, onorm)
    L1kv = load_weights_l1(b, win_b, gain[1, :], wout_b, qn, kn, sbias, which="kv")
    b.mset(S[:], 0.0, ["S"]); b.mset(Sbf[:], 0.0, ["Sbf"])
    for i in range(NT):
        x1t = gla_tile(b, 128, xb[i * 128:(i + 1) * 128, :], x1_d[i * 128:(i + 1) * 128, :], S, Sbf, W, cst, tag="")
        l1_kv_tile(b, 128, x1t[:, :], L1kv[0], L1kv, cst, k_all[i * 128:(i + 1) * 128, :], v_all[i * 128:(i + 1) * 128, :], KT_d, V_d, i * 128, "", xkey="x1t")
    b.dma(st_p.rearrange("h d v -> d h v"), S[:].rearrange("p (h v) -> p h v", h=GH), ["S"], ["st_p"], final=True)
    for s in range(NSAMP):
        b.dma(S[:].rearrange("p (h v) -> p h v", h=GH), state[s].rearrange("h d v -> d h v"), [], ["S"])
        b.cp(Sbf[:], S[:], ["S"], ["Sbf"], eng="act")
        x1t = gla_tile(b, 8, xs[s * 8:(s + 1) * 8, :], xs1_d[s * 8:(s + 1) * 8, :], S, Sbf, W, cst, tag="s")
        b.dma(st_s[s].rearrange("h d v -> d h v"), S[:].rearrange("p (h v) -> p h v", h=GH), ["S"], ["st_s"], final=True)
        b.dma(KTs_d[s][:, :, NPG * 128:NKS].rearrange("c p t -> p c t"),
              zt[:].rearrange("p (c t) -> p c t", c=8), ["zt"], ["KTs_dS"], slow=True)
        b.dma(Vs_d[s][NPG * 128:NKS, :], zt[:], ["zt"], ["Vs_dS"])
        l1_kv_tile_s(b, 8, x1t[0:8, :], L1kv[0], L1kv, cst, k_s[s * 8:(s + 1) * 8, :], v_s[s * 8:(s + 1) * 8, :],
                     KTs_d[s], Vs_d[s], NPG * 128, "s%d" % s)
    b.p.barrier()
    b.release_to(mark)
    TQ = 512
    b.sb("qT", [128, 8, TQ], BF16); b.sb("sgT", [128, 8, TQ], BF16); b.sb("ogT1", [128, 8, TQ], BF16)
    b.sb("x1own", [128, TQ // 128, D], F32)
    tri = b.sb("tri", [128, 128], BF16); omt = b.sb("omt", [128, 128], BF16)
    b.dma(tri[:], c_tri, [], ["tri"]); b.dma(omt[:], c_omt, [], ["omt"])
    L1qg = load_weights_l1(b, win_b, gain[1, :], wout_b, qn, kn, sbias, which="qg")
    biasb = L1qg[4]
    x1own = b.t["x1own"]
    orow = b.sb("orow", [128, NOWN // 128], I32)
    b.dma(orow[:], own_rows, [], ["orow"])
    mark2 = len(b._cms)
    b.sb("e_t", [128, 128], F32); b.sb("L_t", [128, 128], BF16); b.sb("P_t", [128, 128], F32); b.sb("A_t", [128, 128], BF16)
    masknew = b.sb("masknew", [128, 128], BF16)
    b.dma(masknew[:], c_masknew, [], ["masknew"])
    biasfull = b.sb("biasfull", [128, SH, 8], F32)
    b.cp(biasfull[:], biasb[:, :].unsqueeze(2).to_broadcast([128, SH, 8]), ["biasb"], ["biasfull"])
    biasfull2 = biasfull[:].rearrange("p h q -> p (h q)")
    Qbd = b.sb("Qbd", [128, 8, 16], BF16)
    b.mset(Qbd[:], 0.0, ["Qbd"])
    for par in range(2):
        b.sb("kpg%d" % par, [128, D], F32); b.sb("vpg%d" % par, [128, D], F32)
        b.sb("kbfp%d" % par, [128, D], BF16); b.sb("vbp%d" % par, [128, D], BF16); b.sb("kttp%d" % par, [128, 8, 128], BF16)
    ptb = b.sb("ptb", [128, NSAMP * NPG], I32); ptf = b.sb("ptf", [128, NSAMP * NPG], F32)
    idx = b.sb("idx", [128, NSAMP * NPG], I32); iot = b.sb("iot", [128, 1], F32)
    b.dma(ptb[:], pt.rearrange("(o n) -> o n", o=1).partition_broadcast(128), [], ["ptb"], slow=True)
    b.dma(iot[:], c_iota, [], ["iot"])
    b.cp(ptf[:], ptb[:], ["ptb"], ["ptf"])
    b.ts(ptf[:], ptf[:], 128.0, iot[:, 0:1], ALU.mult, ALU.add, ["ptf", "iot"], ["ptf"])
    b.cp(idx[:], ptf[:], ["ptf"], ["idx"])
    for s in range(NSAMP):
        b.dma(x1own[0:8, 0, :], xs1_d[s * 8:(s + 1) * 8, :], ["x1drams"], ["x1own"])
        l1_qg_tile(b, 8, x1own[0:8, 0, :], L1qg[0], L1qg, cst, b.t["qT"], b.t["sgT"], 0)
        sample_attn(b, s, L1qg, cst, (masknew, tri, omt, biasfull2), NPG, ck, cv, idx,
                    KTs_d[s][:, :, NPG * 128:NKS], Vs_d[s][NPG * 128:NKS, :], b.t["qT"], b.t["sgT"],
                    ys[s * 8:(s + 1) * 8, :], x1own, zt)
    b.p.barrier()
    b.release_to(mark2)
    b.sb("KTs", [128, NTOK], BF16); b.sb("Vs", [128, NTOK // 128, 128], BF16)
    for nm_ in ("pe00", "pe01", "pe10", "pe11", "pL00", "pL01", "pL10", "pL11", "pP0", "pP1", "pA0", "pA1"):
        b.sb(nm_, [128, TQ], BF16)
    maskt = b.sb("mask", [128, NM, 512], BF16)
    b.dma(maskt[:], c_mask, [], ["mask"])
    for g in range(NG):
        for ti in range(4):
            lt = g * 4 + ti
            b.p.dma("pool", lambda e, lt=lt, ti=ti: e.indirect_dma_start(
                out=x1own[:, ti, :], out_offset=None, in_=x1_d[:, :],
                in_offset=bass.IndirectOffsetOnAxis(ap=orow[:, lt:lt + 1], axis=0)), ["orow", "x1dram"], ["x1own"])
            l1_qg_tile(b, 128, x1own[:, ti, :], L1qg[0], L1qg, cst, b.t["qT"], b.t["sgT"], ti * 128)
        b.p.barrier()
        nkb = 4 * NS * (g + 1)
        sb_group3(b, L1qg, cst, (maskt, tri, omt), nkb, KT_d, V_d, b.t["qT"], b.t["sgT"], 512,
                  [y_own[(g * 4 + ti) * 128:(g * 4 + ti + 1) * 128, :] for ti in range(4)], x1own)
        b.p.barrier()
    b.finish()
    return b


def l1_kv_tile_s(b, T, xsb, W1, L1W, cst, k_out, v_out, KT_d, V_d, tok0, tag):
    l1_kv_tile(b, T, xsb, W1, L1W, cst, k_out, v_out, KT_d, V_d, tok0, tag, xkey="x1t",
               ktkey="KTs_dS", vkey="Vs_dS")


def alloc_l1b(b, nkeys_max, TQ=512):
    b.sb("e_t", [128, TQ], F32)
    b.sb("L_t", [128, TQ], BF16)
    b.sb("P_t", [128, TQ], F32)
    b.sb("A_t", [128, TQ], BF16)
    b.sb("KTs", [128, nkeys_max], BF16)
    b.sb("Vs", [128, nkeys_max // 128, 128], BF16)
    b.sb("qT", [128, 8, TQ], BF16)
    b.sb("sgT", [128, 8, TQ], BF16)
    b.sb("ogT1", [128, 8, TQ], BF16)
    b.sb("x1own", [128, TQ // 128, D], F32)
    b.sb("yt", [128, D], F32)


def alloc_gla(b):
    b.sb("acT", [17, 128], BF16)
    b.sb("vbf", [128, D], BF16)
    b.sb("spt", [128, 512], F32)
    b.sb("E1", [128, 512], F32)
    b.sb("E2", [128, 512], F32)
    b.sb("E3", [128, 512], F32)
    b.sb("nbl", [128, 4], F32)
    b.sb("dec", [128, 4], F32)
    b.sb("qd", [128, 512], BF16)
    b.sb("kd", [128, 512], BF16)
    b.sb("kb", [128, 512], BF16)
    b.sb("attm", [128, 512], BF16)
    b.sb("kbt", [128, 512], BF16)
    b.sb("sso", [128, 4], F32)
    b.sb("og", [128, D], BF16)
    b.sb("ogT", [128, 8, 128], BF16)
    b.mset(b.t["acT"][:], 1.0, ["acT"])


def sb_outproj(b, L1W, TQ, y_dst_tiles, x1own):
    t = b.t
    Wo1 = L1W[1]
    PD, yt, ogT1 = t["PD"], t["on"], t["ogT1"]
    Tt = min(128, TQ)
    for ti in range(max(1, TQ // 128)):
        for n in range(2):
            for c in range(8):
                b.mm(PD[0:Tt, n * 512:(n + 1) * 512], ogT1[:, c, ti * 128:ti * 128 + Tt], Wo1[:, c, n * 512:(n + 1) * 512],
                     ["ogT1", "Wo1"], ["PD"], start=(c == 0), stop=(c == 7))
        b.tt(yt[0:Tt, :], PD[0:Tt, :], x1own[0:Tt, ti, :], ALU.add, ["PD", "x1own"], ["on"])
        b.dma(y_dst_tiles[ti], yt[0:Tt, :], ["on"], ["ydst"], q="pool", final=True)


def sb_group2(b, L1W, cst, sbc, nkb, KT_d, V_d, qT, sgT, TQ, y_dst_tiles, x1own):
    t = b.t
    _, Wo1, qng, kng, biasb = L1W
    maskt, tri, omt = sbc
    nmask = maskt.shape[1]
    KTs, Vs, ogT1 = t["KTs"], t["Vs"], t["ogT1"]
    PA, PB, PC, PD = t["PA"], t["PB"], t["PC"], t["PD"]
    Zs = [PB[:, 0:512], PC[:, 0:512]]
    ACCs = [PB[:, 512:1024], PC[:, 512:1024]]
    Os = [PA[:, 512:1024], PD[:, 0:512]]
    one = t["one"]
    for pr in range(8):
        b.dma(KTs[:, 0:nkb * 128], KT_d[pr, :, 0:nkb * 128], ["KT_d"], ["KTs"])
        b.dma(Vs[:, 0:nkb, :], V_d[0:nkb * 128, pr * 128:(pr + 1) * 128].rearrange("(k p) c -> p k c", p=128),
              ["V_d"], ["Vs"], q="pool")
        for i, kb in enumerate(range(nkb - 1, -1, -1)):
            mrel = kb - (nkb - nmask)
            m_ap = maskt[:, mrel, 0:TQ] if mrel >= 0 else None
            first, last = (i == 0), (kb == 0)
            KTb = KTs[:, kb * 128:(kb + 1) * 128]
            Vb = Vs[:, kb, :]
            L2 = range(2)
            e = [t["e_t"], t["e_t2"]]; Lt = [t["L_t"], t["L_t2"]]; P = [t["P_t"], t["P_t2"]]; A = [t["A_t"], t["A_t2"]]
            k = lambda n, l: n + str(l)
            for l in L2:
                p0 = 64 * l
                b.mm(Zs[l][:, 0:TQ], KTb[p0:p0 + 64, :], qT[p0:p0 + 64, pr, 0:TQ], ["KTs", "qT"], [k("Z", l)])
            for l in L2:
                h = 2 * pr + l
                b.act(e[l][:, 0:TQ], Zs[l][:, 0:TQ], AF.Exp, [k("Z", l), "biasb"], [k("e", l)], bias=biasb[:, h:h + 1])
            if m_ap is not None:
                for l in L2:
                    b.tt(e[l][:, 0:TQ], e[l][:, 0:TQ], m_ap, ALU.mult, [k("e", l), "mask"], [k("e", l)],
                         eng="dve" if l == 0 else "pool")
            for l in L2:
                b.act(Lt[l][:, 0:TQ], e[l][:, 0:TQ], AF.Ln, [k("e", l), "one"], [k("L", l)], bias=one[:, 0:1])
            for l in L2:
                b.mm(ACCs[l][:, 0:TQ], tri[:, :], Lt[l][:, 0:TQ], [k("L", l), "tri"], [k("ACC", l)], start=first, stop=False)
            for l in L2:
                b.act(P[l][:, 0:TQ], ACCs[l][:, 0:TQ], AF.Exp, [k("ACC", l)], [k("P", l)], scale=-1.0)
            for l in L2:
                b.mm(ACCs[l][:, 0:TQ], omt[:, :], Lt[l][:, 0:TQ], [k("L", l), "omt"], [k("ACC", l)], start=False, stop=last)
            for l in L2:
                b.tt(A[l][:, 0:TQ], e[l][:, 0:TQ], P[l][:, 0:TQ], ALU.mult, [k("e", l), k("P", l)], [k("A", l)])
            for l in L2:
                b.mm(Os[l][:, 0:TQ], Vb, A[l][:, 0:TQ], ["Vs", k("A", l)], [k("O", l)], start=first, stop=last)
        for l in range(2):
            p0 = 64 * l
            b.tt(ogT1[p0:p0 + 64, pr, 0:TQ], Os[l][p0:p0 + 64, 0:TQ], sgT[p0:p0 + 64, pr, 0:TQ], ALU.mult,
                 ["O" + str(l), "sgT"], ["ogT1"])
    b.p.barrier()
    sb_outproj(b, L1W, TQ, y_dst_tiles, x1own)


def sample_attn(b, s, L1W, cst, smc, NPG, ck, cv, idx, KTn_d, Vn_d, qT, sgT, y_dst, x1own, zt):
    t = b.t
    _, Wo1, qng, kng, biasb = L1W
    masknew, tri, omt, biasfull = smc
    ident_bf = cst[0]
    PA, PB, PC = t["PA"], t["PB"], t["PC"]
    PAb = PA[:].bitcast(BF16)
    Z, ACC, O = PB[:, 0:128], PB[:, 512:640], PC[:, 0:128]
    Qbd, ogT1 = t["Qbd"], t["ogT1"]
    one = t["one"]
    b.cp(Qbd[0:64, :, 0:8], qT[0:64, :, 0:8], ["qT"], ["Qbd"])
    b.cp(Qbd[64:128, :, 8:16], qT[64:128, :, 0:8], ["qT"], ["Qbd"])
    b.mm(O, zt[:, 0:128], zt[:, 0:128], ["zt"], ["O"], start=True, stop=False)
    nblk = NPG + 1
    kbl = list(range(nblk - 1, -1, -1))

    def prep(i):
        kb = kbl[i]
        par = i % 2
        kpg, vpg, kbf, vbp, ktt = t["kpg%d" % par], t["vpg%d" % par], t["kbfp%d" % par], t["vbp%d" % par], t["kttp%d" % par]
        kk = lambda n: n + str(par)
        if kb == NPG:
            b.dma(ktt[:, :, :], KTn_d.rearrange("c p t -> p c t"), ["KTs_dS"], [kk("ktt")], slow=True)
            b.dma(vbp[:, :], Vn_d, ["Vs_dS"], [kk("vbp")])
        else:
            c = s * NPG + kb
            b.p.dma("pool", lambda e_, c=c, kpg=kpg: e_.indirect_dma_start(
                out=kpg[:, :], out_offset=None, in_=ck[:, :],
                in_offset=bass.IndirectOffsetOnAxis(ap=idx[:, c:c + 1], axis=0)), ["idx"], [kk("kpg")])
            b.p.dma("pool", lambda e_, c=c, vpg=vpg: e_.indirect_dma_start(
                out=vpg[:, :], out_offset=None, in_=cv[:, :],
                in_offset=bass.IndirectOffsetOnAxis(ap=idx[:, c:c + 1], axis=0)), ["idx"], [kk("vpg")])
            b.cp(kbf[:, :], kpg[:, :], [kk("kpg")], [kk("kbf")], eng="dve")
            for c8 in range(8):
                b.tr(PAb[:, c8 * 128:(c8 + 1) * 128], kbf[:, c8 * 128:(c8 + 1) * 128], ident_bf[:, :],
                     [kk("kbf"), "ident_bf"], ["PA"])
            b.cp(ktt[:, :, :], PAb[:, 0:1024].rearrange("p (c t) -> p c t", c=8), ["PA"], [kk("ktt")])
            b.cp(vbp[:, :], vpg[:, :], [kk("vpg")], [kk("vbp")], eng="act")

    def unit(i):
        kb = kbl[i]
        par = i % 2
        ktt, vbp = t["kttp%d" % par], t["vbp%d" % par]
        kk = lambda n: n + str(par)
        first, last = (i == 0), (kb == 0)
        for pr in range(8):
            b.mm(Z[:, pr * 16:(pr + 1) * 16], ktt[:, pr, :], Qbd[:, pr, :], [kk("ktt"), "Qbd"], ["Zs"])
        e, Lt, P, A = t["e_t"], t["L_t"], t["P_t"], t["A_t"]
        b.tt(e[:, 0:128], Z, biasfull[:, :], ALU.add, ["Zs", "biasfull"], ["e0"])
        b.act(e[:, 0:128], e[:, 0:128], AF.Exp, ["e0"], ["e0"])
        if kb == NPG:
            b.tt(e[:, 0:128], e[:, 0:128], masknew[:, :], ALU.mult, ["e0", "masknew"], ["e0"])
        b.act(Lt[:, 0:128], e[:, 0:128], AF.Ln, ["e0", "one"], ["L0"], bias=one[:, 0:1])
        b.mm(ACC, tri[:, :], Lt[:, 0:128], ["L0", "tri"], ["ACCs"], start=first, stop=False)
        b.act(P[:, 0:128], ACC, AF.Exp, ["ACCs"], ["P0"], scale=-1.0)
        b.mm(ACC, omt[:, :], Lt[:, 0:128], ["L0", "omt"], ["ACCs"], start=False, stop=last)
        b.tt(A[:, 0:128], e[:, 0:128], P[:, 0:128], ALU.mult, ["e0", "P0"], ["A0"])
        for pr in range(8):
            b.mm(O[:, pr * 16:(pr + 1) * 16], vbp[:, pr * 128:(pr + 1) * 128], A[:, pr * 16:(pr + 1) * 16],
                 [kk("vbp"), "A0"], ["O"], start=False, stop=(last and pr == 7))

    prep(0)
    for i in range(nblk):
        if i + 1 < nblk:
            prep(i + 1)
        unit(i)
    O3 = O.rearrange("p (c x) -> p c x", c=8)
    b.tt(ogT1[0:64, :, 0:8], O3[0:64, :, 0:8], sgT[0:64, :, 0:8], ALU.mult, ["O", "sgT"], ["ogT1"])
    b.tt(ogT1[64:128, :, 0:8], O3[64:128, :, 8:16], sgT[64:128, :, 0:8], ALU.mult, ["O", "sgT"], ["ogT1"])
    sb_outproj(b, L1W, 8, [y_dst], x1own)


def sb_group3(b, L1W, cst, sbc, nkb, KT_d, V_d, qT, sgT, TQ, y_dst_tiles, x1own):
    t = b.t
    _, Wo1, qng, kng, biasb = L1W
    maskt, tri, omt = sbc
    nmask = maskt.shape[1]
    KTs, Vs, ogT1 = t["KTs"], t["Vs"], t["ogT1"]
    PA, PB, PC, PD = t["PA"], t["PB"], t["PC"], t["PD"]
    Zs = [[PA[:, 0:512], PA[:, 512:1024]], [PC[:, 0:512], PC[:, 512:1024]]]
    ACCs = [PB[:, 0:512], PD[:, 0:512]]
    Os = [PB[:, 512:1024], PD[:, 512:1024]]
    one = t["one"]
    e = [[t["pe00"], t["pe01"]], [t["pe10"], t["pe11"]]]
    Lt = [[t["pL00"], t["pL01"]], [t["pL10"], t["pL11"]]]
    P = [t["pP0"], t["pP1"]]; A = [t["pA0"], t["pA1"]]
    for pr in range(8):
        hb = nkb // 2
        b.dma(KTs[:, hb * 128:nkb * 128], KT_d[pr, :, hb * 128:nkb * 128], ["KT_d"], ["KTs_hi"])
        b.dma(Vs[:, hb:nkb, :], V_d[hb * 128:nkb * 128, pr * 128:(pr + 1) * 128].rearrange("(k p) c -> p k c", p=128),
              ["V_d"], ["Vs_hi"], q="pool")
        b.dma(KTs[:, 0:hb * 128], KT_d[pr, :, 0:hb * 128], ["KT_d"], ["KTs_lo"])
        b.dma(Vs[:, 0:hb, :], V_d[0:hb * 128, pr * 128:(pr + 1) * 128].rearrange("(k p) c -> p k c", p=128),
              ["V_d"], ["Vs_lo"], q="pool")
        kbs = list(range(nkb - 1, -1, -1))
        NSg = nmask // 4

        def col0(kb):
            kbrel = kb - (nkb - nmask)
            if kbrel < 0:
                return 0
            return 128 * max(0, min(3, -((-(kbrel - NSg + 1)) // NSg)))

        for l in range(2):
            b.mm(ACCs[l][:, 0:TQ], t["zt"][:, 0:128], t["zt"][:, 0:TQ], ["zt"], ["ACC%d" % l], start=True, stop=False)
            b.mm(Os[l][:, 0:TQ], t["zt"][:, 0:128], t["zt"][:, 0:TQ], ["zt"], ["O%d" % l], start=True, stop=False)

        def front(i):
            kb = kbs[i]; par = i % 2
            mrel = kb - (nkb - nmask)
            c0 = col0(kb)
            m_ap = maskt[:, mrel, c0:TQ] if mrel >= 0 else None
            KTb = KTs[:, kb * 128:(kb + 1) * 128]
            for l in range(2):
                p0 = 64 * l
                b.mm(Zs[l][par][:, c0:TQ], KTb[p0:p0 + 64, :], qT[p0:p0 + 64, pr, c0:TQ], ["KTs_hi" if kb >= hb else "KTs_lo", "qT"], ["Z%d%d" % (l, par)])
            for l in range(2):
                h = 2 * pr + l
                b.act(e[l][par][:, c0:TQ], Zs[l][par][:, c0:TQ], AF.Exp, ["Z%d%d" % (l, par), "biasb"], ["e%d%d" % (l, par)],
                      bias=biasb[:, h:h + 1])
            if m_ap is not None:
                for l in range(2):
                    b.tt(e[l][par][:, c0:TQ], e[l][par][:, c0:TQ], m_ap, ALU.mult, ["e%d%d" % (l, par), "mask"],
                         ["e%d%d" % (l, par)], eng="dve")
            for l in range(2):
                b.act(Lt[l][par][:, c0:TQ], e[l][par][:, c0:TQ], AF.Ln, ["e%d%d" % (l, par), "one"], ["L%d%d" % (l, par)],
                      bias=one[:, 0:1])

        def back_a(i):
            kb = kbs[i]; par = i % 2
            c0 = col0(kb)
            for l in range(2):
                b.mm(ACCs[l][:, c0:TQ], tri[:, :], Lt[l][par][:, c0:TQ], ["L%d%d" % (l, par), "tri"], ["ACC%d" % l],
                     start=False, stop=False)
            for l in range(2):
                b.act(P[l][:, c0:TQ], ACCs[l][:, c0:TQ], AF.Exp, ["ACC%d" % l], ["P%d" % l], scale=-1.0)
            for l in range(2):
                b.tt(A[l][:, c0:TQ], e[l][par][:, c0:TQ], P[l][:, c0:TQ], ALU.mult, ["e%d%d" % (l, par), "P%d" % l], ["A%d" % l])

        def back_b(i):
            kb = kbs[i]; par = i % 2
            c0 = col0(kb)
            last = (kb == 0)
            Vb = Vs[:, kb, :]
            for l in range(2):
                b.mm(ACCs[l][:, c0:TQ], omt[:, :], Lt[l][par][:, c0:TQ], ["L%d%d" % (l, par), "omt"], ["ACC%d" % l],
                     start=False, stop=last)
            for l in range(2):
                b.mm(Os[l][:, c0:TQ], Vb, A[l][:, c0:TQ], ["Vs_hi" if kb >= hb else "Vs_lo", "A%d" % l], ["O%d" % l], start=False, stop=last)

        n = len(kbs)
        front(0)
        for i in range(n):
            back_a(i)
            if i + 1 < n:
                front(i + 1)
            back_b(i)
        for l in range(2):
            p0 = 64 * l
            b.tt(ogT1[p0:p0 + 64, pr, 0:TQ], Os[l][p0:p0 + 64, 0:TQ], sgT[p0:p0 + 64, pr, 0:TQ], ALU.mult,
                 ["O%d" % l, "sgT"], ["ogT1"])
    b.p.barrier()
    sb_outproj(b, L1W, TQ, y_dst_tiles, x1own)


def _run(inp, cfg, ncores=8):
    NTOK, NS, NSAMP, NPG, NPHYS = cfg["NTOK"], cfg["NS"], cfg["NSAMP"], cfg["NPG"], cfg["NPHYS"]
    NOWN = NTOK // NS
    nc = bass.Bass("TRN2", target_bir_lowering=False)
    b = build_program(nc, cfg)
    c = consts_np()
    f32 = np.float32
    cs = np.zeros((128, 16, 8), f32)
    for s in range(8):
        cs[s, :, :] = (s < np.arange(8))[None, :]
    cs = cs.reshape(128, 128)
    import ml_dtypes
    ck = np.ascontiguousarray(inp["cache_k"][0].reshape(NPHYS * 128, 1024))
    cv = np.ascontiguousarray(inp["cache_v"][0].reshape(NPHYS * 128, 1024))
    in_maps = []
    for core in range(ncores):
        bb, j = core // NS, core % NS
        c2 = consts_sb_np(j, NS)
        own_tiles = NS * np.arange(NOWN // 128) + j
        own_rows = (own_tiles[None, :] * 128 + np.arange(128)[:, None]).astype(np.int32)
        m = {
            "xb": np.ascontiguousarray(inp["x_prompt"][bb]), "own_rows": own_rows,
            "xs": np.ascontiguousarray(inp["x_sample"][core * NSAMP:(core + 1) * NSAMP].reshape(NSAMP * 8, 1024)),
            "state": np.ascontiguousarray(inp["state_gla"][0, core * NSAMP:(core + 1) * NSAMP]),
            "ck": ck, "cv": cv,
            "pt": np.ascontiguousarray(inp["page_table"][core * NSAMP:(core + 1) * NSAMP].reshape(-1)).astype(np.int32),
            "gain": inp["norm_gain"], "win_a": inp["w_in_a"][0], "wup": inp["w_alpha_up"][0], "ba": inp["b_alpha"][0],
            "onorm": inp["onorm_a"][0], "wout_a": inp["w_out_a"][0], "win_b": inp["w_in_b"][0], "qn": inp["qnorm_b"][0],
            "kn": inp["knorm_b"][0], "sbias": inp["sb_bias"][0], "wout_b": inp["w_out_b"][0],
            "c_ib": c["ident_bf"], "c_if": c["ident_f"], "c_u": c["u_f"], "c_u4": c["u4_f"],
            "c_mask": c2["mask"], "c_tri": c2["tri"], "c_omt": c2["omt"],
            "c_masknew": cs.astype(ml_dtypes.bfloat16), "c_iota": np.arange(128, dtype=f32).reshape(128, 1),
        }
        in_maps.append({k: np.ascontiguousarray(v) for k, v in m.items()})
    res = run_bass_kernel_spmd(nc, in_maps, core_ids=list(range(ncores))).results
    NB = ncores // NS
    DB = ncores * NSAMP
    y_p = np.zeros((NB, NTOK, 1024), f32); y_s = np.zeros((DB, 8, 1024), f32)
    sp = np.zeros((1, NB, 4, 128, 256), f32); ss = np.zeros((1, DB, 4, 128, 256), f32)
    kp = np.zeros((1, NB, NTOK, 16, 64), f32); vp = np.zeros_like(kp)
    ks = np.zeros((1, DB, 8, 16, 64), f32); vs = np.zeros_like(ks)
    for core in range(ncores):
        bb, j = core // NS, core % NS
        r = res[core]
        yo = r["y_own"].reshape(NOWN // 128, 128, 1024)
        for i in range(NOWN // 128):
            t = NS * i + j
            y_p[bb, t * 128:(t + 1) * 128] = yo[i]
        y_s[core * NSAMP:(core + 1) * NSAMP] = r["ys"].reshape(NSAMP, 8, 1024)
        ss[0, core * NSAMP:(core + 1) * NSAMP] = r["st_s"]
        ks[0, core * NSAMP:(core + 1) * NSAMP] = r["k_s"].reshape(NSAMP, 8, 16, 64)
        vs[0, core * NSAMP:(core + 1) * NSAMP] = r["v_s"].reshape(NSAMP, 8, 16, 64)
        if j == 0:
            sp[0, bb] = r["st_p"]
            kp[0, bb] = r["k_all"].reshape(NTOK, 16, 64)
            vp[0, bb] = r["v_all"].reshape(NTOK, 16, 64)
    return (y_p, y_s, sp, ss, kp, vp, ks, vs)


CFG = dict(NTOK=8192, NS=4, NSAMP=16, NPG=16, NPHYS=2560)


def kernel(**inputs):
    inp = {k: np.asarray(v) for k, v in inputs.items()}
    return _run(inp, CFG, 8)
```
